# Optimizing a Trainium2 kernel written in Bass

```python
import jax, jax.numpy as jnp
from jax import lax
import numpy as np

D_MODEL = 1024
BATCH = 2
SEQ = 16384
DEPTH = 2
DEC_BATCH = 4
DEC_SEQ = 8192
PAST_LEN = 128

N_META = 16
GRID_W = 64
Q_BLOCK = 128
ROPE_THETA = 10000.0
NORM_EPS = 1e-6

MLA_HEADS = 8
MLA_Q_LORA = 384
MLA_KV_LORA = 256
MLA_NOPE = 64
MLA_ROPE = 32
MLA_V = 64
MLA_QK = MLA_NOPE + MLA_ROPE
MLA_OUT = MLA_HEADS * MLA_V

GQA_HEADS = 8
GQA_KV_HEADS = 2
GQA_GROUP = GQA_HEADS // GQA_KV_HEADS
GQA_HEAD_DIM = 64
GQA_AXIS_DIM = GQA_HEAD_DIM // 2
GQA_OUT = GQA_HEADS * GQA_HEAD_DIM

MIX_WIDTH = MLA_OUT + GQA_OUT
D_FF = 4 * D_MODEL

IN_SIZES = (MLA_Q_LORA, MLA_KV_LORA, MLA_ROPE,
            GQA_HEADS * GQA_HEAD_DIM, GQA_KV_HEADS * GQA_HEAD_DIM, GQA_KV_HEADS * GQA_HEAD_DIM)
IN_COLS = sum(IN_SIZES)
IN_OFFSETS = tuple(int(v) for v in np.cumsum(IN_SIZES)[:-1])

kernel_name = "hybrid_mla_axial_gqa_encoder"


def rms_norm(x, g):
    xf = x.astype(jnp.float32)
    y = xf * lax.rsqrt(jnp.mean(xf * xf, axis=-1, keepdims=True) + NORM_EPS)
    return (y * g.astype(jnp.float32)).astype(x.dtype)


def inv_freq(dim):
    return 1.0 / (ROPE_THETA ** (jnp.arange(0, dim, 2, dtype=jnp.float32) / dim))


def cos_sin(ang):
    a = jnp.concatenate([ang, ang], axis=-1)
    return jnp.cos(a), jnp.sin(a)


def apply_rope(x, cos, sin):
    half = x.shape[-1] // 2
    x1, x2 = x[..., :half], x[..., half:]
    rot = jnp.concatenate([-x2, x1], axis=-1)
    c = cos[None, :, None, :].astype(x.dtype)
    s = sin[None, :, None, :].astype(x.dtype)
    return x * c + rot * s


def apply_axial_rope(x, row_cs, col_cs):
    xr = apply_rope(x[..., :GQA_AXIS_DIM], *row_cs)
    xc = apply_rope(x[..., GQA_AXIS_DIM:], *col_cs)
    return jnp.concatenate([xr, xc], axis=-1)


def block_attention(q, k, v, scale):
    B, L, Hk, G, D = q.shape
    n = L - N_META
    nb = n // Q_BLOCK

    def attend(qb):
        s = jnp.einsum('bqhgd,bkhd->bhgqk', qb, k).astype(jnp.float32) * scale
        p = jax.nn.softmax(s, axis=-1).astype(v.dtype)
        return jnp.einsum('bhgqk,bkhe->bqhge', p, v)

    o_meta = attend(q[:, :N_META])
    qr = q[:, N_META:].reshape(B, nb, Q_BLOCK, Hk, G, D)
    qr = jnp.moveaxis(qr, 1, 0)
    o_real = lax.map(attend, qr)
    o_real = jnp.moveaxis(o_real, 0, 1).reshape(B, n, Hk, G, v.shape[-1])
    return jnp.concatenate([o_meta, o_real], axis=1)


def encoder_layer(x, mla_cs, row_cs, col_cs, attn_norm_g, w_in, q_a_norm_g, w_q_b,
                  kv_a_norm_g, w_kv_b, gqa_q_norm_g, gqa_k_norm_g, mla_out_norm_g,
                  gqa_out_norm_g, w_out, mlp_norm_g, w_up, w_down):
    B, L, _ = x.shape
    h = rms_norm(x, attn_norm_g)
    proj = h @ w_in
    c_q, c_kv, k_rope, gq, gk, gv = jnp.split(proj, IN_OFFSETS, axis=-1)

    q = (rms_norm(c_q, q_a_norm_g) @ w_q_b).reshape(B, L, MLA_HEADS, MLA_QK)
    q_pe = apply_rope(q[..., MLA_NOPE:], *mla_cs)
    q_mla = jnp.concatenate([q[..., :MLA_NOPE], q_pe], axis=-1)[:, :, :, None, :]
    kv = (rms_norm(c_kv, kv_a_norm_g) @ w_kv_b).reshape(B, L, MLA_HEADS, MLA_NOPE + MLA_V)
    k_nope, v_mla = kv[..., :MLA_NOPE], kv[..., MLA_NOPE:]
    k_pe = apply_rope(k_rope[:, :, None, :], *mla_cs)
    k_mla = jnp.concatenate(
        [k_nope, jnp.broadcast_to(k_pe, (B, L, MLA_HEADS, MLA_ROPE))], axis=-1)
    o_mla = block_attention(q_mla, k_mla, v_mla, MLA_QK ** -0.5).reshape(B, L, MLA_OUT)

    gq = rms_norm(gq.reshape(B, L, GQA_HEADS, GQA_HEAD_DIM), gqa_q_norm_g)
    gk = rms_norm(gk.reshape(B, L, GQA_KV_HEADS, GQA_HEAD_DIM), gqa_k_norm_g)
    gv = gv.reshape(B, L, GQA_KV_HEADS, GQA_HEAD_DIM)
    gq = apply_axial_rope(gq, row_cs, col_cs).reshape(B, L, GQA_KV_HEADS, GQA_GROUP, GQA_HEAD_DIM)
    gk = apply_axial_rope(gk, row_cs, col_cs)
    o_gqa = block_attention(gq, gk, gv, GQA_HEAD_DIM ** -0.5).reshape(B, L, GQA_OUT)

    mixed = jnp.concatenate([rms_norm(o_mla, mla_out_norm_g),
                             rms_norm(o_gqa, gqa_out_norm_g)], axis=-1) @ w_out
    x = x + mixed

    u = rms_norm(x, mlp_norm_g) @ w_up
    return x + jnp.square(jax.nn.relu(u)) @ w_down


def run_trunk(x, meta_tokens, attn_norm_g, w_in, q_a_norm_g, w_q_b, kv_a_norm_g, w_kv_b,
              gqa_q_norm_g, gqa_k_norm_g, mla_out_norm_g, gqa_out_norm_g, w_out,
              mlp_norm_g, w_up, w_down, final_norm_g):
    B, n, _ = x.shape
    L = n + N_META
    meta = jnp.broadcast_to(meta_tokens.astype(x.dtype)[None], (B, N_META, D_MODEL))
    h = jnp.concatenate([meta, x], axis=1)

    pos = jnp.arange(L, dtype=jnp.float32)
    mla_cs = cos_sin(pos[:, None] * inv_freq(MLA_ROPE)[None, :])

    rows_n = n // GRID_W
    t_rows = jnp.repeat(jnp.arange(rows_n, dtype=jnp.float32), GRID_W)
    t_cols = jnp.tile(jnp.arange(GRID_W, dtype=jnp.float32), rows_n)
    zeros = jnp.zeros((N_META,), jnp.float32)
    f_ax = inv_freq(GQA_AXIS_DIM)
    row_cs = cos_sin(jnp.concatenate([zeros, t_rows])[:, None] * f_ax[None, :])
    col_cs = cos_sin(jnp.concatenate([zeros, t_cols])[:, None] * f_ax[None, :])

    for l in range(DEPTH):
        h = encoder_layer(h, mla_cs, row_cs, col_cs, attn_norm_g[l], w_in[l], q_a_norm_g[l],
                          w_q_b[l], kv_a_norm_g[l], w_kv_b[l], gqa_q_norm_g[l],
                          gqa_k_norm_g[l], mla_out_norm_g[l], gqa_out_norm_g[l], w_out[l],
                          mlp_norm_g[l], w_up[l], w_down[l])
    h = rms_norm(h, final_norm_g)
    return h[:, N_META:]


def setup_inputs(seed: int = 0) -> dict:
    key = jax.random.key(seed)
    ks = jax.random.split(key, 20)
    f32 = jnp.float32

    def w(k, shape, fan_in):
        return jax.random.normal(k, shape, f32) * (fan_in ** -0.5)

    def gain(k, shape):
        return 1.0 + 0.1 * jax.random.normal(k, shape, f32)

    return {
        "x_prompt": jax.random.normal(ks[0], (BATCH, SEQ, D_MODEL), f32),
        "x_sample": jax.random.normal(ks[1], (DEC_BATCH, DEC_SEQ, D_MODEL), f32),
        "meta_tokens": jax.random.normal(ks[2], (N_META, D_MODEL), f32),
        "attn_norm_g": gain(ks[3], (DEPTH, D_MODEL)),
        "w_in": w(ks[4], (DEPTH, D_MODEL, IN_COLS), D_MODEL),
        "q_a_norm_g": gain(ks[5], (DEPTH, MLA_Q_LORA)),
        "w_q_b": w(ks[6], (DEPTH, MLA_Q_LORA, MLA_HEADS * MLA_QK), MLA_Q_LORA),
        "kv_a_norm_g": gain(ks[7], (DEPTH, MLA_KV_LORA)),
        "w_kv_b": w(ks[8], (DEPTH, MLA_KV_LORA, MLA_HEADS * (MLA_NOPE + MLA_V)), MLA_KV_LORA),
        "gqa_q_norm_g": gain(ks[9], (DEPTH, GQA_HEAD_DIM)),
        "gqa_k_norm_g": gain(ks[10], (DEPTH, GQA_HEAD_DIM)),
        "mla_out_norm_g": gain(ks[11], (DEPTH, MLA_OUT)),
        "gqa_out_norm_g": gain(ks[12], (DEPTH, GQA_OUT)),
        "w_out": w(ks[13], (DEPTH, MIX_WIDTH, D_MODEL), MIX_WIDTH),
        "mlp_norm_g": gain(ks[14], (DEPTH, D_MODEL)),
        "w_up": w(ks[15], (DEPTH, D_MODEL, D_FF), D_MODEL),
        "w_down": w(ks[16], (DEPTH, D_FF, D_MODEL), D_FF),
        "final_norm_g": gain(ks[17], (D_MODEL,)),
    }


def reference(x_prompt, x_sample, meta_tokens, attn_norm_g, w_in, q_a_norm_g, w_q_b,
              kv_a_norm_g, w_kv_b, gqa_q_norm_g, gqa_k_norm_g, mla_out_norm_g,
              gqa_out_norm_g, w_out, mlp_norm_g, w_up, w_down, final_norm_g):
    y_prompt = run_trunk(x_prompt, meta_tokens, attn_norm_g, w_in, q_a_norm_g, w_q_b,
                         kv_a_norm_g, w_kv_b, gqa_q_norm_g, gqa_k_norm_g, mla_out_norm_g,
                         gqa_out_norm_g, w_out, mlp_norm_g, w_up, w_down, final_norm_g)
    y_sample = run_trunk(x_sample, meta_tokens, attn_norm_g, w_in, q_a_norm_g, w_q_b,
                         kv_a_norm_g, w_kv_b, gqa_q_norm_g, gqa_k_norm_g, mla_out_norm_g,
                         gqa_out_norm_g, w_out, mlp_norm_g, w_up, w_down, final_norm_g)
    return (y_prompt, y_sample)
```

```python
import contextlib
import numpy as np
import ml_dtypes
import concourse.bass as bass
import concourse.mybir as mybir
from concourse.bass_utils import run_bass_kernel_spmd

F32 = mybir.dt.float32
BF16 = mybir.dt.bfloat16
AF = mybir.ActivationFunctionType
ALU = mybir.AluOpType
AX = mybir.AxisListType

ENGS = ("pe", "act", "dve", "pool", "sp")
D = 1024
EPS = 1e-6
NMETA = 16


class Tok:
    __slots__ = ("eng", "sem", "val", "dma")

    def __init__(self, eng, sem, val, dma):
        self.eng, self.sem, self.val, self.dma = eng, sem, val, dma


class Buf:
    __slots__ = ("name", "w", "r")

    def __init__(self, name=""):
        self.name = name
        self.w = None
        self.r = {}


class Tracker:
    def __init__(self, sems, rings):
        self.sem = sems
        self.rings = rings
        self.streams = {e: [] for e in ENGS}
        self.cnt = {e: 0 for e in ENGS}
        self.waited = {e: {} for e in ENGS}
        self.ring_idx = {q: 0 for q in rings}
        self.ring_val = {}
        self.ring_tok = {}

    def _need(self, eng, tok, waits):
        if tok is None:
            return
        if (not tok.dma) and tok.eng == eng and eng == "pe":
            return
        w = self.waited[eng]
        key = id(tok.sem)
        if w.get(key, 0) >= tok.val:
            return
        w[key] = tok.val
        waits.append((tok.sem, tok.val))

    def _deps(self, eng, reads, writes):
        waits = []
        for b in reads:
            self._need(eng, b.w, waits)
        for b in writes:
            self._need(eng, b.w, waits)
            for t in b.r.values():
                self._need(eng, t, waits)
        return waits

    def _commit(self, tok, reads, writes):
        k = id(tok.sem)
        for b in reads:
            o = b.r.get(k)
            if o is None or o.val < tok.val:
                b.r[k] = tok
        for b in writes:
            b.w = tok
            b.r = {}

    def _skip(self):
        self.nrec = getattr(self, "nrec", 0) + 1
        return self.nrec > getattr(self, "maxops", 1 << 60)

    def op(self, eng, fn, reads=(), writes=()):
        if self._skip():
            return None
        waits = self._deps(eng, reads, writes)
        self.cnt[eng] += 1
        tok = Tok(eng, self.sem[eng], self.cnt[eng], False)
        self.streams[eng].append((waits, fn, self.sem[eng], 1))
        self._commit(tok, reads, writes)
        return tok

    def _ring(self, q, fn, inc, reads, writes):
        if self._skip():
            return None
        ring = self.rings[q]
        sem = ring[self.ring_idx[q] % len(ring)]
        self.ring_idx[q] += 1
        waits = self._deps(q, reads, writes)
        prev = self.ring_tok.get(id(sem))
        if prev is not None:
            self._need(q, prev, waits)
        val = self.ring_val.get(id(sem), 0) + inc
        self.ring_val[id(sem)] = val
        tok = Tok(q, sem, val, True)
        self.ring_tok[id(sem)] = tok
        self.streams[q].append((waits, fn, sem, inc))
        self._commit(tok, reads, writes)
        return tok

    def dma(self, q, out, in_, reads=(), writes=()):
        return self._ring(q, lambda e, out=out, in_=in_: e.dma_start(out=out, in_=in_), 16, reads, writes)

    def custom(self, q, fn, inc, reads=(), writes=(), ring="cc"):
        if self._skip():
            return None
        rg = self.rings[ring]
        sem = rg[self.ring_idx[ring] % len(rg)]
        self.ring_idx[ring] += 1
        waits = self._deps(q, reads, writes)
        prev = self.ring_tok.get(id(sem))
        if prev is not None:
            self._need(q, prev, waits)
        val = self.ring_val.get(id(sem), 0) + inc
        self.ring_val[id(sem)] = val
        tok = Tok(q, sem, val, True)
        self.ring_tok[id(sem)] = tok
        self.streams[q].append((waits, fn, sem, inc))
        self._commit(tok, reads, writes)
        return tok

    def barrier(self):
        toks = []
        for e in ENGS:
            if self.cnt[e] > 0:
                toks.append(Tok(e, self.sem[e], self.cnt[e], False))
        toks.extend(self.ring_tok.values())
        for e in ENGS:
            waits = []
            for t in toks:
                if (not t.dma) and t.eng == e:
                    continue
                self._need(e, t, waits)
            if waits:
                self.streams[e].append((waits, None, None, 0))

    def replay(self, eng, e):
        for waits, fn, sem, inc in self.streams[eng]:
            for s, v in waits:
                e.wait_ge(s, v)
            if fn is not None:
                fn(e).then_inc(sem, inc)


class Pool:
    def __init__(self, tensor, size):
        self.t, self.size, self.off, self.marks = tensor, size, 0, []

    def alloc(self, n, name=""):
        a = self.off
        self.off += n
        assert self.off <= self.size, (name, self.off, self.size)
        return self.t[:, a:a + n], Buf(name)

    def mark(self):
        self.marks.append(self.off)

    def release(self):
        self.off = self.marks.pop()


class Ctx:
    pass


def build(N_OWN, depth=2, N16=80100, N32=10500, nphase=None, debug=False):
    NT = N_OWN // 128
    NC = NT
    NQ = N_OWN + NMETA
    RS = (4, 2)
    nc = bass.Bass("TRN2", target_bir_lowering=False)

    def din(name, shape, dt=F32):
        return nc.dram_tensor(name, list(shape), dt, kind="ExternalInput").ap()

    xq = din("xq", [2, N_OWN, D])
    meta = din("meta", [NMETA, D])
    ident = din("ident", [128, 128])
    csm_d = din("csm", [2, NQ, 64])
    csg_d = din("csg", [2, NQ, 128])
    w_in_d = din("w_in", [depth, D, 1440])
    w_qb_d = din("w_q_b", [depth, 384, 768])
    w_kvb_d = din("w_kv_b", [depth, 256, 1024])
    w_out_d = din("w_out", [depth, D, D])
    w_up_d = din("w_up", [depth, D, 4096])
    w_down_d = din("w_down", [depth, 4096, D])
    g_attn_d = din("attn_norm_g", [depth, D])
    g_qa_d = din("q_a_norm_g", [depth, 384])
    g_kva_d = din("kv_a_norm_g", [depth, 256])
    g_gq_d = din("gqa_q_norm_g", [depth, 64])
    g_gk_d = din("gqa_k_norm_g", [depth, 64])
    g_out_d = din("out_norm_g", [depth, D])
    g_mlp_d = din("mlp_norm_g", [depth, D])
    g_fin_d = din("final_norm_g", [D])
    y_d = nc.dram_tensor("y", [2, N_OWN, D], F32, kind="ExternalOutput").ap()
    import os as _os2
    if _os2.environ.get("K_PAD"):
        din("pad", [int(_os2.environ["K_PAD"]), 1024])

    dk = dict(kind="ExternalOutput") if debug else {}
    QTm = [nc.dram_tensor(f"QTm{s}", [8 * 96, NQ], BF16, **dk).ap() for s in range(2)]
    QTg = [nc.dram_tensor(f"QTg{s}", [8 * 64, NQ], BF16, **dk).ap() for s in range(2)]
    KTloc = [nc.dram_tensor(f"KTloc{s}", [672, N_OWN], BF16) for s in range(2)]
    Vloc = [nc.dram_tensor(f"Vloc{s}", [10 * N_OWN, 65], BF16) for s in range(2)]
    KPIECES = [(h * 64, 64) for h in range(8)] + [(512, 32), (544, 64), (608, 64)]
    KTall = [[nc.dram_tensor(f"KTall{s}_{i}", [RS[s] * n, N_OWN], BF16) for i, (a, n) in enumerate(KPIECES)] for s in range(2)]
    Vall = [[nc.dram_tensor(f"Vall{s}_{h}", [RS[s] * N_OWN, 65], BF16) for h in range(10)] for s in range(2)]
    KTmeta = [nc.dram_tensor(f"KTmeta{s}", [672, NMETA], BF16, **dk).ap() for s in range(2)]
    Vmeta = [nc.dram_tensor(f"Vmeta{s}", [10 * NMETA, 65], BF16, **dk).ap() for s in range(2)]
    AO = [nc.dram_tensor(f"AO{s}", [NQ, D], F32, **dk).ap() for s in range(2)]
    X1 = [nc.dram_tensor(f"X1{s}", [NQ, D], F32, **dk).ap() for s in range(2)]

    es = contextlib.ExitStack()
    with es:
        sb32 = es.enter_context(nc.sbuf_tensor("sb32", [128, N32], F32))
        sb16 = es.enter_context(nc.sbuf_tensor("sb16", [128, N16], BF16))
        ps32 = es.enter_context(nc.psum_tensor("ps32", [128, 7 * 512], F32))
        ps16 = es.enter_context(nc.psum_tensor("ps16", [128, 1024], BF16))
        sems = {e: es.enter_context(nc.semaphore("s_" + e)) for e in ENGS}
        rings = {q: [es.enter_context(nc.semaphore(f"r_{q}{i}")) for i in range(8 if q != "cc" else 4)] for q in ("sp", "pool", "cc")}
        block = es.enter_context(nc.Block())
        T = Tracker(sems, rings)
        import os as _os
        if _os.environ.get("K_MAXOPS"):
            T.maxops = int(_os.environ["K_MAXOPS"])
        P32 = Pool(sb32, N32)
        P16 = Pool(sb16, N16)
        bank = [ps32[:, i * 512:(i + 1) * 512] for i in range(7)]
        bankb = [Buf(f"bank{i}") for i in range(7)]
        pT = ps16
        pTb = Buf("pT16")

        id32, id32b = P32.alloc(128, "id32")
        idb, idbb = P16.alloc(128, "idb")
        T.dma("sp", id32, ident, writes=[id32b])
        T.op("dve", lambda e: e.tensor_copy(out=idb, in_=id32), reads=[id32b], writes=[idbb])

        def bcast_load(dst, dbuf, src1d, n):
            T.dma("sp", dst[:, :n], src1d.partition_broadcast(128), writes=[dbuf])

        stage_n = 2048

        def load_w(dst3, src2, stages, engs=("dve", "act")):
            rows, N = src2.shape
            KC = (rows + 127) // 128
            i = 0
            for k in range(KC):
                pr = min(128, rows - k * 128)
                for n0 in range(0, N, stage_n):
                    n1 = min(N, n0 + stage_n)
                    st, stb = stages[load_w.i % len(stages)]
                    eng = engs[load_w.i % len(engs)]
                    load_w.i += 1
                    T.dma("sp", st[:pr, :n1 - n0], src2[k * 128:k * 128 + pr, n0:n1], writes=[stb])
                    if eng == "dve":
                        T.op("dve", lambda e, o=dst3[:pr, k, n0:n1], i_=st[:pr, :n1 - n0]: e.tensor_copy(out=o, in_=i_),
                             reads=[stb], writes=[Buf()])
                    else:
                        T.op("act", lambda e, o=dst3[:pr, k, n0:n1], i_=st[:pr, :n1 - n0]: e.activation(out=o, in_=i_, func=AF.Copy),
                             reads=[stb], writes=[Buf()])
        load_w.i = 0

        def rstd(src, srcb, P, n, c):
            T.op("act", lambda e: e.activation(out=c.junk[:P, :src.shape[-1]] if len(src.shape) == 2 else c.junk[:P, :src.shape[-1]],
                                               in_=src, func=AF.Square, accum_out=c.ss[:P]),
                 reads=[srcb], writes=[c.junkb, c.ssb])
            T.op("act", lambda e: e.activation(out=c.sd[:P], in_=c.ss[:P], func=AF.Sqrt, scale=1.0 / n, bias=EPS),
                 reads=[c.ssb], writes=[c.sdb])
            T.op("dve", lambda e: e.reciprocal(out=c.r[:P], in_=c.sd[:P]), reads=[c.sdb], writes=[c.rb])

        def transposes(src16, srcb, P, nblk, width, dstT, dstTb):
            def f(e):
                ins = None
                for j in range(nblk):
                    ins = e.transpose(out=pT[:width, j * 128:j * 128 + P], in_=src16[:P, j * width:(j + 1) * width],
                                      identity=idb[:P, :P])
                return ins
            T.op("pe", f, reads=[srcb, idbb], writes=[pTb])
            pv = pT[:width, :nblk * 128].rearrange("f (j p) -> f j p", p=128)[:, :, :P]
            T.op("act", lambda e: e.activation(out=dstT[:width, :nblk, :P], in_=pv, func=AF.Copy),
                 reads=[pTb], writes=[dstTb])

        def mm_tokmajor(out_ps, outb, lhsT3, lhsTb, P, W3, kc, c0, c1):
            def f(e):
                ins = None
                for k in range(kc):
                    ins = e.matmul(out_ps[:P, :c1 - c0], lhsT=lhsT3[:, k, :P], rhs=W3[:, k, c0:c1],
                                   start=(k == 0), stop=(k == kc - 1))
                return ins
            T.op("pe", f, reads=[lhsTb], writes=[outb])

        def rope(src3, srcb, P, H, Dh, blocks, cs, csb, out3, outb, c):
            t1 = c.rt1[:P, :H * Dh].rearrange("p (h d) -> p h d", d=Dh)
            t2 = c.rt2[:P, :H * Dh].rearrange("p (h d) -> p h d", d=Dh)
            cosb = cs[:P, 0:Dh].unsqueeze(1).broadcast_to([P, H, Dh])
            T.op("dve", lambda e: e.tensor_tensor(out=t1, in0=src3, in1=cosb, op=ALU.mult),
                 reads=[srcb, csb], writes=[c.rt1b])
            hb = Dh // blocks // 2
            for b in range(blocks):
                lo = b * 2 * hb
                s_lo = cs[:P, Dh + lo:Dh + lo + hb].unsqueeze(1).broadcast_to([P, H, hb])
                s_hi = cs[:P, Dh + lo + hb:Dh + lo + 2 * hb].unsqueeze(1).broadcast_to([P, H, hb])
                T.op("dve", lambda e, lo=lo, s_lo=s_lo: e.tensor_tensor(out=t2[:, :, lo:lo + hb], in0=src3[:, :, lo + hb:lo + 2 * hb],
                                                                       in1=s_lo, op=ALU.mult),
                     reads=[srcb, csb], writes=[c.rt2b])
                T.op("dve", lambda e, lo=lo, s_hi=s_hi: e.tensor_tensor(out=t2[:, :, lo + hb:lo + 2 * hb], in0=src3[:, :, lo:lo + hb],
                                                                       in1=s_hi, op=ALU.mult),
                     reads=[srcb, csb], writes=[c.rt2b])
            T.op("dve", lambda e: e.tensor_tensor(out=out3, in0=t1, in1=t2, op=ALU.add),
                 reads=[c.rt1b, c.rt2b], writes=[outb])

        def phase1(l):
            P16.mark()
            P32.mark()
            Win_f, _ = P16.alloc(8 * 1440, "Win")
            Win = Win_f.rearrange("p (k n) -> p k n", n=1440)
            Wq_f, _ = P16.alloc(3 * 768, "Wq")
            Wq = Wq_f.rearrange("p (k n) -> p k n", n=768)
            Wkv_f, _ = P16.alloc(2 * 1024, "Wkv")
            Wkv = Wkv_f.rearrange("p (k n) -> p k n", n=1024)
            g_attn, gb1 = P32.alloc(1024)
            g_qa, gb2 = P32.alloc(384)
            g_kva, gb3 = P32.alloc(256)
            g_gq, gb4 = P32.alloc(64)
            g_gk, gb5 = P32.alloc(64)
            bcast_load(g_attn, gb1, g_attn_d[l], 1024)
            bcast_load(g_qa, gb2, g_qa_d[l], 384)
            bcast_load(g_kva, gb3, g_kva_d[l], 256)
            bcast_load(g_gq, gb4, g_gq_d[l], 64)
            bcast_load(g_gk, gb5, g_gk_d[l], 64)
            gbufs = [gb1, gb2, gb3, gb4, gb5]
            P32.mark()
            stages = [P32.alloc(stage_n) for _ in range(3)]
            load_w(Win, w_in_d[l], stages)
            load_w(Wq, w_qb_d[l], stages)
            load_w(Wkv, w_kvb_d[l], stages)
            T.barrier()
            P32.release()

            ctxs = []
            for i in range(2):
                c = Ctx()
                c.x, c.xb = P32.alloc(1024)
                c.csm, c.csmb = P32.alloc(64)
                c.csg, c.csgb = P32.alloc(128)
                c.ss, c.ssb = P32.alloc(1)
                c.sd, c.sdb = P32.alloc(1)
                c.r, c.rb = P32.alloc(1)
                c.ss10, c.ss10b = P32.alloc(10)
                c.sd10, c.sd10b = P32.alloc(10)
                c.r10, c.r10b = P32.alloc(10)
                c.rt1, c.rt1b = P32.alloc(640)
                c.rt2, c.rt2b = P32.alloc(512)
                c.gn, c.gnb = P32.alloc(640)
                c.q32, c.q32b = P32.alloc(768)
                c.sq10, c.sq10b = c.rt1, c.rt1b
                c.kr32, c.kr32b = P32.alloc(32)
                c.junk, c.junkb = P16.alloc(1024)
                c.hn, c.hnb = P16.alloc(1024)
                c.hnT, c.hnTb = P16.alloc(1024)
                c.cqn, c.cqnb = P16.alloc(384)
                c.cqnT, c.cqnTb = P16.alloc(384)
                c.ckvn, c.ckvnb = P16.alloc(256)
                c.ckvnT, c.ckvnTb = P16.alloc(256)
                c.q16, c.q16b = P16.alloc(768)
                c.qT, c.qTb = P16.alloc(1024)
                c.kn, c.knb = P16.alloc(512)
                c.knT, c.knTb = P16.alloc(512)
                c.vst, c.vstb = P16.alloc(650)
                T.op("dve", lambda e, c=c: e.memset(c.vst.rearrange("p (h d) -> p h d", d=65)[:, :, 64:65], 1.0), writes=[c.vstb])
                c.kpe, c.kpeb = P16.alloc(32)
                c.kpeT, c.kpeTb = P16.alloc(128)
                c.gq16, c.gq16b = P16.alloc(512)
                c.gqT, c.gqTb = P16.alloc(512)
                c.gk16, c.gk16b = P16.alloc(128)
                c.gkT, c.gkTb = P16.alloc(128)
                ctxs.append(c)

            tile_i = 0
            for s in range(2):
                for t in range(NT + 1):
                    c = ctxs[tile_i % 2]
                    tile_i += 1
                    is_meta = (t == NT)
                    P = NMETA if is_meta else 128
                    tok0 = t * 128
                    if l == 0:
                        src = meta[:, :] if is_meta else xq[s, tok0:tok0 + P, :]
                    else:
                        src = X1[s][tok0:tok0 + P, :]
                    T.dma("sp", c.x[:P], src, writes=[c.xb])
                    T.dma("sp", c.csm[:P], csm_d[s, tok0:tok0 + P, :], writes=[c.csmb])
                    T.dma("sp", c.csg[:P], csg_d[s, tok0:tok0 + P, :], writes=[c.csgb])
                    rstd(c.x[:P], c.xb, P, 1024, c)
                    T.op("dve", lambda e, c=c, P=P: e.scalar_tensor_tensor(out=c.hn[:P], in0=c.x[:P], scalar=c.r[:P], in1=g_attn[:P],
                                                                         op0=ALU.mult, op1=ALU.mult),
                         reads=[c.xb, c.rb] + gbufs, writes=[c.hnb])
                    hnT3 = c.hnT.rearrange("p (k t) -> p k t", t=128)
                    transposes(c.hn, c.hnb, P, 8, 128, hnT3, c.hnTb)
                    mm_tokmajor(bank[0], bankb[0], hnT3, c.hnTb, P, Win, 8, 0, 512)
                    mm_tokmajor(bank[2], bankb[2], hnT3, c.hnTb, P, Win, 8, 1024, 1440)
                    mm_tokmajor(bank[1], bankb[1], hnT3, c.hnTb, P, Win, 8, 512, 1024)
                    rstd(bank[0][:P, 0:384], bankb[0], P, 384, c)
                    T.op("dve", lambda e, c=c, P=P: e.scalar_tensor_tensor(out=c.cqn[:P], in0=bank[0][:P, 0:384], scalar=c.r[:P],
                                                                         in1=g_qa[:P], op0=ALU.mult, op1=ALU.mult),
                         reads=[bankb[0], c.rb] + gbufs, writes=[c.cqnb])
                    cqnT3 = c.cqnT.rearrange("p (k t) -> p k t", t=128)
                    transposes(c.cqn, c.cqnb, P, 3, 128, cqnT3, c.cqnTb)
                    rstd(bank[2][:P, 0:256], bankb[2], P, 256, c)
                    T.op("dve", lambda e, c=c, P=P: e.scalar_tensor_tensor(out=c.ckvn[:P], in0=bank[2][:P, 0:256], scalar=c.r[:P],
                                                                         in1=g_kva[:P], op0=ALU.mult, op1=ALU.mult),
                         reads=[bankb[2], c.rb] + gbufs, writes=[c.ckvnb])
                    ckvnT3 = c.ckvnT.rearrange("p (k t) -> p k t", t=128)
                    transposes(c.ckvn, c.ckvnb, P, 2, 128, ckvnT3, c.ckvnTb)
                    mm_tokmajor(bank[3], bankb[3], cqnT3, c.cqnTb, P, Wq, 3, 0, 384)
                    mm_tokmajor(bank[4], bankb[4], cqnT3, c.cqnTb, P, Wq, 3, 384, 768)
                    mm_tokmajor(bank[5], bankb[5], ckvnT3, c.ckvnTb, P, Wkv, 2, 0, 512)
                    mm_tokmajor(bank[6], bankb[6], ckvnT3, c.ckvnTb, P, Wkv, 2, 512, 1024)
                    q3 = c.q16.rearrange("p (h d) -> p h d", d=96)
                    q32 = c.q32.rearrange("p (h d) -> p h d", d=96)
                    for hb_, bk in ((0, 3), (1, 4)):
                        T.op("act", lambda e, bk=bk, hb_=hb_, P=P, c=c: e.activation(out=c.q32[:P, hb_ * 384:(hb_ + 1) * 384], in_=bank[bk][:P, 0:384],
                                                                                   func=AF.Copy),
                             reads=[bankb[bk]], writes=[c.q32b])
                    T.op("dve", lambda e, P=P, q3=q3, q32=q32: e.tensor_copy(out=q3[:P, :, 0:64], in_=q32[:P, :, 0:64]),
                         reads=[c.q32b], writes=[c.q16b])
                    rope(q32[:P, :, 64:96], c.q32b, P, 8, 32, 1, c.csm, c.csmb, q3[:P, :, 64:96], c.q16b, c)
                    def fq(e, c=c, P=P):
                        ins = None
                        for h in range(8):
                            ins = e.transpose(out=pT[:96, h * 128:h * 128 + P], in_=c.q16[:P, h * 96:(h + 1) * 96], identity=idb[:P, :P])
                        return ins
                    T.op("pe", fq, reads=[c.q16b, idbb], writes=[pTb])
                    qT3 = c.qT.rearrange("p (h t) -> p h t", t=128)
                    T.op("act", lambda e, P=P, qT3=qT3: e.activation(out=qT3[:96, :, :P],
                                                                   in_=pT[:96, :].rearrange("f (h t) -> f h t", t=128)[:, :, :P], func=AF.Copy),
                         reads=[pTb], writes=[c.qTb])
                    T.dma("pool", QTm[s].rearrange("(h d) t -> d h t", d=96)[:, :, tok0:tok0 + P], qT3[:96, :, :P],
                          reads=[c.qTb], writes=[Buf()])
                    v3 = c.vst.rearrange("p (h d) -> p h d", d=65)
                    for hb_, bk in ((0, 5), (1, 6)):
                        pkv = bank[bk][:P, :].rearrange("p (h d) -> p h d", d=128)
                        T.op("act", lambda e, pkv=pkv, hb_=hb_, P=P, v3=v3: e.activation(out=v3[:P, hb_ * 4:hb_ * 4 + 4, 0:64], in_=pkv[:, :, 64:128],
                                                                                       func=AF.Copy),
                             reads=[bankb[bk]], writes=[c.vstb])
                    T.op("act", lambda e, P=P, v3=v3: e.activation(out=v3[:P, 8:10, 0:64],
                                                                 in_=bank[0][:P, 384:512].rearrange("p (h d) -> p h d", d=64), func=AF.Copy),
                         reads=[bankb[0]], writes=[c.vstb])
                    if is_meta:
                        vdst = Vmeta[s].rearrange("(h t) d -> t h d", t=NMETA)
                    else:
                        vdst = Vloc[s].ap().rearrange("(h t) d -> t h d", t=N_OWN)[tok0:tok0 + P]
                    T.dma("pool", vdst, v3[:P], reads=[c.vstb], writes=[Buf()])
                    kn3 = c.kn.rearrange("p (h d) -> p h d", d=64)
                    for hb_, bk in ((0, 5), (1, 6)):
                        pkv = bank[bk][:P, :].rearrange("p (h d) -> p h d", d=128)
                        T.op("dve", lambda e, pkv=pkv, hb_=hb_, P=P, kn3=kn3: e.tensor_copy(out=kn3[:P, hb_ * 4:hb_ * 4 + 4, :], in_=pkv[:, :, 0:64]),
                             reads=[bankb[bk]], writes=[c.knb])
                    knT3 = c.knT.rearrange("p (j t) -> p j t", t=128)
                    transposes(c.kn, c.knb, P, 4, 128, knT3, c.knTb)
                    ktd = KTmeta[s] if is_meta else KTloc[s].ap()[:, tok0:tok0 + P]
                    T.dma("pool", ktd[0:512].rearrange("(j p) t -> p j t", p=128), knT3[:, :, :P], reads=[c.knTb], writes=[Buf()])
                    kpe3 = c.kpe[:, 0:32].rearrange("p (h d) -> p h d", d=32)
                    T.op("act", lambda e, c=c, P=P: e.activation(out=c.kr32[:P], in_=bank[2][:P, 256:288], func=AF.Copy),
                         reads=[bankb[2]], writes=[c.kr32b])
                    rope(c.kr32[:P].rearrange("p (h d) -> p h d", d=32), c.kr32b, P, 1, 32, 1, c.csm, c.csmb, kpe3[:P], c.kpeb, c)
                    kpeT3 = c.kpeT.rearrange("p (j t) -> p j t", t=128)
                    transposes(c.kpe, c.kpeb, P, 1, 32, kpeT3, c.kpeTb)
                    T.dma("pool", ktd[512:544], kpeT3[:32, 0, :P], reads=[c.kpeTb], writes=[Buf()])
                    gn3 = c.gn[:P, :].rearrange("p (h d) -> p h d", d=64)
                    T.op("act", lambda e, c=c, P=P: e.activation(out=c.gn[:P, 0:512], in_=bank[1][:P, :], func=AF.Copy),
                         reads=[bankb[1]], writes=[c.gnb])
                    T.op("act", lambda e, c=c, P=P: e.activation(out=c.gn[:P, 512:640], in_=bank[2][:P, 288:416], func=AF.Copy),
                         reads=[bankb[2]], writes=[c.gnb])
                    T.op("dve", lambda e, c=c, P=P: e.tensor_tensor(out=c.sq10[:P], in0=c.gn[:P], in1=c.gn[:P], op=ALU.mult),
                         reads=[c.gnb], writes=[c.sq10b])
                    T.op("dve", lambda e, c=c, P=P: e.tensor_reduce(out=c.ss10[:P], in_=c.sq10[:P].rearrange("p (h d) -> p h d", d=64),
                                                                  axis=AX.X, op=ALU.add),
                         reads=[c.sq10b], writes=[c.ss10b])
                    T.op("act", lambda e, c=c, P=P: e.activation(out=c.sd10[:P], in_=c.ss10[:P], func=AF.Sqrt, scale=1.0 / 64, bias=EPS),
                         reads=[c.ss10b], writes=[c.sd10b])
                    T.op("dve", lambda e, c=c, P=P: e.reciprocal(out=c.r10[:P], in_=c.sd10[:P]), reads=[c.sd10b], writes=[c.r10b])
                    T.op("dve", lambda e, c=c, P=P, gn3=gn3: e.tensor_tensor(
                        out=gn3, in0=gn3, in1=c.r10[:P, 0:10].unsqueeze(2).broadcast_to([P, 10, 64]), op=ALU.mult),
                        reads=[c.gnb, c.r10b], writes=[c.gnb])
                    T.op("dve", lambda e, P=P, gn3=gn3: e.tensor_tensor(
                        out=gn3[:, 0:8, :], in0=gn3[:, 0:8, :], in1=g_gq[:P, :].unsqueeze(1).broadcast_to([P, 8, 64]), op=ALU.mult),
                        reads=[c.gnb] + gbufs, writes=[c.gnb])
                    T.op("dve", lambda e, P=P, gn3=gn3: e.tensor_tensor(
                        out=gn3[:, 8:10, :], in0=gn3[:, 8:10, :], in1=g_gk[:P, :].unsqueeze(1).broadcast_to([P, 2, 64]), op=ALU.mult),
                        reads=[c.gnb] + gbufs, writes=[c.gnb])
                    gq16_3 = c.gq16.rearrange("p (h d) -> p h d", d=64)
                    gk16_3 = c.gk16.rearrange("p (h d) -> p h d", d=64)
                    rope(gn3[:, 0:8, :], c.gnb, P, 8, 64, 2, c.csg, c.csgb, gq16_3[:P], c.gq16b, c)
                    rope(gn3[:, 8:10, :], c.gnb, P, 2, 64, 2, c.csg, c.csgb, gk16_3[:P], c.gk16b, c)
                    gqT3 = c.gqT.rearrange("p (j t) -> p j t", t=128)
                    transposes(c.gq16, c.gq16b, P, 4, 128, gqT3, c.gqTb)
                    T.dma("pool", QTg[s].rearrange("(j p) t -> p j t", p=128)[:, :, tok0:tok0 + P], gqT3[:, :, :P],
                          reads=[c.gqTb], writes=[Buf()])
                    gkT3 = c.gkT.rearrange("p (j t) -> p j t", t=128)
                    transposes(c.gk16, c.gk16b, P, 1, 128, gkT3, c.gkTb)
                    T.dma("pool", ktd[544:672], gkT3[:, 0, :P], reads=[c.gkTb], writes=[Buf()])
            T.barrier()
            P16.release()
            P32.release()

        def phase2():
            for s in range(2):
                R = RS[s]
                groups = [list(range(g * R, (g + 1) * R)) for g in range(8 // R)]
                pieces = [(KTloc[s].ap()[a:a + n], KTall[s][i].ap()) for i, (a, n) in enumerate(KPIECES)]
                pieces += [(Vloc[s].ap()[h * N_OWN:(h + 1) * N_OWN], Vall[s][h].ap()) for h in range(10)]
                for src, dst in pieces:
                    T.custom("pool", lambda e, src=src, dst=dst, groups=groups: e.collective_compute(
                        "AllGather", ALU.bypass, replica_groups=groups, ins=[src], outs=[dst]), 1)
            T.barrier()

        def phase3(l):
            P16.mark()
            P32.mark()
            nq_eff = NQ if l == 0 else N_OWN
            ngrp = (nq_eff + 511) // 512
            units = nq_eff // 16
            base = units // ngrp
            widths = [16 * (base + (1 if i < units - base * ngrp else 0)) for i in range(ngrp)]
            assert sum(widths) == nq_eff and max(widths) <= 512
            g0s = [sum(widths[:i]) for i in range(ngrp)]
            LKMAX = 4 * N_OWN + NMETA
            NCHMAX = 4 * NC
            Kt = [(P16.alloc(LKMAX, f"K{i}")[0], [Buf() for _ in range(4)]) for i in range(2)]
            Vt = [P16.alloc(NCHMAX * 65, f"V{i}") for i in range(2)]
            Vm = [P16.alloc(65, f"Vm{i}") for i in range(2)]
            Qt = [P16.alloc(NQ, f"Q{i}") for i in range(2)]
            NPB = 3
            Pt = [P16.alloc(1024, f"P{i}") for i in range(NPB)]
            Osb = [P32.alloc(512, f"Osb{i}") for i in range(2)]
            aost = [P32.alloc(256, f"ao{i}") for i in range(2)]
            rd = [P32.alloc(4, f"rd{i}") for i in range(2)]
            Sb = [(ps32[:, b * 1024:(b + 1) * 1024], Buf(f"S{b}")) for b in range(2)]
            Ob = [(bank[4], bankb[4]), (bank[5], bankb[5])]
            OT, OTb = bank[6], bankb[6]

            kvsets = []
            for s in range(2):
                for h in range(8):
                    kvsets.append((s, "mla", h))
                for j in range(2):
                    kvsets.append((s, "gqa", j))

            def load_kv(idx):
                s, kind, h = kvsets[idx]
                R = RS[s]
                K, Kb = Kt[idx % 2]
                V, Vb = Vt[idx % 2]
                VM, VMb = Vm[idx % 2]
                def kp(i):
                    return KTall[s][i].ap().rearrange("(r f) t -> f r t", f=KPIECES[i][1])
                Kv = K[:, 0:R * N_OWN].rearrange("d (r t) -> d r t", t=N_OWN)
                mcol = slice(R * N_OWN, R * N_OWN + NMETA)
                if kind == "mla":
                    T.dma("sp", Kv[0:64], kp(h), writes=[Kb[0]])
                    T.dma("sp", Kv[64:96], kp(8), writes=[Kb[1]])
                    T.dma("sp", K[0:64, mcol], KTmeta[s][h * 64:(h + 1) * 64, :], writes=[Kb[2]])
                    T.dma("sp", K[64:96, mcol], KTmeta[s][512:544, :], writes=[Kb[3]])
                    hv = h
                else:
                    T.dma("sp", Kv[0:64], kp(9 + h), writes=[Kb[0], Kb[1]])
                    T.dma("sp", K[0:64, mcol], KTmeta[s][544 + h * 64:544 + (h + 1) * 64, :], writes=[Kb[2], Kb[3]])
                    hv = 8 + h
                vall = Vall[s][hv].ap().rearrange("(r p c) d -> p r c d", p=128, c=NC)
                V4 = V[:, 0:R * NC * 65].rearrange("p (r c d) -> p r c d", c=NC, d=65)
                T.dma("sp", V4, vall, writes=[Vb])
                T.dma("sp", VM[0:NMETA, 0:65], Vmeta[s][hv * NMETA:(hv + 1) * NMETA, :], writes=[VMb])

            def qheads(idx):
                s, kind, h = kvsets[idx]
                if kind == "mla":
                    return [(s, "mla", h, h)]
                return [(s, "gqa", h * 4 + g, 8 + h * 4 + g) for g in range(4)]

            qlist = []
            for idx in range(len(kvsets)):
                for qh in qheads(idx):
                    qlist.append((idx, qh))

            def load_q(qi):
                idx, (s, kind, qh, _) = qlist[qi]
                Q, Qb = Qt[qi % 2]
                if kind == "mla":
                    T.dma("sp", Q[0:96, 0:nq_eff], QTm[s][qh * 96:(qh + 1) * 96, 0:nq_eff], writes=[Qb])
                else:
                    T.dma("sp", Q[0:64, 0:nq_eff], QTg[s][qh * 64:(qh + 1) * 64, 0:nq_eff], writes=[Qb])

            steps = []
            for qi, (idx, (s, kind, qh, cb)) in enumerate(qlist):
                R = RS[s]
                nch = R * NC + 1
                for gi in range(ngrp):
                    for c0 in range(0, R * NC, 2):
                        steps.append((qi, gi, [c0, c0 + 1], nch))
                    steps.append((qi, gi, [R * NC], nch))
            LA = 1
            gcount = [0]

            def kcols(idx, cix):
                s, kind, h = kvsets[idx]
                R = RS[s]
                d = 96 if kind == "mla" else 64
                K, Kb = Kt[idx % 2]
                if cix == R * NC:
                    return K[0:d, R * N_OWN:R * N_OWN + NMETA], NMETA, d
                r, cl = divmod(cix, NC)
                return K[0:d, r * N_OWN:(r + 1) * N_OWN].rearrange("d (p c) -> d p c", c=NC)[:, :, cl], 128, d

            def emit_qk(i):
                qi, gi, chunks, nch = steps[i]
                idx, (s, kind, qh, cb) = qlist[qi]
                Q, Qb = Qt[qi % 2]
                G = widths[gi]
                g0 = g0s[gi]
                S, Sbuf = Sb[i % 2]

                def f(e):
                    ins = None
                    for k, cix in enumerate(chunks):
                        lhsT, kc, d = kcols(idx, cix)
                        ins = e.matmul(S[:kc, k * 512:k * 512 + G], lhsT=lhsT, rhs=Q[0:d, g0:g0 + G], start=True, stop=True)
                    return ins
                T.op("pe", f, reads=Kt[idx % 2][1] + [Qb], writes=[Sbuf])

            def emit_exp_pv(i):
                qi, gi, chunks, nch = steps[i]
                idx, (s, kind, qh, cb) = qlist[qi]
                R = RS[s]
                d = 96 if kind == "mla" else 64
                scale = float(d) ** -0.5
                n = len(chunks)
                kc = NMETA if chunks[0] == R * NC else 128
                G = widths[gi]
                S, Sbuf = Sb[i % 2]
                Pp, Pb = Pt[i % NPB]
                gidx = gcount[0]
                O, Obuf = Ob[gidx % 2]
                S3 = S.rearrange("p (k g) -> p k g", g=512)[:kc, 0:n, 0:G]
                P3 = Pp.rearrange("p (k g) -> p k g", g=512)[:kc, 0:n, 0:G]
                T.op("act", lambda e, S3=S3, P3=P3, scale=scale: e.activation(out=P3, in_=S3, func=AF.Exp, scale=scale),
                     reads=[Sbuf], writes=[Pb])
                if kc == 128:
                    V, Vb = Vt[idx % 2]
                    V3 = V.rearrange("p (c d) -> p c d", d=65)
                    lhs = [V3[:, cix, :] for cix in chunks]
                else:
                    V, Vb = Vm[idx % 2]
                    lhs = [V[0:NMETA, 0:65]]

                def fpv(e):
                    ins = None
                    for k, cix in enumerate(chunks):
                        ins = e.matmul(O[0:65, :G], lhsT=lhs[k], rhs=Pp[:kc, k * 512:k * 512 + G],
                                       start=(cix == 0), stop=(cix == nch - 1))
                    return ins
                T.op("pe", fpv, reads=[Vb, Pb], writes=[Obuf])
                cix = chunks[-1]
                if cix == nch - 1:
                    gcount[0] += 1
                    Os, Osbuf = Osb[gidx % 2]
                    T.op("dve", lambda e, Os=Os, O=O, G=G: e.tensor_copy(out=Os[0:65, :G], in_=O[0:65, :G]),
                         reads=[Obuf], writes=[Osbuf])
                    nsub = (G + 127) // 128
                    OT3 = OT[:, 0:4 * 65].rearrange("p (j d) -> p j d", d=65)

                    def ftr(e, Os=Os, G=G, nsub=nsub):
                        ins = None
                        for j in range(nsub):
                            w = min(128, G - j * 128)
                            ins = e.transpose(out=OT3[:w, j, :], in_=Os[0:65, j * 128:j * 128 + w], identity=id32[0:65, 0:65])
                        return ins
                    T.op("pe", ftr, reads=[Osbuf, id32b], writes=[OTb])
                    rdt, rdb = rd[gidx % 2]
                    ao, aob = aost[gidx % 2]
                    ao3 = ao.rearrange("p (j d) -> p j d", d=64)
                    g0 = g0s[gi]
                    full = G // 128
                    rem = G - full * 128
                    parts = []
                    if full:
                        parts.append((128, 0, full))
                    if rem:
                        parts.append((rem, full, full + 1))
                    for (pw, j0, j1) in parts:
                        T.op("dve", lambda e, pw=pw, j0=j0, j1=j1, rdt=rdt: e.reciprocal(out=rdt[:pw, j0:j1], in_=OT3[:pw, j0:j1, 64]),
                             reads=[OTb], writes=[rdb])
                        for jj in range(j0, j1):
                            T.op("dve", lambda e, pw=pw, jj=jj, rdt=rdt, ao3=ao3: e.tensor_scalar(
                                out=ao3[:pw, jj, :], in0=OT3[:pw, jj, 0:64], scalar1=rdt[:pw, jj:jj + 1], scalar2=None, op0=ALU.mult),
                                reads=[OTb, rdb], writes=[aob])
                        if j1 - j0 > 1 or True:
                            dst = AO[s][g0 + j0 * 128:g0 + j0 * 128 + (j1 - j0 - 1) * 128 + pw, cb * 64:(cb + 1) * 64]
                            if j1 - j0 == 1:
                                T.dma("pool", dst, ao3[:pw, j0, :], reads=[aob], writes=[Buf()])
                            else:
                                T.dma("pool", dst.rearrange("(j p) d -> p j d", p=128), ao3[:pw, j0:j1, :], reads=[aob], writes=[Buf()])

            load_kv(0)
            load_q(0)
            started_kv = {0}
            started_q = {0}
            n = len(steps)
            for i in range(-LA, n):
                j = i + LA
                if j < n:
                    qi = steps[j][0]
                    if qi not in started_q:
                        load_q(qi)
                        started_q.add(qi)
                    idx = qlist[qi][0]
                    if idx not in started_kv:
                        load_kv(idx)
                        started_kv.add(idx)
                    emit_qk(j)
                if i >= 0:
                    emit_exp_pv(i)
                    qi = steps[i][0]
                    if steps[i][1] == 0 and steps[i][2][0] == 0:
                        if qi + 1 < len(qlist) and (qi + 1) not in started_q:
                            load_q(qi + 1)
                            started_q.add(qi + 1)
                            nidx = qlist[qi + 1][0]
                            if nidx not in started_kv:
                                load_kv(nidx)
                                started_kv.add(nidx)
            T.barrier()
            P16.release()
            P32.release()

        def phase4(l):
            last = (l == depth - 1)
            P16.mark()
            P32.mark()
            Wo_f, _ = P16.alloc(8 * 1024, "Wo")
            Wo = Wo_f.rearrange("p (k n) -> p k n", n=1024)
            Wu_f, _ = P16.alloc(8 * 4096, "Wu")
            Wu = Wu_f.rearrange("p (k n) -> p k n", n=4096)
            Wd_f, _ = P16.alloc(32 * 1024, "Wd")
            Wd = Wd_f.rearrange("p (k n) -> p k n", n=1024)
            g_out, gb1 = P32.alloc(1024)
            g_mlp, gb2 = P32.alloc(1024)
            bcast_load(g_out, gb1, g_out_d[l], 1024)
            bcast_load(g_mlp, gb2, g_mlp_d[l], 1024)
            gbufs = [gb1, gb2]
            if last:
                g_fin, gb3 = P32.alloc(1024)
                bcast_load(g_fin, gb3, g_fin_d, 1024)
                gbufs.append(gb3)
            P32.mark()
            stages = [P32.alloc(stage_n) for _ in range(3)]
            load_w(Wo, w_out_d[l], stages)
            load_w(Wu, w_up_d[l], stages)
            load_w(Wd, w_down_d[l], stages)
            T.barrier()
            P32.release()

            ctxs = []
            for i in range(2):
                c = Ctx()
                c.x, c.xb = P32.alloc(1024)
                c.ao, c.aob = P32.alloc(1024)
                c.ss, c.ssb = P32.alloc(1)
                c.sd, c.sdb = P32.alloc(1)
                c.r, c.rb = P32.alloc(1)
                c.ss2, c.ss2b = P32.alloc(2)
                c.sd2, c.sd2b = P32.alloc(2)
                c.r2, c.r2b = P32.alloc(2)
                if i == 0:
                    c.junk, c.junkb = P16.alloc(1024)
                    c.mix, c.mixb = P16.alloc(1024)
                    c.mixT, c.mixTb = P16.alloc(1024)
                    c.hn, c.hnb = P16.alloc(1024)
                    c.hnT, c.hnTb = P16.alloc(1024)
                else:
                    for nm in ("junk", "mix", "mixT", "hn", "hnT"):
                        setattr(c, nm, getattr(ctxs[0], nm))
                        setattr(c, nm + "b", getattr(ctxs[0], nm + "b"))
                ctxs.append(c)
            xmid, xmidb = P32.alloc(1024)
            xnew, xnewb = P32.alloc(1024)
            r32 = [P32.alloc(512) for _ in range(2)]
            aT = [P16.alloc(512) for _ in range(2)]
            pso = [(bank[0], bankb[0]), (bank[1], bankb[1])]
            pu = [(bank[2], bankb[2]), (bank[3], bankb[3])]
            py = [(bank[4], bankb[4]), (bank[5], bankb[5])]

            tile_i = 0
            for s in range(2):
                for t in range(NT + (0 if last else 1)):
                    c = ctxs[tile_i % 2]
                    tile_i += 1
                    is_meta = (t == NT)
                    P = NMETA if is_meta else 128
                    tok0 = t * 128
                    if l == 0:
                        src = meta[:, :] if is_meta else xq[s, tok0:tok0 + P, :]
                    else:
                        src = X1[s][tok0:tok0 + P, :]
                    T.dma("sp", c.x[:P], src, writes=[c.xb])
                    T.dma("sp", c.ao[:P], AO[s][tok0:tok0 + P, :], writes=[c.aob])
                    for hf in range(2):
                        T.op("act", lambda e, c=c, P=P, hf=hf: e.activation(out=c.junk[:P, 0:512], in_=c.ao[:P, hf * 512:(hf + 1) * 512],
                                                                          func=AF.Square, accum_out=c.ss2[:P, hf:hf + 1]),
                             reads=[c.aob], writes=[c.junkb, c.ss2b])
                    T.op("act", lambda e, c=c, P=P: e.activation(out=c.sd2[:P], in_=c.ss2[:P], func=AF.Sqrt, scale=1.0 / 512, bias=EPS),
                         reads=[c.ss2b], writes=[c.sd2b])
                    T.op("dve", lambda e, c=c, P=P: e.reciprocal(out=c.r2[:P], in_=c.sd2[:P]), reads=[c.sd2b], writes=[c.r2b])
                    for hf in range(2):
                        sl = slice(hf * 512, (hf + 1) * 512)
                        T.op("dve", lambda e, c=c, P=P, hf=hf, sl=sl: e.scalar_tensor_tensor(
                            out=c.mix[:P, sl], in0=c.ao[:P, sl], scalar=c.r2[:P, hf:hf + 1], in1=g_out[:P, sl], op0=ALU.mult, op1=ALU.mult),
                            reads=[c.aob, c.r2b] + gbufs, writes=[c.mixb])
                    mixT3 = c.mixT.rearrange("p (k t) -> p k t", t=128)
                    transposes(c.mix, c.mixb, P, 8, 128, mixT3, c.mixTb)
                    for j in range(2):
                        mm_tokmajor(pso[j][0], pso[j][1], mixT3, c.mixTb, P, Wo, 8, j * 512, (j + 1) * 512)
                    for j in range(2):
                        sl = slice(j * 512, (j + 1) * 512)
                        T.op("dve", lambda e, c=c, P=P, j=j, sl=sl: e.tensor_tensor(out=xmid[:P, sl], in0=pso[j][0][:P, :], in1=c.x[:P, sl], op=ALU.add),
                             reads=[pso[j][1], c.xb], writes=[xmidb])
                    rstd(xmid[:P], xmidb, P, 1024, c)
                    T.op("dve", lambda e, c=c, P=P: e.scalar_tensor_tensor(out=c.hn[:P], in0=xmid[:P], scalar=c.r[:P], in1=g_mlp[:P],
                                                                         op0=ALU.mult, op1=ALU.mult),
                         reads=[xmidb, c.rb] + gbufs, writes=[c.hnb])
                    hnT3 = c.hnT.rearrange("p (k t) -> p k t", t=128)
                    transposes(c.hn, c.hnb, P, 8, 128, hnT3, c.hnTb)

                    def up(fb, c=c, P=P, hnT3=hnT3):
                        U, Ub = pu[fb % 2]

                        def f(e):
                            ins = None
                            for q in range(4):
                                fidx = fb * 4 + q
                                for k in range(8):
                                    ins = e.matmul(U[:, q * 128:q * 128 + P], lhsT=Wu[:, k, fidx * 128:(fidx + 1) * 128], rhs=hnT3[:, k, :P],
                                                   start=(k == 0), stop=(k == 7))
                            return ins
                        T.op("pe", f, reads=[c.hnTb], writes=[Ub])
                        U3 = U.rearrange("p (q t) -> p q t", t=128)[:, :, :P]
                        R3 = r32[fb % 2][0].rearrange("p (q t) -> p q t", t=128)[:, :, :P]
                        A3 = aT[fb % 2][0].rearrange("p (q t) -> p q t", t=128)[:, :, :P]
                        T.op("act", lambda e: e.activation(out=R3, in_=U3, func=AF.Relu), reads=[Ub], writes=[r32[fb % 2][1]])
                        T.op("dve", lambda e: e.tensor_tensor(out=A3, in0=R3, in1=R3, op=ALU.mult), reads=[r32[fb % 2][1]], writes=[aT[fb % 2][1]])

                    def down(fb, P=P):
                        A3 = aT[fb % 2][0].rearrange("p (q t) -> p q t", t=128)

                        def f(e):
                            ins = None
                            for q in range(4):
                                fidx = fb * 4 + q
                                for j in range(2):
                                    ins = e.matmul(py[j][0][:P, :], lhsT=A3[:, q, :P], rhs=Wd[:, fidx, j * 512:(j + 1) * 512],
                                                   start=(fidx == 0), stop=(fidx == 31))
                            return ins
                        T.op("pe", f, reads=[aT[fb % 2][1]], writes=[py[0][1], py[1][1]])

                    for fb in range(8):
                        up(fb)
                        if fb >= 1:
                            down(fb - 1)
                    down(7)
                    for j in range(2):
                        sl = slice(j * 512, (j + 1) * 512)
                        T.op("dve", lambda e, P=P, j=j, sl=sl: e.tensor_tensor(out=xnew[:P, sl], in0=py[j][0][:P, :], in1=xmid[:P, sl], op=ALU.add),
                             reads=[py[j][1], xmidb], writes=[xnewb])
                    if not last:
                        T.dma("pool", X1[s][tok0:tok0 + P, :], xnew[:P], reads=[xnewb], writes=[Buf()])
                    else:
                        rstd(xnew[:P], xnewb, P, 1024, c)
                        T.op("dve", lambda e, c=c, P=P: e.scalar_tensor_tensor(out=c.ao[:P], in0=xnew[:P], scalar=c.r[:P], in1=g_fin[:P],
                                                                             op0=ALU.mult, op1=ALU.mult),
                             reads=[xnewb, c.rb] + gbufs, writes=[c.aob])
                        T.dma("pool", y_d[s, tok0:tok0 + P, :], c.ao[:P], reads=[c.aob], writes=[Buf()])
            T.barrier()
            P16.release()
            P32.release()

        plist = []
        for l in range(depth):
            plist += [lambda l=l: phase1(l), phase2, lambda l=l: phase3(l), lambda l=l: phase4(l)]
        for ph in plist[:nphase]:
            ph()
        if debug:
            dbg = {}
            for s in range(2):
                for nm, t in (("KTloc", KTloc[s]), ("Vloc", Vloc[s])):
                    o = nc.dram_tensor(f"dbg_{nm}{s}", list(t.ap().shape), BF16, kind="ExternalOutput")
                    T.dma("pool", o.ap(), t.ap())
            T.barrier()

        @block.sync
        def _(e):
            T.replay("sp", e)

        @block.tensor
        def _(e):
            T.replay("pe", e)

        @block.scalar
        def _(e):
            T.replay("act", e)

        @block.vector
        def _(e):
            T.replay("dve", e)

        @block.gpsimd
        def _(e):
            T.replay("pool", e)
    return nc


def _inv_freq(dim):
    return (np.float32(1.0) / np.power(np.float32(10000.0), np.arange(0, dim, 2, dtype=np.float32) / np.float32(dim))).astype(np.float32)


def _tables(pos, rows, cols):
    def cs(p, f):
        ang = (p[:, None].astype(np.float32) * f[None, :].astype(np.float32)).astype(np.float32)
        a = np.concatenate([ang, ang], axis=-1).astype(np.float64)
        c, s = np.cos(a), np.sin(a)
        h = ang.shape[1]
        sp = np.concatenate([-s[:, :h], s[:, h:]], axis=-1)
        return c.astype(np.float32), sp.astype(np.float32)
    cm, sm = cs(pos, _inv_freq(32))
    cr, sr = cs(rows, _inv_freq(32))
    cc, sc = cs(cols, _inv_freq(32))
    csm = np.concatenate([cm, sm], axis=-1)
    csg = np.concatenate([cr, cc, sr, sc], axis=-1)
    return np.ascontiguousarray(csm, np.float32), np.ascontiguousarray(csg, np.float32)


_PERM = np.concatenate([np.arange(0, 384), np.arange(1312, 1440), np.arange(672, 1184),
                        np.arange(384, 640), np.arange(640, 672), np.arange(1184, 1312)])

_NC_CACHE = {}


def run_model(x_prompt, x_sample, meta_tokens, attn_norm_g, w_in, q_a_norm_g, w_q_b, kv_a_norm_g, w_kv_b,
              gqa_q_norm_g, gqa_k_norm_g, mla_out_norm_g, gqa_out_norm_g, w_out, mlp_norm_g, w_up, w_down,
              final_norm_g, trace=False, nphase=None, debug=False):
    f = lambda a: np.ascontiguousarray(np.asarray(a), dtype=np.float32)
    x_prompt, x_sample = f(x_prompt), f(x_sample)
    B, n_long, _ = x_prompt.shape
    Bs, n_short, _ = x_sample.shape
    assert B == 2 and Bs == 4 and n_long == 2 * n_short
    N_OWN = n_long // 4
    depth = np.asarray(w_in).shape[0]
    shared = {
        "meta": f(meta_tokens), "ident": np.eye(128, dtype=np.float32),
        "w_in": np.ascontiguousarray(f(w_in)[:, :, _PERM]), "w_q_b": f(w_q_b), "w_kv_b": f(w_kv_b), "w_out": f(w_out),
        "w_up": f(w_up), "w_down": f(w_down), "attn_norm_g": f(attn_norm_g), "q_a_norm_g": f(q_a_norm_g),
        "kv_a_norm_g": f(kv_a_norm_g), "gqa_q_norm_g": f(gqa_q_norm_g), "gqa_k_norm_g": f(gqa_k_norm_g),
        "out_norm_g": np.ascontiguousarray(np.concatenate([f(mla_out_norm_g), f(gqa_out_norm_g)], axis=-1)),
        "mlp_norm_g": f(mlp_norm_g), "final_norm_g": f(final_norm_g),
    }
    in_maps = []
    for c in range(8):
        bl, rl = c // 4, c % 4
        bs, rs = c // 2, c % 2
        xq = np.stack([x_prompt[bl, rl * N_OWN:(rl + 1) * N_OWN], x_sample[bs, rs * N_OWN:(rs + 1) * N_OWN]])
        csm, csg = [], []
        for r in (rl, rs):
            t = np.arange(r * N_OWN, (r + 1) * N_OWN, dtype=np.float32)
            pos = np.concatenate([t + np.float32(NMETA), np.arange(NMETA, dtype=np.float32)])
            rows = np.concatenate([np.floor(t / 64.0), np.zeros(NMETA)]).astype(np.float32)
            cols = np.concatenate([np.mod(t, 64.0), np.zeros(NMETA)]).astype(np.float32)
            a, b = _tables(pos, rows, cols)
            csm.append(a)
            csg.append(b)
        m = dict(shared)
        m["xq"] = np.ascontiguousarray(xq)
        m["csm"] = np.stack(csm)
        m["csg"] = np.stack(csg)
        import os as _os3
        if _os3.environ.get("K_PAD"):
            m["pad"] = np.zeros((int(_os3.environ["K_PAD"]), 1024), np.float32)
        in_maps.append(m)
    key = (N_OWN, depth, nphase, debug)
    if key not in _NC_CACHE:
        _NC_CACHE[key] = build(N_OWN, depth, nphase=nphase, debug=debug)
    nc = _NC_CACHE[key]
    res = run_bass_kernel_spmd(nc, in_maps, core_ids=list(range(8)), trace=trace)
    y_prompt = np.empty_like(x_prompt)
    y_sample = np.empty_like(x_sample)
    for c in range(8):
        y = np.asarray(res.results[c]["y"], dtype=np.float32)
        y_prompt[c // 4, (c % 4) * N_OWN:(c % 4 + 1) * N_OWN] = y[0]
        y_sample[c // 2, (c % 2) * N_OWN:(c % 2 + 1) * N_OWN] = y[1]
    return (y_prompt, y_sample), res


def kernel(**inputs):
    out, _ = run_model(**inputs)
    return out
```

```python
import contextlib
import numpy as np
import ml_dtypes
import concourse.bass as bass
import concourse.mybir as mybir
from concourse.bass_utils import run_bass_kernel_spmd

F32 = mybir.dt.float32
BF16 = mybir.dt.bfloat16
AF = mybir.ActivationFunctionType
ALU = mybir.AluOpType
AX = mybir.AxisListType

ENGS = ("pe", "act", "dve", "pool", "sp")
D = 1024
EPS = 1e-6
NMETA = 16


class Tok:
    __slots__ = ("eng", "sem", "val", "dma")

    def __init__(self, eng, sem, val, dma):
        self.eng, self.sem, self.val, self.dma = eng, sem, val, dma


class Buf:
    __slots__ = ("name", "w", "r")

    def __init__(self, name=""):
        self.name = name
        self.w = None
        self.r = {}


class Tracker:
    def __init__(self, sems, rings):
        self.sem = sems
        self.rings = rings
        self.streams = {e: [] for e in ENGS}
        self.cnt = {e: 0 for e in ENGS}
        self.waited = {e: {} for e in ENGS}
        self.ring_idx = {q: 0 for q in rings}
        self.ring_val = {}
        self.ring_tok = {}

    def _need(self, eng, tok, waits):
        if tok is None:
            return
        if (not tok.dma) and tok.eng == eng and eng == "pe":
            return
        w = self.waited[eng]
        key = id(tok.sem)
        if w.get(key, 0) >= tok.val:
            return
        w[key] = tok.val
        waits.append((tok.sem, tok.val))

    def _deps(self, eng, reads, writes):
        waits = []
        for b in reads:
            self._need(eng, b.w, waits)
        for b in writes:
            self._need(eng, b.w, waits)
            for t in b.r.values():
                self._need(eng, t, waits)
        return waits

    def _commit(self, tok, reads, writes):
        k = id(tok.sem)
        for b in reads:
            o = b.r.get(k)
            if o is None or o.val < tok.val:
                b.r[k] = tok
        for b in writes:
            b.w = tok
            b.r = {}

    def _skip(self):
        self.nrec = getattr(self, "nrec", 0) + 1
        return self.nrec > getattr(self, "maxops", 1 << 60)

    def op(self, eng, fn, reads=(), writes=()):
        if self._skip():
            return None
        waits = self._deps(eng, reads, writes)
        self.cnt[eng] += 1
        tok = Tok(eng, self.sem[eng], self.cnt[eng], False)
        self.streams[eng].append((waits, fn, self.sem[eng], 1))
        self._commit(tok, reads, writes)
        return tok

    def _ring(self, q, fn, inc, reads, writes):
        if self._skip():
            return None
        ring = self.rings[q]
        sem = ring[self.ring_idx[q] % len(ring)]
        self.ring_idx[q] += 1
        waits = self._deps(q, reads, writes)
        prev = self.ring_tok.get(id(sem))
        if prev is not None:
            self._need(q, prev, waits)
        val = self.ring_val.get(id(sem), 0) + inc
        self.ring_val[id(sem)] = val
        tok = Tok(q, sem, val, True)
        self.ring_tok[id(sem)] = tok
        self.streams[q].append((waits, fn, sem, inc))
        self._commit(tok, reads, writes)
        return tok

    def dma(self, q, out, in_, reads=(), writes=()):
        return self._ring(q, lambda e, out=out, in_=in_: e.dma_start(out=out, in_=in_), 16, reads, writes)

    def custom(self, q, fn, inc, reads=(), writes=(), ring="cc"):
        if self._skip():
            return None
        rg = self.rings[ring]
        sem = rg[self.ring_idx[ring] % len(rg)]
        self.ring_idx[ring] += 1
        waits = self._deps(q, reads, writes)
        prev = self.ring_tok.get(id(sem))
        if prev is not None:
            self._need(q, prev, waits)
        val = self.ring_val.get(id(sem), 0) + inc
        self.ring_val[id(sem)] = val
        tok = Tok(q, sem, val, True)
        self.ring_tok[id(sem)] = tok
        self.streams[q].append((waits, fn, sem, inc))
        self._commit(tok, reads, writes)
        return tok

    def barrier(self):
        toks = []
        for e in ENGS:
            if self.cnt[e] > 0:
                toks.append(Tok(e, self.sem[e], self.cnt[e], False))
        toks.extend(self.ring_tok.values())
        for e in ENGS:
            waits = []
            for t in toks:
                if (not t.dma) and t.eng == e:
                    continue
                self._need(e, t, waits)
            if waits:
                self.streams[e].append((waits, None, None, 0))

    def replay(self, eng, e):
        for waits, fn, sem, inc in self.streams[eng]:
            for s, v in waits:
                e.wait_ge(s, v)
            if fn is not None:
                fn(e).then_inc(sem, inc)


class Pool:
    def __init__(self, tensor, size):
        self.t, self.size, self.off, self.marks = tensor, size, 0, []

    def alloc(self, n, name=""):
        a = self.off
        self.off += n
        assert self.off <= self.size, (name, self.off, self.size)
        return self.t[:, a:a + n], Buf(name)

    def mark(self):
        self.marks.append(self.off)

    def release(self):
        self.off = self.marks.pop()


class Ctx:
    pass


def build(N_OWN, depth=2, N16=80100, N32=10500, nphase=None, debug=False):
    NT = N_OWN // 128
    NC = NT
    NQ = N_OWN + NMETA
    RS = (4, 2)
    nc = bass.Bass("TRN2", target_bir_lowering=False)

    def din(name, shape, dt=F32):
        return nc.dram_tensor(name, list(shape), dt, kind="ExternalInput").ap()

    xq = din("xq", [2, N_OWN, D])
    meta = din("meta", [NMETA, D])
    ident = din("ident", [128, 128])
    csm_d = din("csm", [2, NQ, 64])
    csg_d = din("csg", [2, NQ, 128])
    w_in_d = din("w_in", [depth, D, 1440])
    w_qb_d = din("w_q_b", [depth, 384, 768])
    w_kvb_d = din("w_kv_b", [depth, 256, 1024])
    w_out_d = din("w_out", [depth, D, D])
    w_up_d = din("w_up", [depth, D, 4096])
    w_down_d = din("w_down", [depth, 4096, D])
    g_attn_d = din("attn_norm_g", [depth, D])
    g_qa_d = din("q_a_norm_g", [depth, 384])
    g_kva_d = din("kv_a_norm_g", [depth, 256])
    g_gq_d = din("gqa_q_norm_g", [depth, 64])
    g_gk_d = din("gqa_k_norm_g", [depth, 64])
    g_out_d = din("out_norm_g", [depth, D])
    g_mlp_d = din("mlp_norm_g", [depth, D])
    g_fin_d = din("final_norm_g", [D])
    y_d = nc.dram_tensor("y", [2, N_OWN, D], F32, kind="ExternalOutput").ap()
    import os as _os2
    if _os2.environ.get("K_PAD"):
        din("pad", [int(_os2.environ["K_PAD"]), 1024])

    dk = dict(kind="ExternalOutput") if debug else {}
    QTm = [nc.dram_tensor(f"QTm{s}", [8 * 96, NQ], BF16, **dk).ap() for s in range(2)]
    QTg = [nc.dram_tensor(f"QTg{s}", [8 * 64, NQ], BF16, **dk).ap() for s in range(2)]
    KTloc = [nc.dram_tensor(f"KTloc{s}", [672, N_OWN], BF16) for s in range(2)]
    Vloc = [nc.dram_tensor(f"Vloc{s}", [10 * N_OWN, 65], BF16) for s in range(2)]
    KPIECES = [(h * 64, 64) for h in range(8)] + [(512, 32), (544, 64), (608, 64)]
    KTall = [[nc.dram_tensor(f"KTall{s}_{i}", [RS[s] * n, N_OWN], BF16) for i, (a, n) in enumerate(KPIECES)] for s in range(2)]
    Vall = [[nc.dram_tensor(f"Vall{s}_{h}", [RS[s] * N_OWN, 65], BF16) for h in range(10)] for s in range(2)]
    KTmeta = [nc.dram_tensor(f"KTmeta{s}", [672, NMETA], BF16, **dk).ap() for s in range(2)]
    Vmeta = [nc.dram_tensor(f"Vmeta{s}", [10 * NMETA, 65], BF16, **dk).ap() for s in range(2)]
    AO = [nc.dram_tensor(f"AO{s}", [NQ, D], F32, **dk).ap() for s in range(2)]
    X1 = [nc.dram_tensor(f"X1{s}", [NQ, D], F32, **dk).ap() for s in range(2)]

    es = contextlib.ExitStack()
    with es:
        sb32 = es.enter_context(nc.sbuf_tensor("sb32", [128, N32], F32))
        sb16 = es.enter_context(nc.sbuf_tensor("sb16", [128, N16], BF16))
        ps32 = es.enter_context(nc.psum_tensor("ps32", [128, 7 * 512], F32))
        ps16 = es.enter_context(nc.psum_tensor("ps16", [128, 1024], BF16))
        sems = {e: es.enter_context(nc.semaphore("s_" + e)) for e in ENGS}
        rings = {q: [es.enter_context(nc.semaphore(f"r_{q}{i}")) for i in range(8 if q != "cc" else 4)] for q in ("sp", "pool", "cc")}
        block = es.enter_context(nc.Block())
        T = Tracker(sems, rings)
        import os as _os
        if _os.environ.get("K_MAXOPS"):
            T.maxops = int(_os.environ["K_MAXOPS"])
        P32 = Pool(sb32, N32)
        P16 = Pool(sb16, N16)
        bank = [ps32[:, i * 512:(i + 1) * 512] for i in range(7)]
        bankb = [Buf(f"bank{i}") for i in range(7)]
        pT = ps16
        pTb = Buf("pT16")

        id32, id32b = P32.alloc(128, "id32")
        idb, idbb = P16.alloc(128, "idb")
        T.dma("sp", id32, ident, writes=[id32b])
        T.op("dve", lambda e: e.tensor_copy(out=idb, in_=id32), reads=[id32b], writes=[idbb])

        def bcast_load(dst, dbuf, src1d, n):
            T.dma("sp", dst[:, :n], src1d.partition_broadcast(128), writes=[dbuf])

        stage_n = 2048

        def load_w(dst3, src2, stages, engs=("dve", "act")):
            rows, N = src2.shape
            KC = (rows + 127) // 128
            i = 0
            for k in range(KC):
                pr = min(128, rows - k * 128)
                for n0 in range(0, N, stage_n):
                    n1 = min(N, n0 + stage_n)
                    st, stb = stages[load_w.i % len(stages)]
                    eng = engs[load_w.i % len(engs)]
                    load_w.i += 1
                    T.dma("sp", st[:pr, :n1 - n0], src2[k * 128:k * 128 + pr, n0:n1], writes=[stb])
                    if eng == "dve":
                        T.op("dve", lambda e, o=dst3[:pr, k, n0:n1], i_=st[:pr, :n1 - n0]: e.tensor_copy(out=o, in_=i_),
                             reads=[stb], writes=[Buf()])
                    else:
                        T.op("act", lambda e, o=dst3[:pr, k, n0:n1], i_=st[:pr, :n1 - n0]: e.activation(out=o, in_=i_, func=AF.Copy),
                             reads=[stb], writes=[Buf()])
        load_w.i = 0

        def rstd(src, srcb, P, n, c):
            T.op("act", lambda e: e.activation(out=c.junk[:P, :src.shape[-1]] if len(src.shape) == 2 else c.junk[:P, :src.shape[-1]],
                                               in_=src, func=AF.Square, accum_out=c.ss[:P]),
                 reads=[srcb], writes=[c.junkb, c.ssb])
            T.op("act", lambda e: e.activation(out=c.sd[:P], in_=c.ss[:P], func=AF.Sqrt, scale=1.0 / n, bias=EPS),
                 reads=[c.ssb], writes=[c.sdb])
            T.op("dve", lambda e: e.reciprocal(out=c.r[:P], in_=c.sd[:P]), reads=[c.sdb], writes=[c.rb])

        def transposes(src16, srcb, P, nblk, width, dstT, dstTb):
            def f(e):
                ins = None
                for j in range(nblk):
                    ins = e.transpose(out=pT[:width, j * 128:j * 128 + P], in_=src16[:P, j * width:(j + 1) * width],
                                      identity=idb[:P, :P])
                return ins
            T.op("pe", f, reads=[srcb, idbb], writes=[pTb])
            pv = pT[:width, :nblk * 128].rearrange("f (j p) -> f j p", p=128)[:, :, :P]
            T.op("act", lambda e: e.activation(out=dstT[:width, :nblk, :P], in_=pv, func=AF.Copy),
                 reads=[pTb], writes=[dstTb])

        def mm_tokmajor(out_ps, outb, lhsT3, lhsTb, P, W3, kc, c0, c1):
            def f(e):
                ins = None
                for k in range(kc):
                    ins = e.matmul(out_ps[:P, :c1 - c0], lhsT=lhsT3[:, k, :P], rhs=W3[:, k, c0:c1],
                                   start=(k == 0), stop=(k == kc - 1))
                return ins
            T.op("pe", f, reads=[lhsTb], writes=[outb])

        def rope(src3, srcb, P, H, Dh, blocks, cs, csb, out3, outb, c):
            t1 = c.rt1[:P, :H * Dh].rearrange("p (h d) -> p h d", d=Dh)
            t2 = c.rt2[:P, :H * Dh].rearrange("p (h d) -> p h d", d=Dh)
            cosb = cs[:P, 0:Dh].unsqueeze(1).broadcast_to([P, H, Dh])
            T.op("dve", lambda e: e.tensor_tensor(out=t1, in0=src3, in1=cosb, op=ALU.mult),
                 reads=[srcb, csb], writes=[c.rt1b])
            hb = Dh // blocks // 2
            for b in range(blocks):
                lo = b * 2 * hb
                s_lo = cs[:P, Dh + lo:Dh + lo + hb].unsqueeze(1).broadcast_to([P, H, hb])
                s_hi = cs[:P, Dh + lo + hb:Dh + lo + 2 * hb].unsqueeze(1).broadcast_to([P, H, hb])
                T.op("dve", lambda e, lo=lo, s_lo=s_lo: e.tensor_tensor(out=t2[:, :, lo:lo + hb], in0=src3[:, :, lo + hb:lo + 2 * hb],
                                                                       in1=s_lo, op=ALU.mult),
                     reads=[srcb, csb], writes=[c.rt2b])
                T.op("dve", lambda e, lo=lo, s_hi=s_hi: e.tensor_tensor(out=t2[:, :, lo + hb:lo + 2 * hb], in0=src3[:, :, lo:lo + hb],
                                                                       in1=s_hi, op=ALU.mult),
                     reads=[srcb, csb], writes=[c.rt2b])
            T.op("dve", lambda e: e.tensor_tensor(out=out3, in0=t1, in1=t2, op=ALU.add),
                 reads=[c.rt1b, c.rt2b], writes=[outb])

        def phase1(l):
            P16.mark()
            P32.mark()
            Win_f, _ = P16.alloc(8 * 1440, "Win")
            Win = Win_f.rearrange("p (k n) -> p k n", n=1440)
            Wq_f, _ = P16.alloc(3 * 768, "Wq")
            Wq = Wq_f.rearrange("p (k n) -> p k n", n=768)
            Wkv_f, _ = P16.alloc(2 * 1024, "Wkv")
            Wkv = Wkv_f.rearrange("p (k n) -> p k n", n=1024)
            g_attn, gb1 = P32.alloc(1024)
            g_qa, gb2 = P32.alloc(384)
            g_kva, gb3 = P32.alloc(256)
            g_gq, gb4 = P32.alloc(64)
            g_gk, gb5 = P32.alloc(64)
            bcast_load(g_attn, gb1, g_attn_d[l], 1024)
            bcast_load(g_qa, gb2, g_qa_d[l], 384)
            bcast_load(g_kva, gb3, g_kva_d[l], 256)
            bcast_load(g_gq, gb4, g_gq_d[l], 64)
            bcast_load(g_gk, gb5, g_gk_d[l], 64)
            gbufs = [gb1, gb2, gb3, gb4, gb5]
            P32.mark()
            stages = [P32.alloc(stage_n) for _ in range(3)]
            load_w(Win, w_in_d[l], stages)
            load_w(Wq, w_qb_d[l], stages)
            load_w(Wkv, w_kvb_d[l], stages)
            T.barrier()
            P32.release()

            ctxs = []
            for i in range(2):
                c = Ctx()
                c.x, c.xb = P32.alloc(1024)
                c.csm, c.csmb = P32.alloc(64)
                c.csg, c.csgb = P32.alloc(128)
                c.ss, c.ssb = P32.alloc(1)
                c.sd, c.sdb = P32.alloc(1)
                c.r, c.rb = P32.alloc(1)
                c.ss10, c.ss10b = P32.alloc(10)
                c.sd10, c.sd10b = P32.alloc(10)
                c.r10, c.r10b = P32.alloc(10)
                c.rt1, c.rt1b = P32.alloc(640)
                c.rt2, c.rt2b = P32.alloc(512)
                c.gn, c.gnb = P32.alloc(640)
                c.q32, c.q32b = P32.alloc(768)
                c.sq10, c.sq10b = c.rt1, c.rt1b
                c.kr32, c.kr32b = P32.alloc(32)
                c.junk, c.junkb = P16.alloc(1024)
                c.hn, c.hnb = P16.alloc(1024)
                c.hnT, c.hnTb = P16.alloc(1024)
                c.cqn, c.cqnb = P16.alloc(384)
                c.cqnT, c.cqnTb = P16.alloc(384)
                c.ckvn, c.ckvnb = P16.alloc(256)
                c.ckvnT, c.ckvnTb = P16.alloc(256)
                c.q16, c.q16b = P16.alloc(768)
                c.qT, c.qTb = P16.alloc(1024)
                c.kn, c.knb = P16.alloc(512)
                c.knT, c.knTb = P16.alloc(512)
                c.vst, c.vstb = P16.alloc(650)
                T.op("dve", lambda e, c=c: e.memset(c.vst.rearrange("p (h d) -> p h d", d=65)[:, :, 64:65], 1.0), writes=[c.vstb])
                c.kpe, c.kpeb = P16.alloc(32)
                c.kpeT, c.kpeTb = P16.alloc(128)
                c.gq16, c.gq16b = P16.alloc(512)
                c.gqT, c.gqTb = P16.alloc(512)
                c.gk16, c.gk16b = P16.alloc(128)
                c.gkT, c.gkTb = P16.alloc(128)
                ctxs.append(c)

            tile_i = 0
            for s in range(2):
                for t in range(NT + 1):
                    c = ctxs[tile_i % 2]
                    tile_i += 1
                    is_meta = (t == NT)
                    P = NMETA if is_meta else 128
                    tok0 = t * 128
                    if l == 0:
                        src = meta[:, :] if is_meta else xq[s, tok0:tok0 + P, :]
                    else:
                        src = X1[s][tok0:tok0 + P, :]
                    T.dma("sp", c.x[:P], src, writes=[c.xb])
                    T.dma("sp", c.csm[:P], csm_d[s, tok0:tok0 + P, :], writes=[c.csmb])
                    T.dma("sp", c.csg[:P], csg_d[s, tok0:tok0 + P, :], writes=[c.csgb])
                    rstd(c.x[:P], c.xb, P, 1024, c)
                    T.op("dve", lambda e, c=c, P=P: e.scalar_tensor_tensor(out=c.hn[:P], in0=c.x[:P], scalar=c.r[:P], in1=g_attn[:P],
                                                                         op0=ALU.mult, op1=ALU.mult),
                         reads=[c.xb, c.rb] + gbufs, writes=[c.hnb])
                    hnT3 = c.hnT.rearrange("p (k t) -> p k t", t=128)
                    transposes(c.hn, c.hnb, P, 8, 128, hnT3, c.hnTb)
                    mm_tokmajor(bank[0], bankb[0], hnT3, c.hnTb, P, Win, 8, 0, 512)
                    mm_tokmajor(bank[2], bankb[2], hnT3, c.hnTb, P, Win, 8, 1024, 1440)
                    mm_tokmajor(bank[1], bankb[1], hnT3, c.hnTb, P, Win, 8, 512, 1024)
                    rstd(bank[0][:P, 0:384], bankb[0], P, 384, c)
                    T.op("dve", lambda e, c=c, P=P: e.scalar_tensor_tensor(out=c.cqn[:P], in0=bank[0][:P, 0:384], scalar=c.r[:P],
                                                                         in1=g_qa[:P], op0=ALU.mult, op1=ALU.mult),
                         reads=[bankb[0], c.rb] + gbufs, writes=[c.cqnb])
                    cqnT3 = c.cqnT.rearrange("p (k t) -> p k t", t=128)
                    transposes(c.cqn, c.cqnb, P, 3, 128, cqnT3, c.cqnTb)
                    rstd(bank[2][:P, 0:256], bankb[2], P, 256, c)
                    T.op("dve", lambda e, c=c, P=P: e.scalar_tensor_tensor(out=c.ckvn[:P], in0=bank[2][:P, 0:256], scalar=c.r[:P],
                                                                         in1=g_kva[:P], op0=ALU.mult, op1=ALU.mult),
                         reads=[bankb[2], c.rb] + gbufs, writes=[c.ckvnb])
                    ckvnT3 = c.ckvnT.rearrange("p (k t) -> p k t", t=128)
                    transposes(c.ckvn, c.ckvnb, P, 2, 128, ckvnT3, c.ckvnTb)
                    mm_tokmajor(bank[3], bankb[3], cqnT3, c.cqnTb, P, Wq, 3, 0, 384)
                    mm_tokmajor(bank[4], bankb[4], cqnT3, c.cqnTb, P, Wq, 3, 384, 768)
                    mm_tokmajor(bank[5], bankb[5], ckvnT3, c.ckvnTb, P, Wkv, 2, 0, 512)
                    mm_tokmajor(bank[6], bankb[6], ckvnT3, c.ckvnTb, P, Wkv, 2, 512, 1024)
                    q3 = c.q16.rearrange("p (h d) -> p h d", d=96)
                    q32 = c.q32.rearrange("p (h d) -> p h d", d=96)
                    for hb_, bk in ((0, 3), (1, 4)):
                        T.op("act", lambda e, bk=bk, hb_=hb_, P=P, c=c: e.activation(out=c.q32[:P, hb_ * 384:(hb_ + 1) * 384], in_=bank[bk][:P, 0:384],
                                                                                   func=AF.Copy),
                             reads=[bankb[bk]], writes=[c.q32b])
                    T.op("dve", lambda e, P=P, q3=q3, q32=q32: e.tensor_copy(out=q3[:P, :, 0:64], in_=q32[:P, :, 0:64]),
                         reads=[c.q32b], writes=[c.q16b])
                    rope(q32[:P, :, 64:96], c.q32b, P, 8, 32, 1, c.csm, c.csmb, q3[:P, :, 64:96], c.q16b, c)
                    def fq(e, c=c, P=P):
                        ins = None
                        for h in range(8):
                            ins = e.transpose(out=pT[:96, h * 128:h * 128 + P], in_=c.q16[:P, h * 96:(h + 1) * 96], identity=idb[:P, :P])
                        return ins
                    T.op("pe", fq, reads=[c.q16b, idbb], writes=[pTb])
                    qT3 = c.qT.rearrange("p (h t) -> p h t", t=128)
                    T.op("act", lambda e, P=P, qT3=qT3: e.activation(out=qT3[:96, :, :P],
                                                                   in_=pT[:96, :].rearrange("f (h t) -> f h t", t=128)[:, :, :P], func=AF.Copy),
                         reads=[pTb], writes=[c.qTb])
                    T.dma("pool", QTm[s].rearrange("(h d) t -> d h t", d=96)[:, :, tok0:tok0 + P], qT3[:96, :, :P],
                          reads=[c.qTb], writes=[Buf()])
                    v3 = c.vst.rearrange("p (h d) -> p h d", d=65)
                    for hb_, bk in ((0, 5), (1, 6)):
                        pkv = bank[bk][:P, :].rearrange("p (h d) -> p h d", d=128)
                        T.op("act", lambda e, pkv=pkv, hb_=hb_, P=P, v3=v3: e.activation(out=v3[:P, hb_ * 4:hb_ * 4 + 4, 0:64], in_=pkv[:, :, 64:128],
                                                                                       func=AF.Copy),
                             reads=[bankb[bk]], writes=[c.vstb])
                    T.op("act", lambda e, P=P, v3=v3: e.activation(out=v3[:P, 8:10, 0:64],
                                                                 in_=bank[0][:P, 384:512].rearrange("p (h d) -> p h d", d=64), func=AF.Copy),
                         reads=[bankb[0]], writes=[c.vstb])
                    if is_meta:
                        vdst = Vmeta[s].rearrange("(h t) d -> t h d", t=NMETA)
                    else:
                        vdst = Vloc[s].ap().rearrange("(h t) d -> t h d", t=N_OWN)[tok0:tok0 + P]
                    T.dma("pool", vdst, v3[:P], reads=[c.vstb], writes=[Buf()])
                    kn3 = c.kn.rearrange("p (h d) -> p h d", d=64)
                    for hb_, bk in ((0, 5), (1, 6)):
                        pkv = bank[bk][:P, :].rearrange("p (h d) -> p h d", d=128)
                        T.op("dve", lambda e, pkv=pkv, hb_=hb_, P=P, kn3=kn3: e.tensor_copy(out=kn3[:P, hb_ * 4:hb_ * 4 + 4, :], in_=pkv[:, :, 0:64]),
                             reads=[bankb[bk]], writes=[c.knb])
                    knT3 = c.knT.rearrange("p (j t) -> p j t", t=128)
                    transposes(c.kn, c.knb, P, 4, 128, knT3, c.knTb)
                    ktd = KTmeta[s] if is_meta else KTloc[s].ap()[:, tok0:tok0 + P]
                    T.dma("pool", ktd[0:512].rearrange("(j p) t -> p j t", p=128), knT3[:, :, :P], reads=[c.knTb], writes=[Buf()])
                    kpe3 = c.kpe[:, 0:32].rearrange("p (h d) -> p h d", d=32)
                    T.op("act", lambda e, c=c, P=P: e.activation(out=c.kr32[:P], in_=bank[2][:P, 256:288], func=AF.Copy),
                         reads=[bankb[2]], writes=[c.kr32b])
                    rope(c.kr32[:P].rearrange("p (h d) -> p h d", d=32), c.kr32b, P, 1, 32, 1, c.csm, c.csmb, kpe3[:P], c.kpeb, c)
                    kpeT3 = c.kpeT.rearrange("p (j t) -> p j t", t=128)
                    transposes(c.kpe, c.kpeb, P, 1, 32, kpeT3, c.kpeTb)
                    T.dma("pool", ktd[512:544], kpeT3[:32, 0, :P], reads=[c.kpeTb], writes=[Buf()])
                    gn3 = c.gn[:P, :].rearrange("p (h d) -> p h d", d=64)
                    T.op("act", lambda e, c=c, P=P: e.activation(out=c.gn[:P, 0:512], in_=bank[1][:P, :], func=AF.Copy),
                         reads=[bankb[1]], writes=[c.gnb])
                    T.op("act", lambda e, c=c, P=P: e.activation(out=c.gn[:P, 512:640], in_=bank[2][:P, 288:416], func=AF.Copy),
                         reads=[bankb[2]], writes=[c.gnb])
                    T.op("dve", lambda e, c=c, P=P: e.tensor_tensor(out=c.sq10[:P], in0=c.gn[:P], in1=c.gn[:P], op=ALU.mult),
                         reads=[c.gnb], writes=[c.sq10b])
                    T.op("dve", lambda e, c=c, P=P: e.tensor_reduce(out=c.ss10[:P], in_=c.sq10[:P].rearrange("p (h d) -> p h d", d=64),
                                                                  axis=AX.X, op=ALU.add),
                         reads=[c.sq10b], writes=[c.ss10b])
                    T.op("act", lambda e, c=c, P=P: e.activation(out=c.sd10[:P], in_=c.ss10[:P], func=AF.Sqrt, scale=1.0 / 64, bias=EPS),
                         reads=[c.ss10b], writes=[c.sd10b])
                    T.op("dve", lambda e, c=c, P=P: e.reciprocal(out=c.r10[:P], in_=c.sd10[:P]), reads=[c.sd10b], writes=[c.r10b])
                    T.op("dve", lambda e, c=c, P=P, gn3=gn3: e.tensor_tensor(
                        out=gn3, in0=gn3, in1=c.r10[:P, 0:10].unsqueeze(2).broadcast_to([P, 10, 64]), op=ALU.mult),
                        reads=[c.gnb, c.r10b], writes=[c.gnb])
                    T.op("dve", lambda e, P=P, gn3=gn3: e.tensor_tensor(
                        out=gn3[:, 0:8, :], in0=gn3[:, 0:8, :], in1=g_gq[:P, :].unsqueeze(1).broadcast_to([P, 8, 64]), op=ALU.mult),
                        reads=[c.gnb] + gbufs, writes=[c.gnb])
                    T.op("dve", lambda e, P=P, gn3=gn3: e.tensor_tensor(
                        out=gn3[:, 8:10, :], in0=gn3[:, 8:10, :], in1=g_gk[:P, :].unsqueeze(1).broadcast_to([P, 2, 64]), op=ALU.mult),
                        reads=[c.gnb] + gbufs, writes=[c.gnb])
                    gq16_3 = c.gq16.rearrange("p (h d) -> p h d", d=64)
                    gk16_3 = c.gk16.rearrange("p (h d) -> p h d", d=64)
                    rope(gn3[:, 0:8, :], c.gnb, P, 8, 64, 2, c.csg, c.csgb, gq16_3[:P], c.gq16b, c)
                    rope(gn3[:, 8:10, :], c.gnb, P, 2, 64, 2, c.csg, c.csgb, gk16_3[:P], c.gk16b, c)
                    gqT3 = c.gqT.rearrange("p (j t) -> p j t", t=128)
                    transposes(c.gq16, c.gq16b, P, 4, 128, gqT3, c.gqTb)
                    T.dma("pool", QTg[s].rearrange("(j p) t -> p j t", p=128)[:, :, tok0:tok0 + P], gqT3[:, :, :P],
                          reads=[c.gqTb], writes=[Buf()])
                    gkT3 = c.gkT.rearrange("p (j t) -> p j t", t=128)
                    transposes(c.gk16, c.gk16b, P, 1, 128, gkT3, c.gkTb)
                    T.dma("pool", ktd[544:672], gkT3[:, 0, :P], reads=[c.gkTb], writes=[Buf()])
            T.barrier()
            P16.release()
            P32.release()

        def phase2():
            for s in range(2):
                R = RS[s]
                groups = [list(range(g * R, (g + 1) * R)) for g in range(8 // R)]
                pieces = [(KTloc[s].ap()[a:a + n], KTall[s][i].ap()) for i, (a, n) in enumerate(KPIECES)]
                pieces += [(Vloc[s].ap()[h * N_OWN:(h + 1) * N_OWN], Vall[s][h].ap()) for h in range(10)]
                for src, dst in pieces:
                    T.custom("pool", lambda e, src=src, dst=dst, groups=groups: e.collective_compute(
                        "AllGather", ALU.bypass, replica_groups=groups, ins=[src], outs=[dst]), 1)
            T.barrier()

        def phase3(l):
            P16.mark()
            P32.mark()
            nq_eff = NQ if l == 0 else N_OWN
            ngrp = (nq_eff + 511) // 512
            units = nq_eff // 16
            base = units // ngrp
            widths = [16 * (base + (1 if i < units - base * ngrp else 0)) for i in range(ngrp)]
            assert sum(widths) == nq_eff and max(widths) <= 512
            g0s = [sum(widths[:i]) for i in range(ngrp)]
            LKMAX = 4 * N_OWN + NMETA
            NCHMAX = 4 * NC
            Kt = [(P16.alloc(LKMAX, f"K{i}")[0], [Buf() for _ in range(4)]) for i in range(2)]
            Vt = [P16.alloc(NCHMAX * 65, f"V{i}") for i in range(2)]
            Vm = [P16.alloc(65, f"Vm{i}") for i in range(2)]
            Qt = [(P16.alloc(NQ, f"Q{i}")[0], [Buf(), Buf()]) for i in range(2)]
            for i in range(2):
                T.op("dve", lambda e, i=i: e.memset(Kt[i][0][64:128, :], 0.0), writes=[Kt[i][1][1], Kt[i][1][3]])
            NPB = 3
            Pt = [P16.alloc(1024, f"P{i}") for i in range(NPB)]
            Osb = [P32.alloc(512, f"Osb{i}") for i in range(2)]
            aost = [P32.alloc(256, f"ao{i}") for i in range(2)]
            rd = [P32.alloc(4, f"rd{i}") for i in range(2)]
            Sb = [(ps32[:, b * 1024:(b + 1) * 1024], Buf(f"S{b}")) for b in range(2)]
            Ob = [(bank[4], bankb[4]), (bank[5], bankb[5])]
            OT, OTb = bank[6], bankb[6]

            kvsets = []
            for s in range(2):
                for h in range(8):
                    kvsets.append((s, "mla", h))
                for j in range(2):
                    kvsets.append((s, "gqa", j))

            def load_kv(idx):
                s, kind, h = kvsets[idx]
                R = RS[s]
                K, Kb = Kt[idx % 2]
                V, Vb = Vt[idx % 2]
                VM, VMb = Vm[idx % 2]
                def kp(i):
                    return KTall[s][i].ap().rearrange("(r f) t -> f r t", f=KPIECES[i][1])
                Kv = K[:, 0:R * N_OWN].rearrange("d (r t) -> d r t", t=N_OWN)
                mcol = slice(R * N_OWN, R * N_OWN + NMETA)
                if kind == "mla":
                    T.dma("sp", Kv[0:64], kp(h), writes=[Kb[0]])
                    T.dma("sp", Kv[64:96], kp(8), writes=[Kb[1]])
                    T.dma("sp", K[0:64, mcol], KTmeta[s][h * 64:(h + 1) * 64, :], writes=[Kb[2]])
                    T.dma("sp", K[64:96, mcol], KTmeta[s][512:544, :], writes=[Kb[3]])
                    hv = h
                else:
                    T.dma("sp", Kv[0:64], kp(9 + h), writes=[Kb[0]])
                    T.dma("sp", K[0:64, mcol], KTmeta[s][544 + h * 64:544 + (h + 1) * 64, :], writes=[Kb[2]])
                    hv = 8 + h
                vall = Vall[s][hv].ap().rearrange("(r p c) d -> p r c d", p=128, c=NC)
                V4 = V[:, 0:R * NC * 65].rearrange("p (r c d) -> p r c d", c=NC, d=65)
                T.dma("sp", V4, vall, writes=[Vb])
                T.dma("sp", VM[0:NMETA, 0:65], Vmeta[s][hv * NMETA:(hv + 1) * NMETA, :], writes=[VMb])

            def qheads(idx):
                s, kind, h = kvsets[idx]
                if kind == "mla":
                    return [(s, "mla", h, h)]
                return [(s, "gqa", h * 4 + g, 8 + h * 4 + g) for g in range(4)]

            qlist = []
            for idx in range(len(kvsets)):
                for qh in qheads(idx):
                    qlist.append((idx, qh))

            def load_q(qi):
                idx, (s, kind, qh, _) = qlist[qi]
                Q, Qb = Qt[qi % 2]
                if kind == "mla":
                    T.dma("sp", Q[0:96, 0:nq_eff], QTm[s][qh * 96:(qh + 1) * 96, 0:nq_eff], writes=[Qb[0], Qb[1]])
                else:
                    T.dma("sp", Q[0:64, 0:nq_eff], QTg[s][qh * 64:(qh + 1) * 64, 0:nq_eff], writes=[Qb[0]])
                    T.op("dve", lambda e, Q=Q: e.memset(Q[64:128, 0:nq_eff], 0.0), writes=[Qb[1]])

            steps = []
            for qi, (idx, (s, kind, qh, cb)) in enumerate(qlist):
                R = RS[s]
                nch = R * NC + 1
                for gi in range(ngrp):
                    for c0 in range(0, R * NC, 2):
                        steps.append((qi, gi, [c0, c0 + 1], nch))
                    steps.append((qi, gi, [R * NC], nch))
            LA = 2
            gcount = [0]

            def kcols(idx, cix):
                s, kind, h = kvsets[idx]
                R = RS[s]
                d = 96 if kind == "mla" else 128
                K, Kb = Kt[idx % 2]
                if cix == R * NC:
                    return K[0:d, R * N_OWN:R * N_OWN + NMETA], NMETA, d
                r, cl = divmod(cix, NC)
                return K[0:d, r * N_OWN:(r + 1) * N_OWN].rearrange("d (p c) -> d p c", c=NC)[:, :, cl], 128, d

            def emit_qk(i):
                qi, gi, chunks, nch = steps[i]
                idx, (s, kind, qh, cb) = qlist[qi]
                Q, Qb = Qt[qi % 2]
                G = widths[gi]
                g0 = g0s[gi]
                S, Sbuf = Sb[i % 2]

                def f(e):
                    ins = None
                    for k, cix in enumerate(chunks):
                        lhsT, kc, d = kcols(idx, cix)
                        ins = e.matmul(S[:kc, k * 512:k * 512 + G], lhsT=lhsT, rhs=Q[0:d, g0:g0 + G], start=True, stop=True)
                    return ins
                T.op("pe", f, reads=Kt[idx % 2][1] + Qb, writes=[Sbuf])

            def emit_exp_pv(i):
                qi, gi, chunks, nch = steps[i]
                idx, (s, kind, qh, cb) = qlist[qi]
                R = RS[s]
                scale = (96.0 if kind == "mla" else 64.0) ** -0.5
                n = len(chunks)
                kc = NMETA if chunks[0] == R * NC else 128
                G = widths[gi]
                S, Sbuf = Sb[i % 2]
                Pp, Pb = Pt[i % NPB]
                gidx = gcount[0]
                O, Obuf = Ob[gidx % 2]
                S3 = S.rearrange("p (k g) -> p k g", g=512)[:kc, 0:n, 0:G]
                P3 = Pp.rearrange("p (k g) -> p k g", g=512)[:kc, 0:n, 0:G]
                T.op("act", lambda e, S3=S3, P3=P3, scale=scale: e.activation(out=P3, in_=S3, func=AF.Exp, scale=scale),
                     reads=[Sbuf], writes=[Pb])
                j = i + LA
                if j < len(steps):
                    ensure_loaded(j)
                    emit_qk(j)
                if kc == 128:
                    V, Vb = Vt[idx % 2]
                    V3 = V.rearrange("p (c d) -> p c d", d=65)
                    lhs = [V3[:, cix, :] for cix in chunks]
                else:
                    V, Vb = Vm[idx % 2]
                    lhs = [V[0:NMETA, 0:65]]

                def fpv(e):
                    ins = None
                    for k, cix in enumerate(chunks):
                        ins = e.matmul(O[0:65, :G], lhsT=lhs[k], rhs=Pp[:kc, k * 512:k * 512 + G],
                                       start=(cix == 0), stop=(cix == nch - 1))
                    return ins
                T.op("pe", fpv, reads=[Vb, Pb], writes=[Obuf])
                cix = chunks[-1]
                if cix == nch - 1:
                    gcount[0] += 1
                    Os, Osbuf = Osb[gidx % 2]
                    T.op("dve", lambda e, Os=Os, O=O, G=G: e.tensor_copy(out=Os[0:65, :G], in_=O[0:65, :G]),
                         reads=[Obuf], writes=[Osbuf])
                    nsub = (G + 127) // 128
                    OT3 = OT[:, 0:4 * 65].rearrange("p (j d) -> p j d", d=65)

                    def ftr(e, Os=Os, G=G, nsub=nsub):
                        ins = None
                        for j in range(nsub):
                            w = min(128, G - j * 128)
                            ins = e.transpose(out=OT3[:w, j, :], in_=Os[0:65, j * 128:j * 128 + w], identity=id32[0:65, 0:65])
                        return ins
                    T.op("pe", ftr, reads=[Osbuf, id32b], writes=[OTb])
                    rdt, rdb = rd[gidx % 2]
                    ao, aob = aost[gidx % 2]
                    ao3 = ao.rearrange("p (j d) -> p j d", d=64)
                    g0 = g0s[gi]
                    full = G // 128
                    rem = G - full * 128
                    parts = []
                    if full:
                        parts.append((128, 0, full))
                    if rem:
                        parts.append((rem, full, full + 1))
                    for (pw, j0, j1) in parts:
                        T.op("dve", lambda e, pw=pw, j0=j0, j1=j1, rdt=rdt: e.reciprocal(out=rdt[:pw, j0:j1], in_=OT3[:pw, j0:j1, 64]),
                             reads=[OTb], writes=[rdb])
                        for jj in range(j0, j1):
                            T.op("dve", lambda e, pw=pw, jj=jj, rdt=rdt, ao3=ao3: e.tensor_scalar(
                                out=ao3[:pw, jj, :], in0=OT3[:pw, jj, 0:64], scalar1=rdt[:pw, jj:jj + 1], scalar2=None, op0=ALU.mult),
                                reads=[OTb, rdb], writes=[aob])
                        if j1 - j0 > 1 or True:
                            dst = AO[s][g0 + j0 * 128:g0 + j0 * 128 + (j1 - j0 - 1) * 128 + pw, cb * 64:(cb + 1) * 64]
                            if j1 - j0 == 1:
                                T.dma("pool", dst, ao3[:pw, j0, :], reads=[aob], writes=[Buf()])
                            else:
                                T.dma("pool", dst.rearrange("(j p) d -> p j d", p=128), ao3[:pw, j0:j1, :], reads=[aob], writes=[Buf()])

            load_kv(0)
            load_q(0)
            started_kv = {0}
            started_q = {0}
            n = len(steps)
            def ensure_loaded(j):
                qi = steps[j][0]
                if qi not in started_q:
                    load_q(qi)
                    started_q.add(qi)
                idx = qlist[qi][0]
                if idx not in started_kv:
                    load_kv(idx)
                    started_kv.add(idx)

            for j in range(min(LA, n)):
                ensure_loaded(j)
                emit_qk(j)
            for i in range(n):
                if i >= 0:
                    emit_exp_pv(i)
                    qi = steps[i][0]
                    if steps[i][1] == 0 and steps[i][2][0] == 0:
                        if qi + 1 < len(qlist) and (qi + 1) not in started_q:
                            load_q(qi + 1)
                            started_q.add(qi + 1)
                            nidx = qlist[qi + 1][0]
                            if nidx not in started_kv:
                                load_kv(nidx)
                                started_kv.add(nidx)
            T.barrier()
            P16.release()
            P32.release()

        def phase4(l):
            last = (l == depth - 1)
            P16.mark()
            P32.mark()
            Wo_f, _ = P16.alloc(8 * 1024, "Wo")
            Wo = Wo_f.rearrange("p (k n) -> p k n", n=1024)
            Wu_f, _ = P16.alloc(8 * 4096, "Wu")
            Wu = Wu_f.rearrange("p (k n) -> p k n", n=4096)
            Wd_f, _ = P16.alloc(32 * 1024, "Wd")
            Wd = Wd_f.rearrange("p (k n) -> p k n", n=1024)
            g_out, gb1 = P32.alloc(1024)
            g_mlp, gb2 = P32.alloc(1024)
            bcast_load(g_out, gb1, g_out_d[l], 1024)
            bcast_load(g_mlp, gb2, g_mlp_d[l], 1024)
            gbufs = [gb1, gb2]
            if last:
                g_fin, gb3 = P32.alloc(1024)
                bcast_load(g_fin, gb3, g_fin_d, 1024)
                gbufs.append(gb3)
            P32.mark()
            stages = [P32.alloc(stage_n) for _ in range(3)]
            load_w(Wo, w_out_d[l], stages)
            load_w(Wu, w_up_d[l], stages)
            load_w(Wd, w_down_d[l], stages)
            T.barrier()
            P32.release()

            ctxs = []
            for i in range(2):
                c = Ctx()
                c.x, c.xb = P32.alloc(1024)
                c.ao, c.aob = P32.alloc(1024)
                c.ss, c.ssb = P32.alloc(1)
                c.sd, c.sdb = P32.alloc(1)
                c.r, c.rb = P32.alloc(1)
                c.ss2, c.ss2b = P32.alloc(2)
                c.sd2, c.sd2b = P32.alloc(2)
                c.r2, c.r2b = P32.alloc(2)
                if i == 0:
                    c.junk, c.junkb = P16.alloc(1024)
                    c.mix, c.mixb = P16.alloc(1024)
                    c.mixT, c.mixTb = P16.alloc(1024)
                    c.hn, c.hnb = P16.alloc(1024)
                    c.hnT, c.hnTb = P16.alloc(1024)
                else:
                    for nm in ("junk", "mix", "mixT", "hn", "hnT"):
                        setattr(c, nm, getattr(ctxs[0], nm))
                        setattr(c, nm + "b", getattr(ctxs[0], nm + "b"))
                ctxs.append(c)
            xmid, xmidb = P32.alloc(1024)
            xnew, xnewb = P32.alloc(1024)
            r32 = [P32.alloc(512) for _ in range(2)]
            aT = [P16.alloc(512) for _ in range(2)]
            pso = [(bank[0], bankb[0]), (bank[1], bankb[1])]
            pu = [(bank[2], bankb[2]), (bank[3], bankb[3])]
            py = [(bank[4], bankb[4]), (bank[5], bankb[5])]

            tile_i = 0
            for s in range(2):
                for t in range(NT + (0 if last else 1)):
                    c = ctxs[tile_i % 2]
                    tile_i += 1
                    is_meta = (t == NT)
                    P = NMETA if is_meta else 128
                    tok0 = t * 128
                    if l == 0:
                        src = meta[:, :] if is_meta else xq[s, tok0:tok0 + P, :]
                    else:
                        src = X1[s][tok0:tok0 + P, :]
                    T.dma("sp", c.x[:P], src, writes=[c.xb])
                    T.dma("sp", c.ao[:P], AO[s][tok0:tok0 + P, :], writes=[c.aob])
                    for hf in range(2):
                        T.op("act", lambda e, c=c, P=P, hf=hf: e.activation(out=c.junk[:P, 0:512], in_=c.ao[:P, hf * 512:(hf + 1) * 512],
                                                                          func=AF.Square, accum_out=c.ss2[:P, hf:hf + 1]),
                             reads=[c.aob], writes=[c.junkb, c.ss2b])
                    T.op("act", lambda e, c=c, P=P: e.activation(out=c.sd2[:P], in_=c.ss2[:P], func=AF.Sqrt, scale=1.0 / 512, bias=EPS),
                         reads=[c.ss2b], writes=[c.sd2b])
                    T.op("dve", lambda e, c=c, P=P: e.reciprocal(out=c.r2[:P], in_=c.sd2[:P]), reads=[c.sd2b], writes=[c.r2b])
                    for hf in range(2):
                        sl = slice(hf * 512, (hf + 1) * 512)
                        T.op("dve", lambda e, c=c, P=P, hf=hf, sl=sl: e.scalar_tensor_tensor(
                            out=c.mix[:P, sl], in0=c.ao[:P, sl], scalar=c.r2[:P, hf:hf + 1], in1=g_out[:P, sl], op0=ALU.mult, op1=ALU.mult),
                            reads=[c.aob, c.r2b] + gbufs, writes=[c.mixb])
                    mixT3 = c.mixT.rearrange("p (k t) -> p k t", t=128)
                    transposes(c.mix, c.mixb, P, 8, 128, mixT3, c.mixTb)
                    for j in range(2):
                        mm_tokmajor(pso[j][0], pso[j][1], mixT3, c.mixTb, P, Wo, 8, j * 512, (j + 1) * 512)
                    for j in range(2):
                        sl = slice(j * 512, (j + 1) * 512)
                        T.op("dve", lambda e, c=c, P=P, j=j, sl=sl: e.tensor_tensor(out=xmid[:P, sl], in0=pso[j][0][:P, :], in1=c.x[:P, sl], op=ALU.add),
                             reads=[pso[j][1], c.xb], writes=[xmidb])
                    rstd(xmid[:P], xmidb, P, 1024, c)
                    T.op("dve", lambda e, c=c, P=P: e.scalar_tensor_tensor(out=c.hn[:P], in0=xmid[:P], scalar=c.r[:P], in1=g_mlp[:P],
                                                                         op0=ALU.mult, op1=ALU.mult),
                         reads=[xmidb, c.rb] + gbufs, writes=[c.hnb])
                    hnT3 = c.hnT.rearrange("p (k t) -> p k t", t=128)
                    transposes(c.hn, c.hnb, P, 8, 128, hnT3, c.hnTb)

                    def up(fb, c=c, P=P, hnT3=hnT3):
                        U, Ub = pu[fb % 2]

                        def f(e):
                            ins = None
                            for q in range(4):
                                fidx = fb * 4 + q
                                for k in range(8):
                                    ins = e.matmul(U[:, q * 128:q * 128 + P], lhsT=Wu[:, k, fidx * 128:(fidx + 1) * 128], rhs=hnT3[:, k, :P],
                                                   start=(k == 0), stop=(k == 7))
                            return ins
                        T.op("pe", f, reads=[c.hnTb], writes=[Ub])
                        U3 = U.rearrange("p (q t) -> p q t", t=128)[:, :, :P]
                        R3 = r32[fb % 2][0].rearrange("p (q t) -> p q t", t=128)[:, :, :P]
                        A3 = aT[fb % 2][0].rearrange("p (q t) -> p q t", t=128)[:, :, :P]
                        T.op("act", lambda e: e.activation(out=R3, in_=U3, func=AF.Relu), reads=[Ub], writes=[r32[fb % 2][1]])
                        T.op("dve", lambda e: e.tensor_tensor(out=A3, in0=R3, in1=R3, op=ALU.mult), reads=[r32[fb % 2][1]], writes=[aT[fb % 2][1]])

                    def down(fb, P=P):
                        A3 = aT[fb % 2][0].rearrange("p (q t) -> p q t", t=128)

                        def f(e):
                            ins = None
                            for q in range(4):
                                fidx = fb * 4 + q
                                for j in range(2):
                                    ins = e.matmul(py[j][0][:P, :], lhsT=A3[:, q, :P], rhs=Wd[:, fidx, j * 512:(j + 1) * 512],
                                                   start=(fidx == 0), stop=(fidx == 31))
                            return ins
                        T.op("pe", f, reads=[aT[fb % 2][1]], writes=[py[0][1], py[1][1]])

                    for fb in range(8):
                        up(fb)
                        if fb >= 1:
                            down(fb - 1)
                    down(7)
                    for j in range(2):
                        sl = slice(j * 512, (j + 1) * 512)
                        T.op("dve", lambda e, P=P, j=j, sl=sl: e.tensor_tensor(out=xnew[:P, sl], in0=py[j][0][:P, :], in1=xmid[:P, sl], op=ALU.add),
                             reads=[py[j][1], xmidb], writes=[xnewb])
                    if not last:
                        T.dma("pool", X1[s][tok0:tok0 + P, :], xnew[:P], reads=[xnewb], writes=[Buf()])
                    else:
                        rstd(xnew[:P], xnewb, P, 1024, c)
                        T.op("dve", lambda e, c=c, P=P: e.scalar_tensor_tensor(out=c.ao[:P], in0=xnew[:P], scalar=c.r[:P], in1=g_fin[:P],
                                                                             op0=ALU.mult, op1=ALU.mult),
                             reads=[xnewb, c.rb] + gbufs, writes=[c.aob])
                        T.dma("pool", y_d[s, tok0:tok0 + P, :], c.ao[:P], reads=[c.aob], writes=[Buf()])
            T.barrier()
            P16.release()
            P32.release()

        plist = []
        for l in range(depth):
            plist += [lambda l=l: phase1(l), phase2, lambda l=l: phase3(l), lambda l=l: phase4(l)]
        for ph in plist[:nphase]:
            ph()
        if debug:
            dbg = {}
            for s in range(2):
                for nm, t in (("KTloc", KTloc[s]), ("Vloc", Vloc[s])):
                    o = nc.dram_tensor(f"dbg_{nm}{s}", list(t.ap().shape), BF16, kind="ExternalOutput")
                    T.dma("pool", o.ap(), t.ap())
            T.barrier()

        @block.sync
        def _(e):
            T.replay("sp", e)

        @block.tensor
        def _(e):
            T.replay("pe", e)

        @block.scalar
        def _(e):
            T.replay("act", e)

        @block.vector
        def _(e):
            T.replay("dve", e)

        @block.gpsimd
        def _(e):
            T.replay("pool", e)
    return nc


def _inv_freq(dim):
    return (np.float32(1.0) / np.power(np.float32(10000.0), np.arange(0, dim, 2, dtype=np.float32) / np.float32(dim))).astype(np.float32)


def _tables(pos, rows, cols):
    def cs(p, f):
        ang = (p[:, None].astype(np.float32) * f[None, :].astype(np.float32)).astype(np.float32)
        a = np.concatenate([ang, ang], axis=-1).astype(np.float64)
        c, s = np.cos(a), np.sin(a)
        h = ang.shape[1]
        sp = np.concatenate([-s[:, :h], s[:, h:]], axis=-1)
        return c.astype(np.float32), sp.astype(np.float32)
    cm, sm = cs(pos, _inv_freq(32))
    cr, sr = cs(rows, _inv_freq(32))
    cc, sc = cs(cols, _inv_freq(32))
    csm = np.concatenate([cm, sm], axis=-1)
    csg = np.concatenate([cr, cc, sr, sc], axis=-1)
    return np.ascontiguousarray(csm, np.float32), np.ascontiguousarray(csg, np.float32)


_PERM = np.concatenate([np.arange(0, 384), np.arange(1312, 1440), np.arange(672, 1184),
                        np.arange(384, 640), np.arange(640, 672), np.arange(1184, 1312)])

_NC_CACHE = {}


def run_model(x_prompt, x_sample, meta_tokens, attn_norm_g, w_in, q_a_norm_g, w_q_b, kv_a_norm_g, w_kv_b,
              gqa_q_norm_g, gqa_k_norm_g, mla_out_norm_g, gqa_out_norm_g, w_out, mlp_norm_g, w_up, w_down,
              final_norm_g, trace=False, nphase=None, debug=False):
    f = lambda a: np.ascontiguousarray(np.asarray(a), dtype=np.float32)
    x_prompt, x_sample = f(x_prompt), f(x_sample)
    B, n_long, _ = x_prompt.shape
    Bs, n_short, _ = x_sample.shape
    assert B == 2 and Bs == 4 and n_long == 2 * n_short
    N_OWN = n_long // 4
    depth = np.asarray(w_in).shape[0]
    shared = {
        "meta": f(meta_tokens), "ident": np.eye(128, dtype=np.float32),
        "w_in": np.ascontiguousarray(f(w_in)[:, :, _PERM]), "w_q_b": f(w_q_b), "w_kv_b": f(w_kv_b), "w_out": f(w_out),
        "w_up": f(w_up), "w_down": f(w_down), "attn_norm_g": f(attn_norm_g), "q_a_norm_g": f(q_a_norm_g),
        "kv_a_norm_g": f(kv_a_norm_g), "gqa_q_norm_g": f(gqa_q_norm_g), "gqa_k_norm_g": f(gqa_k_norm_g),
        "out_norm_g": np.ascontiguousarray(np.concatenate([f(mla_out_norm_g), f(gqa_out_norm_g)], axis=-1)),
        "mlp_norm_g": f(mlp_norm_g), "final_norm_g": f(final_norm_g),
    }
    in_maps = []
    for c in range(8):
        bl, rl = c // 4, c % 4
        bs, rs = c // 2, c % 2
        xq = np.stack([x_prompt[bl, rl * N_OWN:(rl + 1) * N_OWN], x_sample[bs, rs * N_OWN:(rs + 1) * N_OWN]])
        csm, csg = [], []
        for r in (rl, rs):
            t = np.arange(r * N_OWN, (r + 1) * N_OWN, dtype=np.float32)
            pos = np.concatenate([t + np.float32(NMETA), np.arange(NMETA, dtype=np.float32)])
            rows = np.concatenate([np.floor(t / 64.0), np.zeros(NMETA)]).astype(np.float32)
            cols = np.concatenate([np.mod(t, 64.0), np.zeros(NMETA)]).astype(np.float32)
            a, b = _tables(pos, rows, cols)
            csm.append(a)
            csg.append(b)
        m = dict(shared)
        m["xq"] = np.ascontiguousarray(xq)
        m["csm"] = np.stack(csm)
        m["csg"] = np.stack(csg)
        import os as _os3
        if _os3.environ.get("K_PAD"):
            m["pad"] = np.zeros((int(_os3.environ["K_PAD"]), 1024), np.float32)
        in_maps.append(m)
    key = (N_OWN, depth, nphase, debug)
    if key not in _NC_CACHE:
        _NC_CACHE[key] = build(N_OWN, depth, nphase=nphase, debug=debug)
    nc = _NC_CACHE[key]
    res = run_bass_kernel_spmd(nc, in_maps, core_ids=list(range(8)), trace=trace)
    y_prompt = np.empty_like(x_prompt)
    y_sample = np.empty_like(x_sample)
    for c in range(8):
        y = np.asarray(res.results[c]["y"], dtype=np.float32)
        y_prompt[c // 4, (c % 4) * N_OWN:(c % 4 + 1) * N_OWN] = y[0]
        y_sample[c // 2, (c % 2) * N_OWN:(c % 2 + 1) * N_OWN] = y[1]
    return (y_prompt, y_sample), res


def kernel(**inputs):
    out, _ = run_model(**inputs)
    return out
```

```python
import contextlib
import numpy as np
import ml_dtypes
import concourse.bass as bass
import concourse.mybir as mybir
from concourse.bass_utils import run_bass_kernel_spmd

F32 = mybir.dt.float32
BF16 = mybir.dt.bfloat16
AF = mybir.ActivationFunctionType
ALU = mybir.AluOpType
AX = mybir.AxisListType

ENGS = ("pe", "act", "dve", "pool", "sp")
D = 1024
EPS = 1e-6
NMETA = 16


class Tok:
    __slots__ = ("eng", "sem", "val", "dma")

    def __init__(self, eng, sem, val, dma):
        self.eng, self.sem, self.val, self.dma = eng, sem, val, dma


class Buf:
    __slots__ = ("name", "w", "r")

    def __init__(self, name=""):
        self.name = name
        self.w = None
        self.r = {}


class Tracker:
    def __init__(self, sems, rings):
        self.sem = sems
        self.rings = rings
        self.streams = {e: [] for e in ENGS}
        self.cnt = {e: 0 for e in ENGS}
        self.waited = {e: {} for e in ENGS}
        self.ring_idx = {q: 0 for q in rings}
        self.ring_val = {}
        self.ring_tok = {}

    def _need(self, eng, tok, waits):
        if tok is None:
            return
        if (not tok.dma) and tok.eng == eng and eng == "pe":
            return
        w = self.waited[eng]
        key = id(tok.sem)
        if w.get(key, 0) >= tok.val:
            return
        w[key] = tok.val
        waits.append((tok.sem, tok.val))

    def _deps(self, eng, reads, writes):
        waits = []
        for b in reads:
            self._need(eng, b.w, waits)
        for b in writes:
            self._need(eng, b.w, waits)
            for t in b.r.values():
                self._need(eng, t, waits)
        return waits

    def _commit(self, tok, reads, writes):
        k = id(tok.sem)
        for b in reads:
            o = b.r.get(k)
            if o is None or o.val < tok.val:
                b.r[k] = tok
        for b in writes:
            b.w = tok
            b.r = {}

    def _skip(self):
        self.nrec = getattr(self, "nrec", 0) + 1
        return self.nrec > getattr(self, "maxops", 1 << 60)

    def op(self, eng, fn, reads=(), writes=()):
        if self._skip():
            return None
        waits = self._deps(eng, reads, writes)
        self.cnt[eng] += 1
        tok = Tok(eng, self.sem[eng], self.cnt[eng], False)
        self.streams[eng].append((waits, fn, self.sem[eng], 1))
        self._commit(tok, reads, writes)
        return tok

    def _ring(self, q, fn, inc, reads, writes):
        if self._skip():
            return None
        ring = self.rings[q]
        sem = ring[self.ring_idx[q] % len(ring)]
        self.ring_idx[q] += 1
        waits = self._deps(q, reads, writes)
        prev = self.ring_tok.get(id(sem))
        if prev is not None:
            self._need(q, prev, waits)
        val = self.ring_val.get(id(sem), 0) + inc
        self.ring_val[id(sem)] = val
        tok = Tok(q, sem, val, True)
        self.ring_tok[id(sem)] = tok
        self.streams[q].append((waits, fn, sem, inc))
        self._commit(tok, reads, writes)
        return tok

    def dma(self, q, out, in_, reads=(), writes=()):
        return self._ring(q, lambda e, out=out, in_=in_: e.dma_start(out=out, in_=in_), 16, reads, writes)

    def custom(self, q, fn, inc, reads=(), writes=(), ring="cc"):
        if self._skip():
            return None
        rg = self.rings[ring]
        sem = rg[self.ring_idx[ring] % len(rg)]
        self.ring_idx[ring] += 1
        waits = self._deps(q, reads, writes)
        prev = self.ring_tok.get(id(sem))
        if prev is not None:
            self._need(q, prev, waits)
        val = self.ring_val.get(id(sem), 0) + inc
        self.ring_val[id(sem)] = val
        tok = Tok(q, sem, val, True)
        self.ring_tok[id(sem)] = tok
        self.streams[q].append((waits, fn, sem, inc))
        self._commit(tok, reads, writes)
        return tok

    def barrier(self):
        toks = []
        for e in ENGS:
            if self.cnt[e] > 0:
                toks.append(Tok(e, self.sem[e], self.cnt[e], False))
        toks.extend(self.ring_tok.values())
        for e in ENGS:
            waits = []
            for t in toks:
                if (not t.dma) and t.eng == e:
                    continue
                self._need(e, t, waits)
            if waits:
                self.streams[e].append((waits, None, None, 0))

    def replay(self, eng, e):
        for waits, fn, sem, inc in self.streams[eng]:
            for s, v in waits:
                e.wait_ge(s, v)
            if fn is not None:
                fn(e).then_inc(sem, inc)


class Pool:
    def __init__(self, tensor, size):
        self.t, self.size, self.off, self.marks = tensor, size, 0, []

    def alloc(self, n, name=""):
        a = self.off
        self.off += n
        assert self.off <= self.size, (name, self.off, self.size)
        return self.t[:, a:a + n], Buf(name)

    def mark(self):
        self.marks.append(self.off)

    def release(self):
        self.off = self.marks.pop()


class Ctx:
    pass


def build(N_OWN, depth=2, N16=80100, N32=10500, nphase=None, debug=False):
    NT = N_OWN // 128
    NC = NT
    NQ = N_OWN + NMETA
    RS = (4, 2)
    nc = bass.Bass("TRN2", target_bir_lowering=False)

    def din(name, shape, dt=F32):
        return nc.dram_tensor(name, list(shape), dt, kind="ExternalInput").ap()

    xq = din("xq", [2, N_OWN, D])
    meta = din("meta", [NMETA, D])
    ident = din("ident", [128, 128])
    csm_d = din("csm", [2, NQ, 64])
    csg_d = din("csg", [2, NQ, 128])
    w_in_d = din("w_in", [depth, D, 1440])
    w_qb_d = din("w_q_b", [depth, 384, 768])
    w_kvb_d = din("w_kv_b", [depth, 256, 1024])
    w_out_d = din("w_out", [depth, D, D])
    w_up_d = din("w_up", [depth, D, 4096])
    w_down_d = din("w_down", [depth, 4096, D])
    g_attn_d = din("attn_norm_g", [depth, D])
    g_qa_d = din("q_a_norm_g", [depth, 384])
    g_kva_d = din("kv_a_norm_g", [depth, 256])
    g_gq_d = din("gqa_q_norm_g", [depth, 64])
    g_gk_d = din("gqa_k_norm_g", [depth, 64])
    g_out_d = din("out_norm_g", [depth, D])
    g_mlp_d = din("mlp_norm_g", [depth, D])
    g_fin_d = din("final_norm_g", [D])
    y_d = nc.dram_tensor("y", [2, N_OWN, D], F32, kind="ExternalOutput").ap()
    import os as _os2
    if _os2.environ.get("K_PAD"):
        din("pad", [int(_os2.environ["K_PAD"]), 1024])

    dk = dict(kind="ExternalOutput") if debug else {}
    QTm = [nc.dram_tensor(f"QTm{s}", [8 * 96, NQ], BF16, **dk).ap() for s in range(2)]
    QTg = [nc.dram_tensor(f"QTg{s}", [8 * 64, NQ], BF16, **dk).ap() for s in range(2)]
    KTloc = [nc.dram_tensor(f"KTloc{s}", [672, N_OWN], BF16) for s in range(2)]
    Vloc = [nc.dram_tensor(f"Vloc{s}", [10 * N_OWN, 65], BF16) for s in range(2)]
    KPIECES = [(h * 64, 64) for h in range(8)] + [(512, 32), (544, 64), (608, 64)]
    KTall = [[nc.dram_tensor(f"KTall{s}_{i}", [RS[s] * n, N_OWN], BF16) for i, (a, n) in enumerate(KPIECES)] for s in range(2)]
    Vall = [[nc.dram_tensor(f"Vall{s}_{h}", [RS[s] * N_OWN, 65], BF16) for h in range(10)] for s in range(2)]
    KTmeta = [nc.dram_tensor(f"KTmeta{s}", [672, NMETA], BF16, **dk).ap() for s in range(2)]
    Vmeta = [nc.dram_tensor(f"Vmeta{s}", [10 * NMETA, 65], BF16, **dk).ap() for s in range(2)]
    AO = [nc.dram_tensor(f"AO{s}", [NQ, D], F32, **dk).ap() for s in range(2)]
    X1 = [nc.dram_tensor(f"X1{s}", [NQ, D], F32, **dk).ap() for s in range(2)]

    es = contextlib.ExitStack()
    with es:
        sb32 = es.enter_context(nc.sbuf_tensor("sb32", [128, N32], F32))
        sb16 = es.enter_context(nc.sbuf_tensor("sb16", [128, N16], BF16))
        ps32 = es.enter_context(nc.psum_tensor("ps32", [128, 7 * 512], F32))
        ps16 = es.enter_context(nc.psum_tensor("ps16", [128, 1024], BF16))
        sems = {e: es.enter_context(nc.semaphore("s_" + e)) for e in ENGS}
        rings = {q: [es.enter_context(nc.semaphore(f"r_{q}{i}")) for i in range(8 if q != "cc" else 4)] for q in ("sp", "pool", "cc")}
        block = es.enter_context(nc.Block())
        T = Tracker(sems, rings)
        import os as _os
        if _os.environ.get("K_MAXOPS"):
            T.maxops = int(_os.environ["K_MAXOPS"])
        P32 = Pool(sb32, N32)
        P16 = Pool(sb16, N16)
        bank = [ps32[:, i * 512:(i + 1) * 512] for i in range(7)]
        bankb = [Buf(f"bank{i}") for i in range(7)]
        pT = ps16
        pTb = Buf("pT16")

        id32, id32b = P32.alloc(128, "id32")
        idb, idbb = P16.alloc(128, "idb")
        T.dma("sp", id32, ident, writes=[id32b])
        T.op("dve", lambda e: e.tensor_copy(out=idb, in_=id32), reads=[id32b], writes=[idbb])

        def bcast_load(dst, dbuf, src1d, n):
            T.dma("sp", dst[:, :n], src1d.partition_broadcast(128), writes=[dbuf])

        stage_n = 2048

        def load_w(dst3, src2, stages, engs=("dve", "act")):
            rows, N = src2.shape
            KC = (rows + 127) // 128
            i = 0
            for k in range(KC):
                pr = min(128, rows - k * 128)
                for n0 in range(0, N, stage_n):
                    n1 = min(N, n0 + stage_n)
                    st, stb = stages[load_w.i % len(stages)]
                    eng = engs[load_w.i % len(engs)]
                    load_w.i += 1
                    T.dma("sp", st[:pr, :n1 - n0], src2[k * 128:k * 128 + pr, n0:n1], writes=[stb])
                    if eng == "dve":
                        T.op("dve", lambda e, o=dst3[:pr, k, n0:n1], i_=st[:pr, :n1 - n0]: e.tensor_copy(out=o, in_=i_),
                             reads=[stb], writes=[Buf()])
                    else:
                        T.op("act", lambda e, o=dst3[:pr, k, n0:n1], i_=st[:pr, :n1 - n0]: e.activation(out=o, in_=i_, func=AF.Copy),
                             reads=[stb], writes=[Buf()])
        load_w.i = 0

        def rstd(src, srcb, P, n, c):
            T.op("act", lambda e: e.activation(out=c.junk[:P, :src.shape[-1]] if len(src.shape) == 2 else c.junk[:P, :src.shape[-1]],
                                               in_=src, func=AF.Square, accum_out=c.ss[:P]),
                 reads=[srcb], writes=[c.junkb, c.ssb])
            T.op("act", lambda e: e.activation(out=c.sd[:P], in_=c.ss[:P], func=AF.Sqrt, scale=1.0 / n, bias=EPS),
                 reads=[c.ssb], writes=[c.sdb])
            T.op("dve", lambda e: e.reciprocal(out=c.r[:P], in_=c.sd[:P]), reads=[c.sdb], writes=[c.rb])

        def transposes(src16, srcb, P, nblk, width, dstT, dstTb):
            def f(e):
                ins = None
                for j in range(nblk):
                    ins = e.transpose(out=pT[:width, j * 128:j * 128 + P], in_=src16[:P, j * width:(j + 1) * width],
                                      identity=idb[:P, :P])
                return ins
            T.op("pe", f, reads=[srcb, idbb], writes=[pTb])
            pv = pT[:width, :nblk * 128].rearrange("f (j p) -> f j p", p=128)[:, :, :P]
            T.op("act", lambda e: e.activation(out=dstT[:width, :nblk, :P], in_=pv, func=AF.Copy),
                 reads=[pTb], writes=[dstTb])

        def mm_tokmajor(out_ps, outb, lhsT3, lhsTb, P, W3, kc, c0, c1):
            def f(e):
                ins = None
                for k in range(kc):
                    ins = e.matmul(out_ps[:P, :c1 - c0], lhsT=lhsT3[:, k, :P], rhs=W3[:, k, c0:c1],
                                   start=(k == 0), stop=(k == kc - 1))
                return ins
            T.op("pe", f, reads=[lhsTb], writes=[outb])

        def rope(src3, srcb, P, H, Dh, blocks, cs, csb, out3, outb, c):
            t1 = c.rt1[:P, :H * Dh].rearrange("p (h d) -> p h d", d=Dh)
            t2 = c.rt2[:P, :H * Dh].rearrange("p (h d) -> p h d", d=Dh)
            cosb = cs[:P, 0:Dh].unsqueeze(1).broadcast_to([P, H, Dh])
            T.op("dve", lambda e: e.tensor_tensor(out=t1, in0=src3, in1=cosb, op=ALU.mult),
                 reads=[srcb, csb], writes=[c.rt1b])
            hb = Dh // blocks // 2
            for b in range(blocks):
                lo = b * 2 * hb
                s_lo = cs[:P, Dh + lo:Dh + lo + hb].unsqueeze(1).broadcast_to([P, H, hb])
                s_hi = cs[:P, Dh + lo + hb:Dh + lo + 2 * hb].unsqueeze(1).broadcast_to([P, H, hb])
                T.op("dve", lambda e, lo=lo, s_lo=s_lo: e.tensor_tensor(out=t2[:, :, lo:lo + hb], in0=src3[:, :, lo + hb:lo + 2 * hb],
                                                                       in1=s_lo, op=ALU.mult),
                     reads=[srcb, csb], writes=[c.rt2b])
                T.op("dve", lambda e, lo=lo, s_hi=s_hi: e.tensor_tensor(out=t2[:, :, lo + hb:lo + 2 * hb], in0=src3[:, :, lo:lo + hb],
                                                                       in1=s_hi, op=ALU.mult),
                     reads=[srcb, csb], writes=[c.rt2b])
            T.op("dve", lambda e: e.tensor_tensor(out=out3, in0=t1, in1=t2, op=ALU.add),
                 reads=[c.rt1b, c.rt2b], writes=[outb])

        def phase1(l):
            P16.mark()
            P32.mark()
            Win_f, _ = P16.alloc(8 * 1440, "Win")
            Win = Win_f.rearrange("p (k n) -> p k n", n=1440)
            Wq_f, _ = P16.alloc(3 * 768, "Wq")
            Wq = Wq_f.rearrange("p (k n) -> p k n", n=768)
            Wkv_f, _ = P16.alloc(2 * 1024, "Wkv")
            Wkv = Wkv_f.rearrange("p (k n) -> p k n", n=1024)
            g_attn, gb1 = P32.alloc(1024)
            g_qa, gb2 = P32.alloc(384)
            g_kva, gb3 = P32.alloc(256)
            g_gq, gb4 = P32.alloc(64)
            g_gk, gb5 = P32.alloc(64)
            bcast_load(g_attn, gb1, g_attn_d[l], 1024)
            bcast_load(g_qa, gb2, g_qa_d[l], 384)
            bcast_load(g_kva, gb3, g_kva_d[l], 256)
            bcast_load(g_gq, gb4, g_gq_d[l], 64)
            bcast_load(g_gk, gb5, g_gk_d[l], 64)
            gbufs = [gb1, gb2, gb3, gb4, gb5]
            P32.mark()
            stages = [P32.alloc(stage_n) for _ in range(3)]
            load_w(Win, w_in_d[l], stages)
            load_w(Wq, w_qb_d[l], stages)
            load_w(Wkv, w_kvb_d[l], stages)
            T.barrier()
            P32.release()

            ctxs = []
            for i in range(2):
                c = Ctx()
                c.x, c.xb = P32.alloc(1024)
                c.csm, c.csmb = P32.alloc(64)
                c.csg, c.csgb = P32.alloc(128)
                c.ss, c.ssb = P32.alloc(1)
                c.sd, c.sdb = P32.alloc(1)
                c.r, c.rb = P32.alloc(1)
                c.ss10, c.ss10b = P32.alloc(10)
                c.sd10, c.sd10b = P32.alloc(10)
                c.r10, c.r10b = P32.alloc(10)
                c.rt1, c.rt1b = P32.alloc(640)
                c.rt2, c.rt2b = P32.alloc(512)
                c.gn, c.gnb = P32.alloc(640)
                c.q32, c.q32b = P32.alloc(768)
                c.sq10, c.sq10b = c.rt1, c.rt1b
                c.kr32, c.kr32b = P32.alloc(32)
                c.junk, c.junkb = P16.alloc(1024)
                c.hn, c.hnb = P16.alloc(1024)
                c.hnT, c.hnTb = P16.alloc(1024)
                c.cqn, c.cqnb = P16.alloc(384)
                c.cqnT, c.cqnTb = P16.alloc(384)
                c.ckvn, c.ckvnb = P16.alloc(256)
                c.ckvnT, c.ckvnTb = P16.alloc(256)
                c.q16, c.q16b = P16.alloc(768)
                c.qT, c.qTb = P16.alloc(1024)
                c.kn, c.knb = P16.alloc(512)
                c.knT, c.knTb = P16.alloc(512)
                c.vst, c.vstb = P16.alloc(650)
                T.op("dve", lambda e, c=c: e.memset(c.vst.rearrange("p (h d) -> p h d", d=65)[:, :, 64:65], 1.0), writes=[c.vstb])
                c.kpe, c.kpeb = P16.alloc(32)
                c.kpeT, c.kpeTb = P16.alloc(128)
                c.gq16, c.gq16b = P16.alloc(512)
                c.gqT, c.gqTb = P16.alloc(512)
                c.gk16, c.gk16b = P16.alloc(128)
                c.gkT, c.gkTb = P16.alloc(128)
                ctxs.append(c)

            tile_i = 0
            for s in range(2):
                for t in range(NT + 1):
                    c = ctxs[tile_i % 2]
                    tile_i += 1
                    is_meta = (t == NT)
                    P = NMETA if is_meta else 128
                    tok0 = t * 128
                    if l == 0:
                        src = meta[:, :] if is_meta else xq[s, tok0:tok0 + P, :]
                    else:
                        src = X1[s][tok0:tok0 + P, :]
                    T.dma("sp", c.x[:P], src, writes=[c.xb])
                    T.dma("sp", c.csm[:P], csm_d[s, tok0:tok0 + P, :], writes=[c.csmb])
                    T.dma("sp", c.csg[:P], csg_d[s, tok0:tok0 + P, :], writes=[c.csgb])
                    rstd(c.x[:P], c.xb, P, 1024, c)
                    T.op("dve", lambda e, c=c, P=P: e.scalar_tensor_tensor(out=c.hn[:P], in0=c.x[:P], scalar=c.r[:P], in1=g_attn[:P],
                                                                         op0=ALU.mult, op1=ALU.mult),
                         reads=[c.xb, c.rb] + gbufs, writes=[c.hnb])
                    hnT3 = c.hnT.rearrange("p (k t) -> p k t", t=128)
                    transposes(c.hn, c.hnb, P, 8, 128, hnT3, c.hnTb)
                    mm_tokmajor(bank[0], bankb[0], hnT3, c.hnTb, P, Win, 8, 0, 512)
                    mm_tokmajor(bank[2], bankb[2], hnT3, c.hnTb, P, Win, 8, 1024, 1440)
                    mm_tokmajor(bank[1], bankb[1], hnT3, c.hnTb, P, Win, 8, 512, 1024)
                    rstd(bank[0][:P, 0:384], bankb[0], P, 384, c)
                    T.op("dve", lambda e, c=c, P=P: e.scalar_tensor_tensor(out=c.cqn[:P], in0=bank[0][:P, 0:384], scalar=c.r[:P],
                                                                         in1=g_qa[:P], op0=ALU.mult, op1=ALU.mult),
                         reads=[bankb[0], c.rb] + gbufs, writes=[c.cqnb])
                    cqnT3 = c.cqnT.rearrange("p (k t) -> p k t", t=128)
                    transposes(c.cqn, c.cqnb, P, 3, 128, cqnT3, c.cqnTb)
                    rstd(bank[2][:P, 0:256], bankb[2], P, 256, c)
                    T.op("dve", lambda e, c=c, P=P: e.scalar_tensor_tensor(out=c.ckvn[:P], in0=bank[2][:P, 0:256], scalar=c.r[:P],
                                                                         in1=g_kva[:P], op0=ALU.mult, op1=ALU.mult),
                         reads=[bankb[2], c.rb] + gbufs, writes=[c.ckvnb])
                    ckvnT3 = c.ckvnT.rearrange("p (k t) -> p k t", t=128)
                    transposes(c.ckvn, c.ckvnb, P, 2, 128, ckvnT3, c.ckvnTb)
                    mm_tokmajor(bank[3], bankb[3], cqnT3, c.cqnTb, P, Wq, 3, 0, 384)
                    mm_tokmajor(bank[4], bankb[4], cqnT3, c.cqnTb, P, Wq, 3, 384, 768)
                    mm_tokmajor(bank[5], bankb[5], ckvnT3, c.ckvnTb, P, Wkv, 2, 0, 512)
                    mm_tokmajor(bank[6], bankb[6], ckvnT3, c.ckvnTb, P, Wkv, 2, 512, 1024)
                    q3 = c.q16.rearrange("p (h d) -> p h d", d=96)
                    q32 = c.q32.rearrange("p (h d) -> p h d", d=96)
                    for hb_, bk in ((0, 3), (1, 4)):
                        T.op("act", lambda e, bk=bk, hb_=hb_, P=P, c=c: e.activation(out=c.q32[:P, hb_ * 384:(hb_ + 1) * 384], in_=bank[bk][:P, 0:384],
                                                                                   func=AF.Copy),
                             reads=[bankb[bk]], writes=[c.q32b])
                    T.op("dve", lambda e, P=P, q3=q3, q32=q32: e.tensor_copy(out=q3[:P, :, 0:64], in_=q32[:P, :, 0:64]),
                         reads=[c.q32b], writes=[c.q16b])
                    rope(q32[:P, :, 64:96], c.q32b, P, 8, 32, 1, c.csm, c.csmb, q3[:P, :, 64:96], c.q16b, c)
                    def fq(e, c=c, P=P):
                        ins = None
                        for h in range(8):
                            ins = e.transpose(out=pT[:96, h * 128:h * 128 + P], in_=c.q16[:P, h * 96:(h + 1) * 96], identity=idb[:P, :P])
                        return ins
                    T.op("pe", fq, reads=[c.q16b, idbb], writes=[pTb])
                    qT3 = c.qT.rearrange("p (h t) -> p h t", t=128)
                    T.op("act", lambda e, P=P, qT3=qT3: e.activation(out=qT3[:96, :, :P],
                                                                   in_=pT[:96, :].rearrange("f (h t) -> f h t", t=128)[:, :, :P], func=AF.Copy),
                         reads=[pTb], writes=[c.qTb])
                    T.dma("pool", QTm[s].rearrange("(h d) t -> d h t", d=96)[:, :, tok0:tok0 + P], qT3[:96, :, :P],
                          reads=[c.qTb], writes=[Buf()])
                    v3 = c.vst.rearrange("p (h d) -> p h d", d=65)
                    for hb_, bk in ((0, 5), (1, 6)):
                        pkv = bank[bk][:P, :].rearrange("p (h d) -> p h d", d=128)
                        T.op("act", lambda e, pkv=pkv, hb_=hb_, P=P, v3=v3: e.activation(out=v3[:P, hb_ * 4:hb_ * 4 + 4, 0:64], in_=pkv[:, :, 64:128],
                                                                                       func=AF.Copy),
                             reads=[bankb[bk]], writes=[c.vstb])
                    T.op("act", lambda e, P=P, v3=v3: e.activation(out=v3[:P, 8:10, 0:64],
                                                                 in_=bank[0][:P, 384:512].rearrange("p (h d) -> p h d", d=64), func=AF.Copy),
                         reads=[bankb[0]], writes=[c.vstb])
                    if is_meta:
                        vdst = Vmeta[s].rearrange("(h t) d -> t h d", t=NMETA)
                    else:
                        vdst = Vloc[s].ap().rearrange("(h t) d -> t h d", t=N_OWN)[tok0:tok0 + P]
                    T.dma("pool", vdst, v3[:P], reads=[c.vstb], writes=[Buf()])
                    kn3 = c.kn.rearrange("p (h d) -> p h d", d=64)
                    for hb_, bk in ((0, 5), (1, 6)):
                        pkv = bank[bk][:P, :].rearrange("p (h d) -> p h d", d=128)
                        T.op("dve", lambda e, pkv=pkv, hb_=hb_, P=P, kn3=kn3: e.tensor_copy(out=kn3[:P, hb_ * 4:hb_ * 4 + 4, :], in_=pkv[:, :, 0:64]),
                             reads=[bankb[bk]], writes=[c.knb])
                    knT3 = c.knT.rearrange("p (j t) -> p j t", t=128)
                    transposes(c.kn, c.knb, P, 4, 128, knT3, c.knTb)
                    ktd = KTmeta[s] if is_meta else KTloc[s].ap()[:, tok0:tok0 + P]
                    T.dma("pool", ktd[0:512].rearrange("(j p) t -> p j t", p=128), knT3[:, :, :P], reads=[c.knTb], writes=[Buf()])
                    kpe3 = c.kpe[:, 0:32].rearrange("p (h d) -> p h d", d=32)
                    T.op("act", lambda e, c=c, P=P: e.activation(out=c.kr32[:P], in_=bank[2][:P, 256:288], func=AF.Copy),
                         reads=[bankb[2]], writes=[c.kr32b])
                    rope(c.kr32[:P].rearrange("p (h d) -> p h d", d=32), c.kr32b, P, 1, 32, 1, c.csm, c.csmb, kpe3[:P], c.kpeb, c)
                    kpeT3 = c.kpeT.rearrange("p (j t) -> p j t", t=128)
                    transposes(c.kpe, c.kpeb, P, 1, 32, kpeT3, c.kpeTb)
                    T.dma("pool", ktd[512:544], kpeT3[:32, 0, :P], reads=[c.kpeTb], writes=[Buf()])
                    gn3 = c.gn[:P, :].rearrange("p (h d) -> p h d", d=64)
                    T.op("act", lambda e, c=c, P=P: e.activation(out=c.gn[:P, 0:512], in_=bank[1][:P, :], func=AF.Copy),
                         reads=[bankb[1]], writes=[c.gnb])
                    T.op("act", lambda e, c=c, P=P: e.activation(out=c.gn[:P, 512:640], in_=bank[2][:P, 288:416], func=AF.Copy),
                         reads=[bankb[2]], writes=[c.gnb])
                    T.op("dve", lambda e, c=c, P=P: e.tensor_tensor(out=c.sq10[:P], in0=c.gn[:P], in1=c.gn[:P], op=ALU.mult),
                         reads=[c.gnb], writes=[c.sq10b])
                    T.op("dve", lambda e, c=c, P=P: e.tensor_reduce(out=c.ss10[:P], in_=c.sq10[:P].rearrange("p (h d) -> p h d", d=64),
                                                                  axis=AX.X, op=ALU.add),
                         reads=[c.sq10b], writes=[c.ss10b])
                    T.op("act", lambda e, c=c, P=P: e.activation(out=c.sd10[:P], in_=c.ss10[:P], func=AF.Sqrt, scale=1.0 / 64, bias=EPS),
                         reads=[c.ss10b], writes=[c.sd10b])
                    T.op("dve", lambda e, c=c, P=P: e.reciprocal(out=c.r10[:P], in_=c.sd10[:P]), reads=[c.sd10b], writes=[c.r10b])
                    T.op("dve", lambda e, c=c, P=P, gn3=gn3: e.tensor_tensor(
                        out=gn3, in0=gn3, in1=c.r10[:P, 0:10].unsqueeze(2).broadcast_to([P, 10, 64]), op=ALU.mult),
                        reads=[c.gnb, c.r10b], writes=[c.gnb])
                    T.op("dve", lambda e, P=P, gn3=gn3: e.tensor_tensor(
                        out=gn3[:, 0:8, :], in0=gn3[:, 0:8, :], in1=g_gq[:P, :].unsqueeze(1).broadcast_to([P, 8, 64]), op=ALU.mult),
                        reads=[c.gnb] + gbufs, writes=[c.gnb])
                    T.op("dve", lambda e, P=P, gn3=gn3: e.tensor_tensor(
                        out=gn3[:, 8:10, :], in0=gn3[:, 8:10, :], in1=g_gk[:P, :].unsqueeze(1).broadcast_to([P, 2, 64]), op=ALU.mult),
                        reads=[c.gnb] + gbufs, writes=[c.gnb])
                    gq16_3 = c.gq16.rearrange("p (h d) -> p h d", d=64)
                    gk16_3 = c.gk16.rearrange("p (h d) -> p h d", d=64)
                    rope(gn3[:, 0:8, :], c.gnb, P, 8, 64, 2, c.csg, c.csgb, gq16_3[:P], c.gq16b, c)
                    rope(gn3[:, 8:10, :], c.gnb, P, 2, 64, 2, c.csg, c.csgb, gk16_3[:P], c.gk16b, c)
                    gqT3 = c.gqT.rearrange("p (j t) -> p j t", t=128)
                    transposes(c.gq16, c.gq16b, P, 4, 128, gqT3, c.gqTb)
                    T.dma("pool", QTg[s].rearrange("(j p) t -> p j t", p=128)[:, :, tok0:tok0 + P], gqT3[:, :, :P],
                          reads=[c.gqTb], writes=[Buf()])
                    gkT3 = c.gkT.rearrange("p (j t) -> p j t", t=128)
                    transposes(c.gk16, c.gk16b, P, 1, 128, gkT3, c.gkTb)
                    T.dma("pool", ktd[544:672], gkT3[:, 0, :P], reads=[c.gkTb], writes=[Buf()])
            T.barrier()
            P16.release()
            P32.release()

        pieceb = {}

        def phase2():
            for s in range(2):
                R = RS[s]
                groups = [list(range(g * R, (g + 1) * R)) for g in range(8 // R)]
                kpc = [(("k", s, i), KTloc[s].ap()[a:a + n], KTall[s][i].ap()) for i, (a, n) in enumerate(KPIECES)]
                vpc = [(("v", s, h), Vloc[s].ap()[h * N_OWN:(h + 1) * N_OWN], Vall[s][h].ap()) for h in range(10)]
                order = [kpc[8]]
                for h in range(8):
                    order += [kpc[h], vpc[h]]
                order += [kpc[9], vpc[8], kpc[10], vpc[9]]
                for key, src, dst in order:
                    pieceb[key] = Buf()
                    T.custom("pool", lambda e, src=src, dst=dst, groups=groups: e.collective_compute(
                        "AllGather", ALU.bypass, replica_groups=groups, ins=[src], outs=[dst]), 1, writes=[pieceb[key]])

        def phase3(l):
            P16.mark()
            P32.mark()
            nq_eff = NQ if l == 0 else N_OWN
            ngrp = (nq_eff + 511) // 512
            units = nq_eff // 16
            base = units // ngrp
            widths = [16 * (base + (1 if i < units - base * ngrp else 0)) for i in range(ngrp)]
            assert sum(widths) == nq_eff and max(widths) <= 512
            g0s = [sum(widths[:i]) for i in range(ngrp)]
            LKMAX = 4 * N_OWN + NMETA
            NCHMAX = 4 * NC
            Kt = [(P16.alloc(LKMAX, f"K{i}")[0], [Buf() for _ in range(4)]) for i in range(2)]
            Vt = [P16.alloc(NCHMAX * 65, f"V{i}") for i in range(2)]
            Vm = [P16.alloc(65, f"Vm{i}") for i in range(2)]
            Qt = [(P16.alloc(NQ, f"Q{i}")[0], [Buf(), Buf()]) for i in range(2)]
            for i in range(2):
                T.op("dve", lambda e, i=i: e.memset(Kt[i][0][64:128, :], 0.0), writes=[Kt[i][1][1], Kt[i][1][3]])
            NPB = 3
            Pt = [P16.alloc(1024, f"P{i}") for i in range(NPB)]
            Osb = [P32.alloc(512, f"Osb{i}") for i in range(2)]
            aost = [P32.alloc(256, f"ao{i}") for i in range(2)]
            rd = [P32.alloc(4, f"rd{i}") for i in range(2)]
            Sb = [(ps32[:, b * 1024:(b + 1) * 1024], Buf(f"S{b}")) for b in range(2)]
            Ob = [(bank[4], bankb[4]), (bank[5], bankb[5])]
            OT, OTb = bank[6], bankb[6]

            kvsets = []
            for s in range(2):
                for h in range(8):
                    kvsets.append((s, "mla", h))
                for j in range(2):
                    kvsets.append((s, "gqa", j))

            def load_kv(idx):
                s, kind, h = kvsets[idx]
                R = RS[s]
                K, Kb = Kt[idx % 2]
                V, Vb = Vt[idx % 2]
                VM, VMb = Vm[idx % 2]
                def kp(i):
                    return KTall[s][i].ap().rearrange("(r f) t -> f r t", f=KPIECES[i][1])
                Kv = K[:, 0:R * N_OWN].rearrange("d (r t) -> d r t", t=N_OWN)
                mcol = slice(R * N_OWN, R * N_OWN + NMETA)
                if kind == "mla":
                    T.dma("sp", Kv[0:64], kp(h), reads=[pieceb[("k", s, h)]], writes=[Kb[0]])
                    T.dma("sp", Kv[64:96], kp(8), reads=[pieceb[("k", s, 8)]], writes=[Kb[1]])
                    T.dma("sp", K[0:64, mcol], KTmeta[s][h * 64:(h + 1) * 64, :], writes=[Kb[2]])
                    T.dma("sp", K[64:96, mcol], KTmeta[s][512:544, :], writes=[Kb[3]])
                    hv = h
                else:
                    T.dma("sp", Kv[0:64], kp(9 + h), reads=[pieceb[("k", s, 9 + h)]], writes=[Kb[0]])
                    T.dma("sp", K[0:64, mcol], KTmeta[s][544 + h * 64:544 + (h + 1) * 64, :], writes=[Kb[2]])
                    hv = 8 + h
                vall = Vall[s][hv].ap().rearrange("(r p c) d -> p r c d", p=128, c=NC)
                V4 = V[:, 0:R * NC * 65].rearrange("p (r c d) -> p r c d", c=NC, d=65)
                T.dma("sp", V4, vall, reads=[pieceb[("v", s, hv)]], writes=[Vb])
                T.dma("sp", VM[0:NMETA, 0:65], Vmeta[s][hv * NMETA:(hv + 1) * NMETA, :], writes=[VMb])

            def qheads(idx):
                s, kind, h = kvsets[idx]
                if kind == "mla":
                    return [(s, "mla", h, h)]
                return [(s, "gqa", h * 4 + g, 8 + h * 4 + g) for g in range(4)]

            qlist = []
            for idx in range(len(kvsets)):
                for qh in qheads(idx):
                    qlist.append((idx, qh))

            def load_q(qi):
                idx, (s, kind, qh, _) = qlist[qi]
                Q, Qb = Qt[qi % 2]
                if kind == "mla":
                    T.dma("sp", Q[0:96, 0:nq_eff], QTm[s][qh * 96:(qh + 1) * 96, 0:nq_eff], writes=[Qb[0], Qb[1]])
                else:
                    T.dma("sp", Q[0:64, 0:nq_eff], QTg[s][qh * 64:(qh + 1) * 64, 0:nq_eff], writes=[Qb[0]])
                    T.op("dve", lambda e, Q=Q: e.memset(Q[64:128, 0:nq_eff], 0.0), writes=[Qb[1]])

            steps = []
            for qi, (idx, (s, kind, qh, cb)) in enumerate(qlist):
                R = RS[s]
                nch = R * NC + 1
                for gi in range(ngrp):
                    for c0 in range(0, R * NC, 2):
                        steps.append((qi, gi, [c0, c0 + 1], nch))
                    steps.append((qi, gi, [R * NC], nch))
            LA = 2
            gcount = [0]

            def kcols(idx, cix):
                s, kind, h = kvsets[idx]
                R = RS[s]
                d = 96 if kind == "mla" else 128
                K, Kb = Kt[idx % 2]
                if cix == R * NC:
                    return K[0:d, R * N_OWN:R * N_OWN + NMETA], NMETA, d
                r, cl = divmod(cix, NC)
                return K[0:d, r * N_OWN:(r + 1) * N_OWN].rearrange("d (p c) -> d p c", c=NC)[:, :, cl], 128, d

            def emit_qk(i):
                qi, gi, chunks, nch = steps[i]
                idx, (s, kind, qh, cb) = qlist[qi]
                Q, Qb = Qt[qi % 2]
                G = widths[gi]
                g0 = g0s[gi]
                S, Sbuf = Sb[i % 2]

                def f(e):
                    ins = None
                    for k, cix in enumerate(chunks):
                        lhsT, kc, d = kcols(idx, cix)
                        ins = e.matmul(S[:kc, k * 512:k * 512 + G], lhsT=lhsT, rhs=Q[0:d, g0:g0 + G], start=True, stop=True)
                    return ins
                T.op("pe", f, reads=Kt[idx % 2][1] + Qb, writes=[Sbuf])

            def emit_exp_pv(i):
                qi, gi, chunks, nch = steps[i]
                idx, (s, kind, qh, cb) = qlist[qi]
                R = RS[s]
                scale = (96.0 if kind == "mla" else 64.0) ** -0.5
                n = len(chunks)
                kc = NMETA if chunks[0] == R * NC else 128
                G = widths[gi]
                S, Sbuf = Sb[i % 2]
                Pp, Pb = Pt[i % NPB]
                gidx = gcount[0]
                O, Obuf = Ob[gidx % 2]
                S3 = S.rearrange("p (k g) -> p k g", g=512)[:kc, 0:n, 0:G]
                P3 = Pp.rearrange("p (k g) -> p k g", g=512)[:kc, 0:n, 0:G]
                T.op("act", lambda e, S3=S3, P3=P3, scale=scale: e.activation(out=P3, in_=S3, func=AF.Exp, scale=scale),
                     reads=[Sbuf], writes=[Pb])
                j = i + LA
                if j < len(steps):
                    ensure_loaded(j)
                    emit_qk(j)
                if kc == 128:
                    V, Vb = Vt[idx % 2]
                    V3 = V.rearrange("p (c d) -> p c d", d=65)
                    lhs = [V3[:, cix, :] for cix in chunks]
                else:
                    V, Vb = Vm[idx % 2]
                    lhs = [V[0:NMETA, 0:65]]

                def fpv(e):
                    ins = None
                    for k, cix in enumerate(chunks):
                        ins = e.matmul(O[0:65, :G], lhsT=lhs[k], rhs=Pp[:kc, k * 512:k * 512 + G],
                                       start=(cix == 0), stop=(cix == nch - 1))
                    return ins
                T.op("pe", fpv, reads=[Vb, Pb], writes=[Obuf])
                cix = chunks[-1]
                if cix == nch - 1:
                    gcount[0] += 1
                    Os, Osbuf = Osb[gidx % 2]
                    T.op("dve", lambda e, Os=Os, O=O, G=G: e.tensor_copy(out=Os[0:65, :G], in_=O[0:65, :G]),
                         reads=[Obuf], writes=[Osbuf])
                    nsub = (G + 127) // 128
                    OT3 = OT[:, 0:4 * 65].rearrange("p (j d) -> p j d", d=65)

                    def ftr(e, Os=Os, G=G, nsub=nsub):
                        ins = None
                        for j in range(nsub):
                            w = min(128, G - j * 128)
                            ins = e.transpose(out=OT3[:w, j, :], in_=Os[0:65, j * 128:j * 128 + w], identity=id32[0:65, 0:65])
                        return ins
                    T.op("pe", ftr, reads=[Osbuf, id32b], writes=[OTb])
                    rdt, rdb = rd[gidx % 2]
                    ao, aob = aost[gidx % 2]
                    ao3 = ao.rearrange("p (j d) -> p j d", d=64)
                    g0 = g0s[gi]
                    full = G // 128
                    rem = G - full * 128
                    parts = []
                    if full:
                        parts.append((128, 0, full))
                    if rem:
                        parts.append((rem, full, full + 1))
                    for (pw, j0, j1) in parts:
                        T.op("dve", lambda e, pw=pw, j0=j0, j1=j1, rdt=rdt: e.reciprocal(out=rdt[:pw, j0:j1], in_=OT3[:pw, j0:j1, 64]),
                             reads=[OTb], writes=[rdb])
                        for jj in range(j0, j1):
                            T.op("dve", lambda e, pw=pw, jj=jj, rdt=rdt, ao3=ao3: e.tensor_scalar(
                                out=ao3[:pw, jj, :], in0=OT3[:pw, jj, 0:64], scalar1=rdt[:pw, jj:jj + 1], scalar2=None, op0=ALU.mult),
                                reads=[OTb, rdb], writes=[aob])
                        if j1 - j0 > 1 or True:
                            dst = AO[s][g0 + j0 * 128:g0 + j0 * 128 + (j1 - j0 - 1) * 128 + pw, cb * 64:(cb + 1) * 64]
                            if j1 - j0 == 1:
                                T.dma("sp", dst, ao3[:pw, j0, :], reads=[aob], writes=[Buf()])
                            else:
                                T.dma("sp", dst.rearrange("(j p) d -> p j d", p=128), ao3[:pw, j0:j1, :], reads=[aob], writes=[Buf()])

            load_kv(0)
            load_q(0)
            started_kv = {0}
            started_q = {0}
            n = len(steps)
            def ensure_loaded(j):
                qi = steps[j][0]
                if qi not in started_q:
                    load_q(qi)
                    started_q.add(qi)
                idx = qlist[qi][0]
                if idx not in started_kv:
                    load_kv(idx)
                    started_kv.add(idx)

            for j in range(min(LA, n)):
                ensure_loaded(j)
                emit_qk(j)
            for i in range(n):
                if i >= 0:
                    emit_exp_pv(i)
                    qi = steps[i][0]
                    if steps[i][1] == 0 and steps[i][2][0] == 0:
                        if qi + 1 < len(qlist) and (qi + 1) not in started_q:
                            load_q(qi + 1)
                            started_q.add(qi + 1)
                            nidx = qlist[qi + 1][0]
                            if nidx not in started_kv:
                                load_kv(nidx)
                                started_kv.add(nidx)
            T.barrier()
            P16.release()
            P32.release()

        def phase4(l):
            last = (l == depth - 1)
            P16.mark()
            P32.mark()
            Wo_f, _ = P16.alloc(8 * 1024, "Wo")
            Wo = Wo_f.rearrange("p (k n) -> p k n", n=1024)
            Wu_f, _ = P16.alloc(8 * 4096, "Wu")
            Wu = Wu_f.rearrange("p (k n) -> p k n", n=4096)
            Wd_f, _ = P16.alloc(32 * 1024, "Wd")
            Wd = Wd_f.rearrange("p (k n) -> p k n", n=1024)
            g_out, gb1 = P32.alloc(1024)
            g_mlp, gb2 = P32.alloc(1024)
            bcast_load(g_out, gb1, g_out_d[l], 1024)
            bcast_load(g_mlp, gb2, g_mlp_d[l], 1024)
            gbufs = [gb1, gb2]
            if last:
                g_fin, gb3 = P32.alloc(1024)
                bcast_load(g_fin, gb3, g_fin_d, 1024)
                gbufs.append(gb3)
            P32.mark()
            stages = [P32.alloc(stage_n) for _ in range(3)]
            load_w(Wo, w_out_d[l], stages)
            load_w(Wu, w_up_d[l], stages)
            load_w(Wd, w_down_d[l], stages)
            T.barrier()
            P32.release()

            ctxs = []
            for i in range(2):
                c = Ctx()
                c.x, c.xb = P32.alloc(1024)
                c.ao, c.aob = P32.alloc(1024)
                c.ss, c.ssb = P32.alloc(1)
                c.sd, c.sdb = P32.alloc(1)
                c.r, c.rb = P32.alloc(1)
                c.ss2, c.ss2b = P32.alloc(2)
                c.sd2, c.sd2b = P32.alloc(2)
                c.r2, c.r2b = P32.alloc(2)
                if i == 0:
                    c.junk, c.junkb = P16.alloc(1024)
                    c.mix, c.mixb = P16.alloc(1024)
                    c.mixT, c.mixTb = P16.alloc(1024)
                    c.hn, c.hnb = P16.alloc(1024)
                    c.hnT, c.hnTb = P16.alloc(1024)
                else:
                    for nm in ("junk", "mix", "mixT", "hn", "hnT"):
                        setattr(c, nm, getattr(ctxs[0], nm))
                        setattr(c, nm + "b", getattr(ctxs[0], nm + "b"))
                ctxs.append(c)
            xmid, xmidb = P32.alloc(1024)
            xnew, xnewb = P32.alloc(1024)
            r32 = [P32.alloc(512) for _ in range(2)]
            aT = [P16.alloc(512) for _ in range(2)]
            pso = [(bank[0], bankb[0]), (bank[1], bankb[1])]
            pu = [(bank[2], bankb[2]), (bank[3], bankb[3])]
            py = [(bank[4], bankb[4]), (bank[5], bankb[5])]

            tile_i = 0
            for s in range(2):
                for t in range(NT + (0 if last else 1)):
                    c = ctxs[tile_i % 2]
                    tile_i += 1
                    is_meta = (t == NT)
                    P = NMETA if is_meta else 128
                    tok0 = t * 128
                    if l == 0:
                        src = meta[:, :] if is_meta else xq[s, tok0:tok0 + P, :]
                    else:
                        src = X1[s][tok0:tok0 + P, :]
                    T.dma("sp", c.x[:P], src, writes=[c.xb])
                    T.dma("sp", c.ao[:P], AO[s][tok0:tok0 + P, :], writes=[c.aob])
                    for hf in range(2):
                        T.op("act", lambda e, c=c, P=P, hf=hf: e.activation(out=c.junk[:P, 0:512], in_=c.ao[:P, hf * 512:(hf + 1) * 512],
                                                                          func=AF.Square, accum_out=c.ss2[:P, hf:hf + 1]),
                             reads=[c.aob], writes=[c.junkb, c.ss2b])
                    T.op("act", lambda e, c=c, P=P: e.activation(out=c.sd2[:P], in_=c.ss2[:P], func=AF.Sqrt, scale=1.0 / 512, bias=EPS),
                         reads=[c.ss2b], writes=[c.sd2b])
                    T.op("dve", lambda e, c=c, P=P: e.reciprocal(out=c.r2[:P], in_=c.sd2[:P]), reads=[c.sd2b], writes=[c.r2b])
                    for hf in range(2):
                        sl = slice(hf * 512, (hf + 1) * 512)
                        T.op("dve", lambda e, c=c, P=P, hf=hf, sl=sl: e.scalar_tensor_tensor(
                            out=c.mix[:P, sl], in0=c.ao[:P, sl], scalar=c.r2[:P, hf:hf + 1], in1=g_out[:P, sl], op0=ALU.mult, op1=ALU.mult),
                            reads=[c.aob, c.r2b] + gbufs, writes=[c.mixb])
                    mixT3 = c.mixT.rearrange("p (k t) -> p k t", t=128)
                    transposes(c.mix, c.mixb, P, 8, 128, mixT3, c.mixTb)
                    for j in range(2):
                        mm_tokmajor(pso[j][0], pso[j][1], mixT3, c.mixTb, P, Wo, 8, j * 512, (j + 1) * 512)
                    for j in range(2):
                        sl = slice(j * 512, (j + 1) * 512)
                        T.op("dve", lambda e, c=c, P=P, j=j, sl=sl: e.tensor_tensor(out=xmid[:P, sl], in0=pso[j][0][:P, :], in1=c.x[:P, sl], op=ALU.add),
                             reads=[pso[j][1], c.xb], writes=[xmidb])
                    rstd(xmid[:P], xmidb, P, 1024, c)
                    T.op("dve", lambda e, c=c, P=P: e.scalar_tensor_tensor(out=c.hn[:P], in0=xmid[:P], scalar=c.r[:P], in1=g_mlp[:P],
                                                                         op0=ALU.mult, op1=ALU.mult),
                         reads=[xmidb, c.rb] + gbufs, writes=[c.hnb])
                    hnT3 = c.hnT.rearrange("p (k t) -> p k t", t=128)
                    transposes(c.hn, c.hnb, P, 8, 128, hnT3, c.hnTb)

                    def up(fb, c=c, P=P, hnT3=hnT3):
                        U, Ub = pu[fb % 2]

                        def f(e):
                            ins = None
                            for q in range(4):
                                fidx = fb * 4 + q
                                for k in range(8):
                                    ins = e.matmul(U[:, q * 128:q * 128 + P], lhsT=Wu[:, k, fidx * 128:(fidx + 1) * 128], rhs=hnT3[:, k, :P],
                                                   start=(k == 0), stop=(k == 7))
                            return ins
                        T.op("pe", f, reads=[c.hnTb], writes=[Ub])
                        U3 = U.rearrange("p (q t) -> p q t", t=128)[:, :, :P]
                        R3 = r32[fb % 2][0].rearrange("p (q t) -> p q t", t=128)[:, :, :P]
                        A3 = aT[fb % 2][0].rearrange("p (q t) -> p q t", t=128)[:, :, :P]
                        T.op("act", lambda e: e.activation(out=R3, in_=U3, func=AF.Relu), reads=[Ub], writes=[r32[fb % 2][1]])
                        T.op("dve", lambda e: e.tensor_tensor(out=A3, in0=R3, in1=R3, op=ALU.mult), reads=[r32[fb % 2][1]], writes=[aT[fb % 2][1]])

                    def down(fb, P=P):
                        A3 = aT[fb % 2][0].rearrange("p (q t) -> p q t", t=128)

                        def f(e):
                            ins = None
                            for q in range(4):
                                fidx = fb * 4 + q
                                for j in range(2):
                                    ins = e.matmul(py[j][0][:P, :], lhsT=A3[:, q, :P], rhs=Wd[:, fidx, j * 512:(j + 1) * 512],
                                                   start=(fidx == 0), stop=(fidx == 31))
                            return ins
                        T.op("pe", f, reads=[aT[fb % 2][1]], writes=[py[0][1], py[1][1]])

                    for fb in range(8):
                        up(fb)
                        if fb >= 1:
                            down(fb - 1)
                    down(7)
                    for j in range(2):
                        sl = slice(j * 512, (j + 1) * 512)
                        T.op("dve", lambda e, P=P, j=j, sl=sl: e.tensor_tensor(out=xnew[:P, sl], in0=py[j][0][:P, :], in1=xmid[:P, sl], op=ALU.add),
                             reads=[py[j][1], xmidb], writes=[xnewb])
                    if not last:
                        T.dma("pool", X1[s][tok0:tok0 + P, :], xnew[:P], reads=[xnewb], writes=[Buf()])
                    else:
                        rstd(xnew[:P], xnewb, P, 1024, c)
                        T.op("dve", lambda e, c=c, P=P: e.scalar_tensor_tensor(out=c.ao[:P], in0=xnew[:P], scalar=c.r[:P], in1=g_fin[:P],
                                                                             op0=ALU.mult, op1=ALU.mult),
                             reads=[xnewb, c.rb] + gbufs, writes=[c.aob])
                        T.dma("pool", y_d[s, tok0:tok0 + P, :], c.ao[:P], reads=[c.aob], writes=[Buf()])
            T.barrier()
            P16.release()
            P32.release()

        plist = []
        for l in range(depth):
            plist += [lambda l=l: phase1(l), phase2, lambda l=l: phase3(l), lambda l=l: phase4(l)]
        for ph in plist[:nphase]:
            ph()
        if debug:
            dbg = {}
            for s in range(2):
                for nm, t in (("KTloc", KTloc[s]), ("Vloc", Vloc[s])):
                    o = nc.dram_tensor(f"dbg_{nm}{s}", list(t.ap().shape), BF16, kind="ExternalOutput")
                    T.dma("pool", o.ap(), t.ap())
            T.barrier()

        @block.sync
        def _(e):
            T.replay("sp", e)

        @block.tensor
        def _(e):
            T.replay("pe", e)

        @block.scalar
        def _(e):
            T.replay("act", e)

        @block.vector
        def _(e):
            T.replay("dve", e)

        @block.gpsimd
        def _(e):
            T.replay("pool", e)
    return nc


def _inv_freq(dim):
    return (np.float32(1.0) / np.power(np.float32(10000.0), np.arange(0, dim, 2, dtype=np.float32) / np.float32(dim))).astype(np.float32)


def _tables(pos, rows, cols):
    def cs(p, f):
        ang = (p[:, None].astype(np.float32) * f[None, :].astype(np.float32)).astype(np.float32)
        a = np.concatenate([ang, ang], axis=-1).astype(np.float64)
        c, s = np.cos(a), np.sin(a)
        h = ang.shape[1]
        sp = np.concatenate([-s[:, :h], s[:, h:]], axis=-1)
        return c.astype(np.float32), sp.astype(np.float32)
    cm, sm = cs(pos, _inv_freq(32))
    cr, sr = cs(rows, _inv_freq(32))
    cc, sc = cs(cols, _inv_freq(32))
    csm = np.concatenate([cm, sm], axis=-1)
    csg = np.concatenate([cr, cc, sr, sc], axis=-1)
    return np.ascontiguousarray(csm, np.float32), np.ascontiguousarray(csg, np.float32)


_PERM = np.concatenate([np.arange(0, 384), np.arange(1312, 1440), np.arange(672, 1184),
                        np.arange(384, 640), np.arange(640, 672), np.arange(1184, 1312)])

_NC_CACHE = {}


def run_model(x_prompt, x_sample, meta_tokens, attn_norm_g, w_in, q_a_norm_g, w_q_b, kv_a_norm_g, w_kv_b,
              gqa_q_norm_g, gqa_k_norm_g, mla_out_norm_g, gqa_out_norm_g, w_out, mlp_norm_g, w_up, w_down,
              final_norm_g, trace=False, nphase=None, debug=False):
    f = lambda a: np.ascontiguousarray(np.asarray(a), dtype=np.float32)
    x_prompt, x_sample = f(x_prompt), f(x_sample)
    B, n_long, _ = x_prompt.shape
    Bs, n_short, _ = x_sample.shape
    assert B == 2 and Bs == 4 and n_long == 2 * n_short
    N_OWN = n_long // 4
    depth = np.asarray(w_in).shape[0]
    shared = {
        "meta": f(meta_tokens), "ident": np.eye(128, dtype=np.float32),
        "w_in": np.ascontiguousarray(f(w_in)[:, :, _PERM]), "w_q_b": f(w_q_b), "w_kv_b": f(w_kv_b), "w_out": f(w_out),
        "w_up": f(w_up), "w_down": f(w_down), "attn_norm_g": f(attn_norm_g), "q_a_norm_g": f(q_a_norm_g),
        "kv_a_norm_g": f(kv_a_norm_g), "gqa_q_norm_g": f(gqa_q_norm_g), "gqa_k_norm_g": f(gqa_k_norm_g),
        "out_norm_g": np.ascontiguousarray(np.concatenate([f(mla_out_norm_g), f(gqa_out_norm_g)], axis=-1)),
        "mlp_norm_g": f(mlp_norm_g), "final_norm_g": f(final_norm_g),
    }
    in_maps = []
    for c in range(8):
        bl, rl = c // 4, c % 4
        bs, rs = c // 2, c % 2
        xq = np.stack([x_prompt[bl, rl * N_OWN:(rl + 1) * N_OWN], x_sample[bs, rs * N_OWN:(rs + 1) * N_OWN]])
        csm, csg = [], []
        for r in (rl, rs):
            t = np.arange(r * N_OWN, (r + 1) * N_OWN, dtype=np.float32)
            pos = np.concatenate([t + np.float32(NMETA), np.arange(NMETA, dtype=np.float32)])
            rows = np.concatenate([np.floor(t / 64.0), np.zeros(NMETA)]).astype(np.float32)
            cols = np.concatenate([np.mod(t, 64.0), np.zeros(NMETA)]).astype(np.float32)
            a, b = _tables(pos, rows, cols)
            csm.append(a)
            csg.append(b)
        m = dict(shared)
        m["xq"] = np.ascontiguousarray(xq)
        m["csm"] = np.stack(csm)
        m["csg"] = np.stack(csg)
        import os as _os3
        if _os3.environ.get("K_PAD"):
            m["pad"] = np.zeros((int(_os3.environ["K_PAD"]), 1024), np.float32)
        in_maps.append(m)
    key = (N_OWN, depth, nphase, debug)
    if key not in _NC_CACHE:
        _NC_CACHE[key] = build(N_OWN, depth, nphase=nphase, debug=debug)
    nc = _NC_CACHE[key]
    res = run_bass_kernel_spmd(nc, in_maps, core_ids=list(range(8)), trace=trace)
    y_prompt = np.empty_like(x_prompt)
    y_sample = np.empty_like(x_sample)
    for c in range(8):
        y = np.asarray(res.results[c]["y"], dtype=np.float32)
        y_prompt[c // 4, (c % 4) * N_OWN:(c % 4 + 1) * N_OWN] = y[0]
        y_sample[c // 2, (c % 2) * N_OWN:(c % 2 + 1) * N_OWN] = y[1]
    return (y_prompt, y_sample), res


def kernel(**inputs):
    out, _ = run_model(**inputs)
    return out
```

```python
import contextlib
import numpy as np
import ml_dtypes
import concourse.bass as bass
import concourse.mybir as mybir
from concourse.bass_utils import run_bass_kernel_spmd

F32 = mybir.dt.float32
BF16 = mybir.dt.bfloat16
AF = mybir.ActivationFunctionType
ALU = mybir.AluOpType
AX = mybir.AxisListType

ENGS = ("pe", "act", "dve", "pool", "sp")
D = 1024
EPS = 1e-6
NMETA = 16


class Tok:
    __slots__ = ("eng", "sem", "val", "dma")

    def __init__(self, eng, sem, val, dma):
        self.eng, self.sem, self.val, self.dma = eng, sem, val, dma


class Buf:
    __slots__ = ("name", "w", "r")

    def __init__(self, name=""):
        self.name = name
        self.w = None
        self.r = {}


class Tracker:
    def __init__(self, sems, rings):
        self.sem = sems
        self.rings = rings
        self.streams = {e: [] for e in ENGS}
        self.cnt = {e: 0 for e in ENGS}
        self.waited = {e: {} for e in ENGS}
        self.ring_idx = {q: 0 for q in rings}
        self.ring_val = {}
        self.ring_tok = {}

    def _need(self, eng, tok, waits):
        if tok is None:
            return
        if (not tok.dma) and tok.eng == eng and eng == "pe":
            return
        w = self.waited[eng]
        key = id(tok.sem)
        if w.get(key, 0) >= tok.val:
            return
        w[key] = tok.val
        waits.append((tok.sem, tok.val))

    def _deps(self, eng, reads, writes):
        waits = []
        for b in reads:
            self._need(eng, b.w, waits)
        for b in writes:
            self._need(eng, b.w, waits)
            for t in b.r.values():
                self._need(eng, t, waits)
        return waits

    def _commit(self, tok, reads, writes):
        k = id(tok.sem)
        for b in reads:
            o = b.r.get(k)
            if o is None or o.val < tok.val:
                b.r[k] = tok
        for b in writes:
            b.w = tok
            b.r = {}

    def _skip(self):
        self.nrec = getattr(self, "nrec", 0) + 1
        return self.nrec > getattr(self, "maxops", 1 << 60)

    def op(self, eng, fn, reads=(), writes=()):
        if self._skip():
            return None
        waits = self._deps(eng, reads, writes)
        self.cnt[eng] += 1
        tok = Tok(eng, self.sem[eng], self.cnt[eng], False)
        self.streams[eng].append((waits, fn, self.sem[eng], 1))
        self._commit(tok, reads, writes)
        return tok

    def _ring(self, q, fn, inc, reads, writes):
        if self._skip():
            return None
        ring = self.rings[q]
        sem = ring[self.ring_idx[q] % len(ring)]
        self.ring_idx[q] += 1
        waits = self._deps(q, reads, writes)
        prev = self.ring_tok.get(id(sem))
        if prev is not None:
            self._need(q, prev, waits)
        val = self.ring_val.get(id(sem), 0) + inc
        self.ring_val[id(sem)] = val
        tok = Tok(q, sem, val, True)
        self.ring_tok[id(sem)] = tok
        self.streams[q].append((waits, fn, sem, inc))
        self._commit(tok, reads, writes)
        return tok

    def dma(self, q, out, in_, reads=(), writes=()):
        return self._ring(q, lambda e, out=out, in_=in_: e.dma_start(out=out, in_=in_), 16, reads, writes)

    def custom(self, q, fn, inc, reads=(), writes=(), ring="cc"):
        if self._skip():
            return None
        rg = self.rings[ring]
        sem = rg[self.ring_idx[ring] % len(rg)]
        self.ring_idx[ring] += 1
        waits = self._deps(q, reads, writes)
        prev = self.ring_tok.get(id(sem))
        if prev is not None:
            self._need(q, prev, waits)
        val = self.ring_val.get(id(sem), 0) + inc
        self.ring_val[id(sem)] = val
        tok = Tok(q, sem, val, True)
        self.ring_tok[id(sem)] = tok
        self.streams[q].append((waits, fn, sem, inc))
        self._commit(tok, reads, writes)
        return tok

    def barrier(self):
        toks = []
        for e in ENGS:
            if self.cnt[e] > 0:
                toks.append(Tok(e, self.sem[e], self.cnt[e], False))
        toks.extend(self.ring_tok.values())
        for e in ENGS:
            waits = []
            for t in toks:
                if (not t.dma) and t.eng == e:
                    continue
                self._need(e, t, waits)
            if waits:
                self.streams[e].append((waits, None, None, 0))

    def replay(self, eng, e):
        for waits, fn, sem, inc in self.streams[eng]:
            for s, v in waits:
                e.wait_ge(s, v)
            if fn is not None:
                fn(e).then_inc(sem, inc)


class Pool:
    def __init__(self, tensor, size):
        self.t, self.size, self.off, self.marks = tensor, size, 0, []

    def alloc(self, n, name=""):
        a = self.off
        self.off += n
        assert self.off <= self.size, (name, self.off, self.size)
        return self.t[:, a:a + n], Buf(name)

    def mark(self):
        self.marks.append(self.off)

    def release(self):
        self.off = self.marks.pop()


class Ctx:
    pass


def build(N_OWN, depth=2, N16=80100, N32=10500, nphase=None, debug=False):
    NT = N_OWN // 128
    NC = NT
    NQ = N_OWN + NMETA
    RS = (4, 2)
    nc = bass.Bass("TRN2", target_bir_lowering=False)

    def din(name, shape, dt=F32):
        return nc.dram_tensor(name, list(shape), dt, kind="ExternalInput").ap()

    xq = din("xq", [2, N_OWN, D])
    meta = din("meta", [NMETA, D])
    ident = din("ident", [128, 128])
    csm_d = din("csm", [2, NQ, 64])
    csg_d = din("csg", [2, NQ, 128])
    w_in_d = din("w_in", [depth, D, 1440])
    w_qb_d = din("w_q_b", [depth, 384, 768])
    w_kvb_d = din("w_kv_b", [depth, 256, 1024])
    w_out_d = din("w_out", [depth, D, D])
    w_up_d = din("w_up", [depth, D, 4096])
    w_down_d = din("w_down", [depth, 4096, D])
    g_attn_d = din("attn_norm_g", [depth, D])
    g_qa_d = din("q_a_norm_g", [depth, 384])
    g_kva_d = din("kv_a_norm_g", [depth, 256])
    g_gq_d = din("gqa_q_norm_g", [depth, 64])
    g_gk_d = din("gqa_k_norm_g", [depth, 64])
    g_out_d = din("out_norm_g", [depth, D])
    g_mlp_d = din("mlp_norm_g", [depth, D])
    g_fin_d = din("final_norm_g", [D])
    y_d = nc.dram_tensor("y", [2, N_OWN, D], F32, kind="ExternalOutput").ap()
    import os as _os2
    if _os2.environ.get("K_PAD"):
        din("pad", [int(_os2.environ["K_PAD"]), 1024])

    dk = dict(kind="ExternalOutput") if debug else {}
    QTm = [nc.dram_tensor(f"QTm{s}", [8 * 96, NQ], BF16, **dk).ap() for s in range(2)]
    QTg = [nc.dram_tensor(f"QTg{s}", [8 * 64, NQ], BF16, **dk).ap() for s in range(2)]
    KTloc = [nc.dram_tensor(f"KTloc{s}", [672, N_OWN], BF16) for s in range(2)]
    Vloc = [nc.dram_tensor(f"Vloc{s}", [10 * N_OWN, 65], BF16) for s in range(2)]
    KPIECES = [(h * 64, 64) for h in range(8)] + [(512, 32), (544, 64), (608, 64)]
    KTall = [[nc.dram_tensor(f"KTall{s}_{i}", [RS[s] * n, N_OWN], BF16) for i, (a, n) in enumerate(KPIECES)] for s in range(2)]
    Vall = [[nc.dram_tensor(f"Vall{s}_{h}", [RS[s] * N_OWN, 65], BF16) for h in range(10)] for s in range(2)]
    KTmeta = [nc.dram_tensor(f"KTmeta{s}", [672, NMETA], BF16, **dk).ap() for s in range(2)]
    Vmeta = [nc.dram_tensor(f"Vmeta{s}", [10 * NMETA, 65], BF16, **dk).ap() for s in range(2)]
    AO = [nc.dram_tensor(f"AO{s}", [NQ, D], F32, **dk).ap() for s in range(2)]
    X1 = [nc.dram_tensor(f"X1{s}", [NQ, D], F32, **dk).ap() for s in range(2)]

    es = contextlib.ExitStack()
    with es:
        sb32 = es.enter_context(nc.sbuf_tensor("sb32", [128, N32], F32))
        sb16 = es.enter_context(nc.sbuf_tensor("sb16", [128, N16], BF16))
        ps32 = es.enter_context(nc.psum_tensor("ps32", [128, 7 * 512], F32))
        ps16 = es.enter_context(nc.psum_tensor("ps16", [128, 1024], BF16))
        sems = {e: es.enter_context(nc.semaphore("s_" + e)) for e in ENGS}
        rings = {q: [es.enter_context(nc.semaphore(f"r_{q}{i}")) for i in range(8 if q != "cc" else 4)] for q in ("sp", "pool", "cc")}
        block = es.enter_context(nc.Block())
        T = Tracker(sems, rings)
        import os as _os
        if _os.environ.get("K_MAXOPS"):
            T.maxops = int(_os.environ["K_MAXOPS"])
        P32 = Pool(sb32, N32)
        P16 = Pool(sb16, N16)
        bank = [ps32[:, i * 512:(i + 1) * 512] for i in range(7)]
        bankb = [Buf(f"bank{i}") for i in range(7)]
        pT = ps16
        pTb = Buf("pT16")

        id32, id32b = P32.alloc(128, "id32")
        idb, idbb = P16.alloc(128, "idb")
        T.dma("sp", id32, ident, writes=[id32b])
        T.op("dve", lambda e: e.tensor_copy(out=idb, in_=id32), reads=[id32b], writes=[idbb])

        def bcast_load(dst, dbuf, src1d, n):
            T.dma("sp", dst[:, :n], src1d.partition_broadcast(128), writes=[dbuf])

        stage_n = 2048

        def load_w(dst3, src2, stages, engs=("dve", "act")):
            rows, N = src2.shape
            KC = (rows + 127) // 128
            i = 0
            for k in range(KC):
                pr = min(128, rows - k * 128)
                for n0 in range(0, N, stage_n):
                    n1 = min(N, n0 + stage_n)
                    st, stb = stages[load_w.i % len(stages)]
                    eng = engs[load_w.i % len(engs)]
                    load_w.i += 1
                    T.dma("sp", st[:pr, :n1 - n0], src2[k * 128:k * 128 + pr, n0:n1], writes=[stb])
                    if eng == "dve":
                        T.op("dve", lambda e, o=dst3[:pr, k, n0:n1], i_=st[:pr, :n1 - n0]: e.tensor_copy(out=o, in_=i_),
                             reads=[stb], writes=[Buf()])
                    else:
                        T.op("act", lambda e, o=dst3[:pr, k, n0:n1], i_=st[:pr, :n1 - n0]: e.activation(out=o, in_=i_, func=AF.Copy),
                             reads=[stb], writes=[Buf()])
        load_w.i = 0

        def rstd(src, srcb, P, n, c):
            T.op("act", lambda e: e.activation(out=c.junk[:P, :src.shape[-1]] if len(src.shape) == 2 else c.junk[:P, :src.shape[-1]],
                                               in_=src, func=AF.Square, accum_out=c.ss[:P]),
                 reads=[srcb], writes=[c.junkb, c.ssb])
            T.op("act", lambda e: e.activation(out=c.sd[:P], in_=c.ss[:P], func=AF.Sqrt, scale=1.0 / n, bias=EPS),
                 reads=[c.ssb], writes=[c.sdb])
            T.op("dve", lambda e: e.reciprocal(out=c.r[:P], in_=c.sd[:P]), reads=[c.sdb], writes=[c.rb])

        def transposes(src16, srcb, P, nblk, width, dstT, dstTb):
            def f(e):
                ins = None
                for j in range(nblk):
                    ins = e.transpose(out=pT[:width, j * 128:j * 128 + P], in_=src16[:P, j * width:(j + 1) * width],
                                      identity=idb[:P, :P])
                return ins
            T.op("pe", f, reads=[srcb, idbb], writes=[pTb])
            pv = pT[:width, :nblk * 128].rearrange("f (j p) -> f j p", p=128)[:, :, :P]
            T.op("act", lambda e: e.activation(out=dstT[:width, :nblk, :P], in_=pv, func=AF.Copy),
                 reads=[pTb], writes=[dstTb])

        def mm_tokmajor(out_ps, outb, lhsT3, lhsTb, P, W3, kc, c0, c1):
            def f(e):
                ins = None
                for k in range(kc):
                    ins = e.matmul(out_ps[:P, :c1 - c0], lhsT=lhsT3[:, k, :P], rhs=W3[:, k, c0:c1],
                                   start=(k == 0), stop=(k == kc - 1))
                return ins
            T.op("pe", f, reads=[lhsTb], writes=[outb])

        def rope(src3, srcb, P, H, Dh, blocks, cs, csb, out3, outb, c):
            t1 = c.rt1[:P, :H * Dh].rearrange("p (h d) -> p h d", d=Dh)
            t2 = c.rt2[:P, :H * Dh].rearrange("p (h d) -> p h d", d=Dh)
            cosb = cs[:P, 0:Dh].unsqueeze(1).broadcast_to([P, H, Dh])
            T.op("dve", lambda e: e.tensor_tensor(out=t1, in0=src3, in1=cosb, op=ALU.mult),
                 reads=[srcb, csb], writes=[c.rt1b])
            hb = Dh // blocks // 2
            for b in range(blocks):
                lo = b * 2 * hb
                s_lo = cs[:P, Dh + lo:Dh + lo + hb].unsqueeze(1).broadcast_to([P, H, hb])
                s_hi = cs[:P, Dh + lo + hb:Dh + lo + 2 * hb].unsqueeze(1).broadcast_to([P, H, hb])
                T.op("dve", lambda e, lo=lo, s_lo=s_lo: e.tensor_tensor(out=t2[:, :, lo:lo + hb], in0=src3[:, :, lo + hb:lo + 2 * hb],
                                                                       in1=s_lo, op=ALU.mult),
                     reads=[srcb, csb], writes=[c.rt2b])
                T.op("dve", lambda e, lo=lo, s_hi=s_hi: e.tensor_tensor(out=t2[:, :, lo + hb:lo + 2 * hb], in0=src3[:, :, lo:lo + hb],
                                                                       in1=s_hi, op=ALU.mult),
                     reads=[srcb, csb], writes=[c.rt2b])
            T.op("dve", lambda e: e.tensor_tensor(out=out3, in0=t1, in1=t2, op=ALU.add),
                 reads=[c.rt1b, c.rt2b], writes=[outb])

        def phase1(l):
            P16.mark()
            P32.mark()
            Win_f, _ = P16.alloc(8 * 1440, "Win")
            Win = Win_f.rearrange("p (k n) -> p k n", n=1440)
            Wq_f, _ = P16.alloc(3 * 768, "Wq")
            Wq = Wq_f.rearrange("p (k n) -> p k n", n=768)
            Wkv_f, _ = P16.alloc(2 * 1024, "Wkv")
            Wkv = Wkv_f.rearrange("p (k n) -> p k n", n=1024)
            g_attn, gb1 = P32.alloc(1024)
            g_qa, gb2 = P32.alloc(384)
            g_kva, gb3 = P32.alloc(256)
            g_gq, gb4 = P32.alloc(64)
            g_gk, gb5 = P32.alloc(64)
            bcast_load(g_attn, gb1, g_attn_d[l], 1024)
            bcast_load(g_qa, gb2, g_qa_d[l], 384)
            bcast_load(g_kva, gb3, g_kva_d[l], 256)
            bcast_load(g_gq, gb4, g_gq_d[l], 64)
            bcast_load(g_gk, gb5, g_gk_d[l], 64)
            gbufs = [gb1, gb2, gb3, gb4, gb5]
            P32.mark()
            stages = [P32.alloc(stage_n) for _ in range(3)]
            load_w(Win, w_in_d[l], stages)
            load_w(Wq, w_qb_d[l], stages)
            load_w(Wkv, w_kvb_d[l], stages)
            T.barrier()
            P32.release()

            ctxs = []
            for i in range(2):
                c = Ctx()
                c.x, c.xb = P32.alloc(1024)
                c.csm, c.csmb = P32.alloc(64)
                c.csg, c.csgb = P32.alloc(128)
                c.ss, c.ssb = P32.alloc(1)
                c.sd, c.sdb = P32.alloc(1)
                c.r, c.rb = P32.alloc(1)
                c.ss10, c.ss10b = P32.alloc(10)
                c.sd10, c.sd10b = P32.alloc(10)
                c.r10, c.r10b = P32.alloc(10)
                c.rt1, c.rt1b = P32.alloc(640)
                c.rt2, c.rt2b = P32.alloc(512)
                c.gn, c.gnb = P32.alloc(640)
                c.q32, c.q32b = P32.alloc(768)
                c.sq10, c.sq10b = c.rt1, c.rt1b
                c.kr32, c.kr32b = P32.alloc(32)
                c.junk, c.junkb = P16.alloc(1024)
                c.hn, c.hnb = P16.alloc(1024)
                c.hnT, c.hnTb = P16.alloc(1024)
                c.cqn, c.cqnb = P16.alloc(384)
                c.cqnT, c.cqnTb = P16.alloc(384)
                c.ckvn, c.ckvnb = P16.alloc(256)
                c.ckvnT, c.ckvnTb = P16.alloc(256)
                c.q16, c.q16b = P16.alloc(768)
                c.qT, c.qTb = P16.alloc(1024)
                c.kn, c.knb = P16.alloc(512)
                c.knT, c.knTb = P16.alloc(512)
                c.vst, c.vstb = P16.alloc(650)
                T.op("dve", lambda e, c=c: e.memset(c.vst.rearrange("p (h d) -> p h d", d=65)[:, :, 64:65], 1.0), writes=[c.vstb])
                c.kpe, c.kpeb = P16.alloc(32)
                c.kpeT, c.kpeTb = P16.alloc(128)
                c.gq16, c.gq16b = P16.alloc(512)
                c.gqT, c.gqTb = P16.alloc(512)
                c.gk16, c.gk16b = P16.alloc(128)
                c.gkT, c.gkTb = P16.alloc(128)
                ctxs.append(c)

            tile_i = 0
            for s in range(2):
                for t in range(NT + 1):
                    c = ctxs[tile_i % 2]
                    tile_i += 1
                    is_meta = (t == NT)
                    P = NMETA if is_meta else 128
                    tok0 = t * 128
                    if l == 0:
                        src = meta[:, :] if is_meta else xq[s, tok0:tok0 + P, :]
                    else:
                        src = X1[s][tok0:tok0 + P, :]
                    T.dma("sp", c.x[:P], src, writes=[c.xb])
                    T.dma("sp", c.csm[:P], csm_d[s, tok0:tok0 + P, :], writes=[c.csmb])
                    T.dma("sp", c.csg[:P], csg_d[s, tok0:tok0 + P, :], writes=[c.csgb])
                    rstd(c.x[:P], c.xb, P, 1024, c)
                    T.op("dve", lambda e, c=c, P=P: e.scalar_tensor_tensor(out=c.hn[:P], in0=c.x[:P], scalar=c.r[:P], in1=g_attn[:P],
                                                                         op0=ALU.mult, op1=ALU.mult),
                         reads=[c.xb, c.rb] + gbufs, writes=[c.hnb])
                    hnT3 = c.hnT.rearrange("p (k t) -> p k t", t=128)
                    transposes(c.hn, c.hnb, P, 8, 128, hnT3, c.hnTb)
                    mm_tokmajor(bank[0], bankb[0], hnT3, c.hnTb, P, Win, 8, 0, 512)
                    mm_tokmajor(bank[2], bankb[2], hnT3, c.hnTb, P, Win, 8, 1024, 1440)
                    mm_tokmajor(bank[1], bankb[1], hnT3, c.hnTb, P, Win, 8, 512, 1024)
                    rstd(bank[0][:P, 0:384], bankb[0], P, 384, c)
                    T.op("dve", lambda e, c=c, P=P: e.scalar_tensor_tensor(out=c.cqn[:P], in0=bank[0][:P, 0:384], scalar=c.r[:P],
                                                                         in1=g_qa[:P], op0=ALU.mult, op1=ALU.mult),
                         reads=[bankb[0], c.rb] + gbufs, writes=[c.cqnb])
                    cqnT3 = c.cqnT.rearrange("p (k t) -> p k t", t=128)
                    transposes(c.cqn, c.cqnb, P, 3, 128, cqnT3, c.cqnTb)
                    rstd(bank[2][:P, 0:256], bankb[2], P, 256, c)
                    T.op("dve", lambda e, c=c, P=P: e.scalar_tensor_tensor(out=c.ckvn[:P], in0=bank[2][:P, 0:256], scalar=c.r[:P],
                                                                         in1=g_kva[:P], op0=ALU.mult, op1=ALU.mult),
                         reads=[bankb[2], c.rb] + gbufs, writes=[c.ckvnb])
                    ckvnT3 = c.ckvnT.rearrange("p (k t) -> p k t", t=128)
                    transposes(c.ckvn, c.ckvnb, P, 2, 128, ckvnT3, c.ckvnTb)
                    mm_tokmajor(bank[3], bankb[3], cqnT3, c.cqnTb, P, Wq, 3, 0, 384)
                    mm_tokmajor(bank[4], bankb[4], cqnT3, c.cqnTb, P, Wq, 3, 384, 768)
                    mm_tokmajor(bank[5], bankb[5], ckvnT3, c.ckvnTb, P, Wkv, 2, 0, 512)
                    mm_tokmajor(bank[6], bankb[6], ckvnT3, c.ckvnTb, P, Wkv, 2, 512, 1024)
                    q3 = c.q16.rearrange("p (h d) -> p h d", d=96)
                    q32 = c.q32.rearrange("p (h d) -> p h d", d=96)
                    for hb_, bk in ((0, 3), (1, 4)):
                        T.op("act", lambda e, bk=bk, hb_=hb_, P=P, c=c: e.activation(out=c.q32[:P, hb_ * 384:(hb_ + 1) * 384], in_=bank[bk][:P, 0:384],
                                                                                   func=AF.Copy),
                             reads=[bankb[bk]], writes=[c.q32b])
                    T.op("dve", lambda e, P=P, q3=q3, q32=q32: e.tensor_copy(out=q3[:P, :, 0:64], in_=q32[:P, :, 0:64]),
                         reads=[c.q32b], writes=[c.q16b])
                    rope(q32[:P, :, 64:96], c.q32b, P, 8, 32, 1, c.csm, c.csmb, q3[:P, :, 64:96], c.q16b, c)
                    def fq(e, c=c, P=P):
                        ins = None
                        for h in range(8):
                            ins = e.transpose(out=pT[:96, h * 128:h * 128 + P], in_=c.q16[:P, h * 96:(h + 1) * 96], identity=idb[:P, :P])
                        return ins
                    T.op("pe", fq, reads=[c.q16b, idbb], writes=[pTb])
                    qT3 = c.qT.rearrange("p (h t) -> p h t", t=128)
                    T.op("act", lambda e, P=P, qT3=qT3: e.activation(out=qT3[:96, :, :P],
                                                                   in_=pT[:96, :].rearrange("f (h t) -> f h t", t=128)[:, :, :P], func=AF.Copy),
                         reads=[pTb], writes=[c.qTb])
                    T.dma("pool", QTm[s].rearrange("(h d) t -> d h t", d=96)[:, :, tok0:tok0 + P], qT3[:96, :, :P],
                          reads=[c.qTb], writes=[Buf()])
                    v3 = c.vst.rearrange("p (h d) -> p h d", d=65)
                    for hb_, bk in ((0, 5), (1, 6)):
                        pkv = bank[bk][:P, :].rearrange("p (h d) -> p h d", d=128)
                        T.op("act", lambda e, pkv=pkv, hb_=hb_, P=P, v3=v3: e.activation(out=v3[:P, hb_ * 4:hb_ * 4 + 4, 0:64], in_=pkv[:, :, 64:128],
                                                                                       func=AF.Copy),
                             reads=[bankb[bk]], writes=[c.vstb])
                    T.op("act", lambda e, P=P, v3=v3: e.activation(out=v3[:P, 8:10, 0:64],
                                                                 in_=bank[0][:P, 384:512].rearrange("p (h d) -> p h d", d=64), func=AF.Copy),
                         reads=[bankb[0]], writes=[c.vstb])
                    if is_meta:
                        vdst = Vmeta[s].rearrange("(h t) d -> t h d", t=NMETA)
                    else:
                        vdst = Vloc[s].ap().rearrange("(h t) d -> t h d", t=N_OWN)[tok0:tok0 + P]
                    T.dma("pool", vdst, v3[:P], reads=[c.vstb], writes=[Buf()])
                    kn3 = c.kn.rearrange("p (h d) -> p h d", d=64)
                    for hb_, bk in ((0, 5), (1, 6)):
                        pkv = bank[bk][:P, :].rearrange("p (h d) -> p h d", d=128)
                        T.op("dve", lambda e, pkv=pkv, hb_=hb_, P=P, kn3=kn3: e.tensor_copy(out=kn3[:P, hb_ * 4:hb_ * 4 + 4, :], in_=pkv[:, :, 0:64]),
                             reads=[bankb[bk]], writes=[c.knb])
                    knT3 = c.knT.rearrange("p (j t) -> p j t", t=128)
                    transposes(c.kn, c.knb, P, 4, 128, knT3, c.knTb)
                    ktd = KTmeta[s] if is_meta else KTloc[s].ap()[:, tok0:tok0 + P]
                    T.dma("pool", ktd[0:512].rearrange("(j p) t -> p j t", p=128), knT3[:, :, :P], reads=[c.knTb], writes=[Buf()])
                    kpe3 = c.kpe[:, 0:32].rearrange("p (h d) -> p h d", d=32)
                    T.op("act", lambda e, c=c, P=P: e.activation(out=c.kr32[:P], in_=bank[2][:P, 256:288], func=AF.Copy),
                         reads=[bankb[2]], writes=[c.kr32b])
                    rope(c.kr32[:P].rearrange("p (h d) -> p h d", d=32), c.kr32b, P, 1, 32, 1, c.csm, c.csmb, kpe3[:P], c.kpeb, c)
                    kpeT3 = c.kpeT.rearrange("p (j t) -> p j t", t=128)
                    transposes(c.kpe, c.kpeb, P, 1, 32, kpeT3, c.kpeTb)
                    T.dma("pool", ktd[512:544], kpeT3[:32, 0, :P], reads=[c.kpeTb], writes=[Buf()])
                    gn3 = c.gn[:P, :].rearrange("p (h d) -> p h d", d=64)
                    T.op("act", lambda e, c=c, P=P: e.activation(out=c.gn[:P, 0:512], in_=bank[1][:P, :], func=AF.Copy),
                         reads=[bankb[1]], writes=[c.gnb])
                    T.op("act", lambda e, c=c, P=P: e.activation(out=c.gn[:P, 512:640], in_=bank[2][:P, 288:416], func=AF.Copy),
                         reads=[bankb[2]], writes=[c.gnb])
                    T.op("dve", lambda e, c=c, P=P: e.tensor_tensor(out=c.sq10[:P], in0=c.gn[:P], in1=c.gn[:P], op=ALU.mult),
                         reads=[c.gnb], writes=[c.sq10b])
                    T.op("dve", lambda e, c=c, P=P: e.tensor_reduce(out=c.ss10[:P], in_=c.sq10[:P].rearrange("p (h d) -> p h d", d=64),
                                                                  axis=AX.X, op=ALU.add),
                         reads=[c.sq10b], writes=[c.ss10b])
                    T.op("act", lambda e, c=c, P=P: e.activation(out=c.sd10[:P], in_=c.ss10[:P], func=AF.Sqrt, scale=1.0 / 64, bias=EPS),
                         reads=[c.ss10b], writes=[c.sd10b])
                    T.op("dve", lambda e, c=c, P=P: e.reciprocal(out=c.r10[:P], in_=c.sd10[:P]), reads=[c.sd10b], writes=[c.r10b])
                    T.op("dve", lambda e, c=c, P=P, gn3=gn3: e.tensor_tensor(
                        out=gn3, in0=gn3, in1=c.r10[:P, 0:10].unsqueeze(2).broadcast_to([P, 10, 64]), op=ALU.mult),
                        reads=[c.gnb, c.r10b], writes=[c.gnb])
                    T.op("dve", lambda e, P=P, gn3=gn3: e.tensor_tensor(
                        out=gn3[:, 0:8, :], in0=gn3[:, 0:8, :], in1=g_gq[:P, :].unsqueeze(1).broadcast_to([P, 8, 64]), op=ALU.mult),
                        reads=[c.gnb] + gbufs, writes=[c.gnb])
                    T.op("dve", lambda e, P=P, gn3=gn3: e.tensor_tensor(
                        out=gn3[:, 8:10, :], in0=gn3[:, 8:10, :], in1=g_gk[:P, :].unsqueeze(1).broadcast_to([P, 2, 64]), op=ALU.mult),
                        reads=[c.gnb] + gbufs, writes=[c.gnb])
                    gq16_3 = c.gq16.rearrange("p (h d) -> p h d", d=64)
                    gk16_3 = c.gk16.rearrange("p (h d) -> p h d", d=64)
                    rope(gn3[:, 0:8, :], c.gnb, P, 8, 64, 2, c.csg, c.csgb, gq16_3[:P], c.gq16b, c)
                    rope(gn3[:, 8:10, :], c.gnb, P, 2, 64, 2, c.csg, c.csgb, gk16_3[:P], c.gk16b, c)
                    gqT3 = c.gqT.rearrange("p (j t) -> p j t", t=128)
                    transposes(c.gq16, c.gq16b, P, 4, 128, gqT3, c.gqTb)
                    T.dma("pool", QTg[s].rearrange("(j p) t -> p j t", p=128)[:, :, tok0:tok0 + P], gqT3[:, :, :P],
                          reads=[c.gqTb], writes=[Buf()])
                    gkT3 = c.gkT.rearrange("p (j t) -> p j t", t=128)
                    transposes(c.gk16, c.gk16b, P, 1, 128, gkT3, c.gkTb)
                    T.dma("pool", ktd[544:672], gkT3[:, 0, :P], reads=[c.gkTb], writes=[Buf()])
            T.barrier()
            P16.release()
            P32.release()

        pieceb = {}

        def phase2():
            for s in range(2):
                R = RS[s]
                groups = [list(range(g * R, (g + 1) * R)) for g in range(8 // R)]
                kpc = [(("k", s, i), KTloc[s].ap()[a:a + n], KTall[s][i].ap()) for i, (a, n) in enumerate(KPIECES)]
                vpc = [(("v", s, h), Vloc[s].ap()[h * N_OWN:(h + 1) * N_OWN], Vall[s][h].ap()) for h in range(10)]
                order = [kpc[8]]
                for h in range(8):
                    order += [kpc[h], vpc[h]]
                order += [kpc[9], vpc[8], kpc[10], vpc[9]]
                for key, src, dst in order:
                    pieceb[key] = Buf()
                    T.custom("pool", lambda e, src=src, dst=dst, groups=groups: e.collective_compute(
                        "AllGather", ALU.bypass, replica_groups=groups, ins=[src], outs=[dst]), 1, writes=[pieceb[key]])

        def phase3(l):
            P16.mark()
            P32.mark()
            nq_eff = NQ if l == 0 else N_OWN
            ngrp = (nq_eff + 511) // 512
            units = nq_eff // 16
            base = units // ngrp
            widths = [16 * (base + (1 if i < units - base * ngrp else 0)) for i in range(ngrp)]
            assert sum(widths) == nq_eff and max(widths) <= 512
            g0s = [sum(widths[:i]) for i in range(ngrp)]
            LKMAX = 4 * N_OWN + NMETA
            NCHMAX = 4 * NC
            Kt = [(P16.alloc(LKMAX, f"K{i}")[0], [Buf() for _ in range(4)]) for i in range(2)]
            Vt = [P16.alloc(NCHMAX * 65, f"V{i}") for i in range(2)]
            Vm = [P16.alloc(65, f"Vm{i}") for i in range(2)]
            Qt = [(P16.alloc(NQ, f"Q{i}")[0], [Buf(), Buf()]) for i in range(2)]
            for i in range(2):
                T.op("dve", lambda e, i=i: e.memset(Kt[i][0][64:128, :], 0.0), writes=[Kt[i][1][1], Kt[i][1][3]])
            NPB = 3
            Pt = [P16.alloc(1024, f"P{i}") for i in range(NPB)]
            Osb = [P32.alloc(512, f"Osb{i}") for i in range(2)]
            aost = [P32.alloc(256, f"ao{i}") for i in range(2)]
            rd = [P32.alloc(4, f"rd{i}") for i in range(2)]
            Sb = [(ps32[:, b * 1024:(b + 1) * 1024], Buf(f"S{b}")) for b in range(2)]
            Ob = [(bank[4], bankb[4]), (bank[5], bankb[5])]
            OT, OTb = bank[6], bankb[6]

            kvsets = []
            for s in range(2):
                for h in range(8):
                    kvsets.append((s, "mla", h))
                for j in range(2):
                    kvsets.append((s, "gqa", j))

            def load_kv(idx):
                s, kind, h = kvsets[idx]
                R = RS[s]
                K, Kb = Kt[idx % 2]
                V, Vb = Vt[idx % 2]
                VM, VMb = Vm[idx % 2]
                def kp(i):
                    return KTall[s][i].ap().rearrange("(r f) t -> f r t", f=KPIECES[i][1])
                Kv = K[:, 0:R * N_OWN].rearrange("d (r t) -> d r t", t=N_OWN)
                mcol = slice(R * N_OWN, R * N_OWN + NMETA)
                if kind == "mla":
                    T.dma("sp", Kv[0:64], kp(h), reads=[pieceb[("k", s, h)]], writes=[Kb[0]])
                    T.dma("sp", Kv[64:96], kp(8), reads=[pieceb[("k", s, 8)]], writes=[Kb[1]])
                    T.dma("sp", K[0:64, mcol], KTmeta[s][h * 64:(h + 1) * 64, :], writes=[Kb[2]])
                    T.dma("sp", K[64:96, mcol], KTmeta[s][512:544, :], writes=[Kb[3]])
                    hv = h
                else:
                    T.dma("sp", Kv[0:64], kp(9 + h), reads=[pieceb[("k", s, 9 + h)]], writes=[Kb[0]])
                    T.dma("sp", K[0:64, mcol], KTmeta[s][544 + h * 64:544 + (h + 1) * 64, :], writes=[Kb[2]])
                    hv = 8 + h
                vall = Vall[s][hv].ap().rearrange("(r p c) d -> p r c d", p=128, c=NC)
                V4 = V[:, 0:R * NC * 65].rearrange("p (r c d) -> p r c d", c=NC, d=65)
                T.dma("sp", V4, vall, reads=[pieceb[("v", s, hv)]], writes=[Vb])
                T.dma("sp", VM[0:NMETA, 0:65], Vmeta[s][hv * NMETA:(hv + 1) * NMETA, :], writes=[VMb])

            def qheads(idx):
                s, kind, h = kvsets[idx]
                if kind == "mla":
                    return [(s, "mla", h, h)]
                return [(s, "gqa", h * 4 + g, 8 + h * 4 + g) for g in range(4)]

            qlist = []
            for idx in range(len(kvsets)):
                for qh in qheads(idx):
                    qlist.append((idx, qh))

            def load_q(qi):
                idx, (s, kind, qh, _) = qlist[qi]
                Q, Qb = Qt[qi % 2]
                if kind == "mla":
                    T.dma("sp", Q[0:96, 0:nq_eff], QTm[s][qh * 96:(qh + 1) * 96, 0:nq_eff], writes=[Qb[0], Qb[1]])
                else:
                    T.dma("sp", Q[0:64, 0:nq_eff], QTg[s][qh * 64:(qh + 1) * 64, 0:nq_eff], writes=[Qb[0]])
                    T.op("dve", lambda e, Q=Q: e.memset(Q[64:128, 0:nq_eff], 0.0), writes=[Qb[1]])

            steps = []
            for qi, (idx, (s, kind, qh, cb)) in enumerate(qlist):
                R = RS[s]
                nch = R * NC + 1
                for gi in range(ngrp):
                    for c0 in range(0, R * NC, 2):
                        steps.append((qi, gi, [c0, c0 + 1], nch))
                    steps.append((qi, gi, [R * NC], nch))
            LA = 2
            gcount = [0]
            pending = []

            def kcols(idx, cix):
                s, kind, h = kvsets[idx]
                R = RS[s]
                d = 96 if kind == "mla" else 128
                K, Kb = Kt[idx % 2]
                if cix == R * NC:
                    return K[0:d, R * N_OWN:R * N_OWN + NMETA], NMETA, d
                r, cl = divmod(cix, NC)
                return K[0:d, r * N_OWN:(r + 1) * N_OWN].rearrange("d (p c) -> d p c", c=NC)[:, :, cl], 128, d

            def emit_qk(i):
                qi, gi, chunks, nch = steps[i]
                idx, (s, kind, qh, cb) = qlist[qi]
                Q, Qb = Qt[qi % 2]
                G = widths[gi]
                g0 = g0s[gi]
                S, Sbuf = Sb[i % 2]

                def f(e):
                    ins = None
                    for k, cix in enumerate(chunks):
                        lhsT, kc, d = kcols(idx, cix)
                        ins = e.matmul(S[:kc, k * 512:k * 512 + G], lhsT=lhsT, rhs=Q[0:d, g0:g0 + G], start=True, stop=True)
                    return ins
                T.op("pe", f, reads=Kt[idx % 2][1] + Qb, writes=[Sbuf])

            def emit_exp_pv(i):
                qi, gi, chunks, nch = steps[i]
                idx, (s, kind, qh, cb) = qlist[qi]
                R = RS[s]
                scale = (96.0 if kind == "mla" else 64.0) ** -0.5
                n = len(chunks)
                kc = NMETA if chunks[0] == R * NC else 128
                G = widths[gi]
                S, Sbuf = Sb[i % 2]
                Pp, Pb = Pt[i % NPB]
                gidx = gcount[0]
                O, Obuf = Ob[gidx % 2]
                S3 = S.rearrange("p (k g) -> p k g", g=512)[:kc, 0:n, 0:G]
                P3 = Pp.rearrange("p (k g) -> p k g", g=512)[:kc, 0:n, 0:G]
                T.op("act", lambda e, S3=S3, P3=P3, scale=scale: e.activation(out=P3, in_=S3, func=AF.Exp, scale=scale),
                     reads=[Sbuf], writes=[Pb])
                j = i + LA
                if j < len(steps):
                    ensure_loaded(j)
                    emit_qk(j)
                if kc == 128:
                    V, Vb = Vt[idx % 2]
                    V3 = V.rearrange("p (c d) -> p c d", d=65)
                    lhs = [V3[:, cix, :] for cix in chunks]
                else:
                    V, Vb = Vm[idx % 2]
                    lhs = [V[0:NMETA, 0:65]]

                def fpv(e):
                    ins = None
                    for k, cix in enumerate(chunks):
                        ins = e.matmul(O[0:65, :G], lhsT=lhs[k], rhs=Pp[:kc, k * 512:k * 512 + G],
                                       start=(cix == 0), stop=(cix == nch - 1))
                    return ins
                T.op("pe", fpv, reads=[Vb, Pb], writes=[Obuf])
                cix = chunks[-1]
                if cix == nch - 1:
                    gcount[0] += 1
                    Os, Osbuf = Osb[gidx % 2]
                    T.op("dve", lambda e, Os=Os, O=O, G=G: e.tensor_copy(out=Os[0:65, :G], in_=O[0:65, :G]),
                         reads=[Obuf], writes=[Osbuf])
                    def epi(Os=Os, Osbuf=Osbuf, O=O, G=G, gidx=gidx, gi=gi, s=s, cb=cb):
                        nsub = (G + 127) // 128
                        OT3 = OT[:, 0:4 * 65].rearrange("p (j d) -> p j d", d=65)

                        def ftr(e, Os=Os, G=G, nsub=nsub):
                            ins = None
                            for j in range(nsub):
                                w = min(128, G - j * 128)
                                ins = e.transpose(out=OT3[:w, j, :], in_=Os[0:65, j * 128:j * 128 + w], identity=id32[0:65, 0:65])
                            return ins
                        T.op("pe", ftr, reads=[Osbuf, id32b], writes=[OTb])
                        rdt, rdb = rd[gidx % 2]
                        ao, aob = aost[gidx % 2]
                        ao3 = ao.rearrange("p (j d) -> p j d", d=64)
                        g0 = g0s[gi]
                        full = G // 128
                        rem = G - full * 128
                        parts = []
                        if full:
                            parts.append((128, 0, full))
                        if rem:
                            parts.append((rem, full, full + 1))
                        for (pw, j0, j1) in parts:
                            T.op("dve", lambda e, pw=pw, j0=j0, j1=j1, rdt=rdt: e.reciprocal(out=rdt[:pw, j0:j1], in_=OT3[:pw, j0:j1, 64]),
                                 reads=[OTb], writes=[rdb])
                            for jj in range(j0, j1):
                                T.op("dve", lambda e, pw=pw, jj=jj, rdt=rdt, ao3=ao3: e.tensor_scalar(
                                    out=ao3[:pw, jj, :], in0=OT3[:pw, jj, 0:64], scalar1=rdt[:pw, jj:jj + 1], scalar2=None, op0=ALU.mult),
                                    reads=[OTb, rdb], writes=[aob])
                            if j1 - j0 > 1 or True:
                                dst = AO[s][g0 + j0 * 128:g0 + j0 * 128 + (j1 - j0 - 1) * 128 + pw, cb * 64:(cb + 1) * 64]
                                if j1 - j0 == 1:
                                    T.dma("sp", dst, ao3[:pw, j0, :], reads=[aob], writes=[Buf()])
                                else:
                                    T.dma("sp", dst.rearrange("(j p) d -> p j d", p=128), ao3[:pw, j0:j1, :], reads=[aob], writes=[Buf()])
                    pending.append((i + 2, epi))

            load_kv(0)
            load_q(0)
            started_kv = {0}
            started_q = {0}
            n = len(steps)
            def ensure_loaded(j):
                qi = steps[j][0]
                if qi not in started_q:
                    load_q(qi)
                    started_q.add(qi)
                idx = qlist[qi][0]
                if idx not in started_kv:
                    load_kv(idx)
                    started_kv.add(idx)

            for j in range(min(LA, n)):
                ensure_loaded(j)
                emit_qk(j)
            for i in range(n):
                if i >= 0:
                    emit_exp_pv(i)
                    while pending and pending[0][0] <= i:
                        pending.pop(0)[1]()
                    qi = steps[i][0]
                    if steps[i][1] == 0 and steps[i][2][0] == 0:
                        if qi + 1 < len(qlist) and (qi + 1) not in started_q:
                            load_q(qi + 1)
                            started_q.add(qi + 1)
                            nidx = qlist[qi + 1][0]
                            if nidx not in started_kv:
                                load_kv(nidx)
                                started_kv.add(nidx)
            while pending:
                pending.pop(0)[1]()
            T.barrier()
            P16.release()
            P32.release()

        def phase4(l):
            last = (l == depth - 1)
            P16.mark()
            P32.mark()
            Wo_f, _ = P16.alloc(8 * 1024, "Wo")
            Wo = Wo_f.rearrange("p (k n) -> p k n", n=1024)
            Wu_f, _ = P16.alloc(8 * 4096, "Wu")
            Wu = Wu_f.rearrange("p (k n) -> p k n", n=4096)
            Wd_f, _ = P16.alloc(32 * 1024, "Wd")
            Wd = Wd_f.rearrange("p (k n) -> p k n", n=1024)
            g_out, gb1 = P32.alloc(1024)
            g_mlp, gb2 = P32.alloc(1024)
            bcast_load(g_out, gb1, g_out_d[l], 1024)
            bcast_load(g_mlp, gb2, g_mlp_d[l], 1024)
            gbufs = [gb1, gb2]
            if last:
                g_fin, gb3 = P32.alloc(1024)
                bcast_load(g_fin, gb3, g_fin_d, 1024)
                gbufs.append(gb3)
            P32.mark()
            stages = [P32.alloc(stage_n) for _ in range(3)]
            load_w(Wo, w_out_d[l], stages)
            load_w(Wu, w_up_d[l], stages)
            load_w(Wd, w_down_d[l], stages)
            T.barrier()
            P32.release()

            ctxs = []
            for i in range(2):
                c = Ctx()
                c.x, c.xb = P32.alloc(1024)
                c.ao, c.aob = P32.alloc(1024)
                c.ss, c.ssb = P32.alloc(1)
                c.sd, c.sdb = P32.alloc(1)
                c.r, c.rb = P32.alloc(1)
                c.ss2, c.ss2b = P32.alloc(2)
                c.sd2, c.sd2b = P32.alloc(2)
                c.r2, c.r2b = P32.alloc(2)
                if i == 0:
                    c.junk, c.junkb = P16.alloc(1024)
                    c.mix, c.mixb = P16.alloc(1024)
                    c.mixT, c.mixTb = P16.alloc(1024)
                    c.hn, c.hnb = P16.alloc(1024)
                    c.hnT, c.hnTb = P16.alloc(1024)
                else:
                    for nm in ("junk", "mix", "mixT", "hn", "hnT"):
                        setattr(c, nm, getattr(ctxs[0], nm))
                        setattr(c, nm + "b", getattr(ctxs[0], nm + "b"))
                ctxs.append(c)
            xmid, xmidb = P32.alloc(1024)
            xnew, xnewb = P32.alloc(1024)
            r32 = [P32.alloc(512) for _ in range(2)]
            aT = [P16.alloc(512) for _ in range(2)]
            pso = [(bank[0], bankb[0]), (bank[1], bankb[1])]
            pu = [(bank[2], bankb[2]), (bank[3], bankb[3])]
            py = [(bank[4], bankb[4]), (bank[5], bankb[5])]

            tile_i = 0
            for s in range(2):
                for t in range(NT + (0 if last else 1)):
                    c = ctxs[tile_i % 2]
                    tile_i += 1
                    is_meta = (t == NT)
                    P = NMETA if is_meta else 128
                    tok0 = t * 128
                    if l == 0:
                        src = meta[:, :] if is_meta else xq[s, tok0:tok0 + P, :]
                    else:
                        src = X1[s][tok0:tok0 + P, :]
                    T.dma("sp", c.x[:P], src, writes=[c.xb])
                    T.dma("sp", c.ao[:P], AO[s][tok0:tok0 + P, :], writes=[c.aob])
                    for hf in range(2):
                        T.op("act", lambda e, c=c, P=P, hf=hf: e.activation(out=c.junk[:P, 0:512], in_=c.ao[:P, hf * 512:(hf + 1) * 512],
                                                                          func=AF.Square, accum_out=c.ss2[:P, hf:hf + 1]),
                             reads=[c.aob], writes=[c.junkb, c.ss2b])
                    T.op("act", lambda e, c=c, P=P: e.activation(out=c.sd2[:P], in_=c.ss2[:P], func=AF.Sqrt, scale=1.0 / 512, bias=EPS),
                         reads=[c.ss2b], writes=[c.sd2b])
                    T.op("dve", lambda e, c=c, P=P: e.reciprocal(out=c.r2[:P], in_=c.sd2[:P]), reads=[c.sd2b], writes=[c.r2b])
                    for hf in range(2):
                        sl = slice(hf * 512, (hf + 1) * 512)
                        T.op("dve", lambda e, c=c, P=P, hf=hf, sl=sl: e.scalar_tensor_tensor(
                            out=c.mix[:P, sl], in0=c.ao[:P, sl], scalar=c.r2[:P, hf:hf + 1], in1=g_out[:P, sl], op0=ALU.mult, op1=ALU.mult),
                            reads=[c.aob, c.r2b] + gbufs, writes=[c.mixb])
                    mixT3 = c.mixT.rearrange("p (k t) -> p k t", t=128)
                    transposes(c.mix, c.mixb, P, 8, 128, mixT3, c.mixTb)
                    for j in range(2):
                        mm_tokmajor(pso[j][0], pso[j][1], mixT3, c.mixTb, P, Wo, 8, j * 512, (j + 1) * 512)
                    for j in range(2):
                        sl = slice(j * 512, (j + 1) * 512)
                        T.op("dve", lambda e, c=c, P=P, j=j, sl=sl: e.tensor_tensor(out=xmid[:P, sl], in0=pso[j][0][:P, :], in1=c.x[:P, sl], op=ALU.add),
                             reads=[pso[j][1], c.xb], writes=[xmidb])
                    rstd(xmid[:P], xmidb, P, 1024, c)
                    T.op("dve", lambda e, c=c, P=P: e.scalar_tensor_tensor(out=c.hn[:P], in0=xmid[:P], scalar=c.r[:P], in1=g_mlp[:P],
                                                                         op0=ALU.mult, op1=ALU.mult),
                         reads=[xmidb, c.rb] + gbufs, writes=[c.hnb])
                    hnT3 = c.hnT.rearrange("p (k t) -> p k t", t=128)
                    transposes(c.hn, c.hnb, P, 8, 128, hnT3, c.hnTb)

                    def up(fb, c=c, P=P, hnT3=hnT3):
                        U, Ub = pu[fb % 2]

                        def f(e):
                            ins = None
                            for q in range(4):
                                fidx = fb * 4 + q
                                for k in range(8):
                                    ins = e.matmul(U[:, q * 128:q * 128 + P], lhsT=Wu[:, k, fidx * 128:(fidx + 1) * 128], rhs=hnT3[:, k, :P],
                                                   start=(k == 0), stop=(k == 7))
                            return ins
                        T.op("pe", f, reads=[c.hnTb], writes=[Ub])
                        U3 = U.rearrange("p (q t) -> p q t", t=128)[:, :, :P]
                        R3 = r32[fb % 2][0].rearrange("p (q t) -> p q t", t=128)[:, :, :P]
                        A3 = aT[fb % 2][0].rearrange("p (q t) -> p q t", t=128)[:, :, :P]
                        T.op("act", lambda e: e.activation(out=R3, in_=U3, func=AF.Relu), reads=[Ub], writes=[r32[fb % 2][1]])
                        T.op("dve", lambda e: e.tensor_tensor(out=A3, in0=R3, in1=R3, op=ALU.mult), reads=[r32[fb % 2][1]], writes=[aT[fb % 2][1]])

                    def down(fb, P=P):
                        A3 = aT[fb % 2][0].rearrange("p (q t) -> p q t", t=128)

                        def f(e):
                            ins = None
                            for q in range(4):
                                fidx = fb * 4 + q
                                for j in range(2):
                                    ins = e.matmul(py[j][0][:P, :], lhsT=A3[:, q, :P], rhs=Wd[:, fidx, j * 512:(j + 1) * 512],
                                                   start=(fidx == 0), stop=(fidx == 31))
                            return ins
                        T.op("pe", f, reads=[aT[fb % 2][1]], writes=[py[0][1], py[1][1]])

                    for fb in range(8):
                        up(fb)
                        if fb >= 1:
                            down(fb - 1)
                    down(7)
                    for j in range(2):
                        sl = slice(j * 512, (j + 1) * 512)
                        T.op("dve", lambda e, P=P, j=j, sl=sl: e.tensor_tensor(out=xnew[:P, sl], in0=py[j][0][:P, :], in1=xmid[:P, sl], op=ALU.add),
                             reads=[py[j][1], xmidb], writes=[xnewb])
                    if not last:
                        T.dma("pool", X1[s][tok0:tok0 + P, :], xnew[:P], reads=[xnewb], writes=[Buf()])
                    else:
                        rstd(xnew[:P], xnewb, P, 1024, c)
                        T.op("dve", lambda e, c=c, P=P: e.scalar_tensor_tensor(out=c.ao[:P], in0=xnew[:P], scalar=c.r[:P], in1=g_fin[:P],
                                                                             op0=ALU.mult, op1=ALU.mult),
                             reads=[xnewb, c.rb] + gbufs, writes=[c.aob])
                        T.dma("pool", y_d[s, tok0:tok0 + P, :], c.ao[:P], reads=[c.aob], writes=[Buf()])
            T.barrier()
            P16.release()
            P32.release()

        plist = []
        for l in range(depth):
            plist += [lambda l=l: phase1(l), phase2, lambda l=l: phase3(l), lambda l=l: phase4(l)]
        for ph in plist[:nphase]:
            ph()
        if debug:
            dbg = {}
            for s in range(2):
                for nm, t in (("KTloc", KTloc[s]), ("Vloc", Vloc[s])):
                    o = nc.dram_tensor(f"dbg_{nm}{s}", list(t.ap().shape), BF16, kind="ExternalOutput")
                    T.dma("pool", o.ap(), t.ap())
            T.barrier()

        @block.sync
        def _(e):
            T.replay("sp", e)

        @block.tensor
        def _(e):
            T.replay("pe", e)

        @block.scalar
        def _(e):
            T.replay("act", e)

        @block.vector
        def _(e):
            T.replay("dve", e)

        @block.gpsimd
        def _(e):
            T.replay("pool", e)
    return nc


def _inv_freq(dim):
    return (np.float32(1.0) / np.power(np.float32(10000.0), np.arange(0, dim, 2, dtype=np.float32) / np.float32(dim))).astype(np.float32)


def _tables(pos, rows, cols):
    def cs(p, f):
        ang = (p[:, None].astype(np.float32) * f[None, :].astype(np.float32)).astype(np.float32)
        a = np.concatenate([ang, ang], axis=-1).astype(np.float64)
        c, s = np.cos(a), np.sin(a)
        h = ang.shape[1]
        sp = np.concatenate([-s[:, :h], s[:, h:]], axis=-1)
        return c.astype(np.float32), sp.astype(np.float32)
    cm, sm = cs(pos, _inv_freq(32))
    cr, sr = cs(rows, _inv_freq(32))
    cc, sc = cs(cols, _inv_freq(32))
    csm = np.concatenate([cm, sm], axis=-1)
    csg = np.concatenate([cr, cc, sr, sc], axis=-1)
    return np.ascontiguousarray(csm, np.float32), np.ascontiguousarray(csg, np.float32)


_PERM = np.concatenate([np.arange(0, 384), np.arange(1312, 1440), np.arange(672, 1184),
                        np.arange(384, 640), np.arange(640, 672), np.arange(1184, 1312)])

_NC_CACHE = {}


def run_model(x_prompt, x_sample, meta_tokens, attn_norm_g, w_in, q_a_norm_g, w_q_b, kv_a_norm_g, w_kv_b,
              gqa_q_norm_g, gqa_k_norm_g, mla_out_norm_g, gqa_out_norm_g, w_out, mlp_norm_g, w_up, w_down,
              final_norm_g, trace=False, nphase=None, debug=False):
    f = lambda a: np.ascontiguousarray(np.asarray(a), dtype=np.float32)
    x_prompt, x_sample = f(x_prompt), f(x_sample)
    B, n_long, _ = x_prompt.shape
    Bs, n_short, _ = x_sample.shape
    assert B == 2 and Bs == 4 and n_long == 2 * n_short
    N_OWN = n_long // 4
    depth = np.asarray(w_in).shape[0]
    shared = {
        "meta": f(meta_tokens), "ident": np.eye(128, dtype=np.float32),
        "w_in": np.ascontiguousarray(f(w_in)[:, :, _PERM]), "w_q_b": f(w_q_b), "w_kv_b": f(w_kv_b), "w_out": f(w_out),
        "w_up": f(w_up), "w_down": f(w_down), "attn_norm_g": f(attn_norm_g), "q_a_norm_g": f(q_a_norm_g),
        "kv_a_norm_g": f(kv_a_norm_g), "gqa_q_norm_g": f(gqa_q_norm_g), "gqa_k_norm_g": f(gqa_k_norm_g),
        "out_norm_g": np.ascontiguousarray(np.concatenate([f(mla_out_norm_g), f(gqa_out_norm_g)], axis=-1)),
        "mlp_norm_g": f(mlp_norm_g), "final_norm_g": f(final_norm_g),
    }
    in_maps = []
    for c in range(8):
        bl, rl = c // 4, c % 4
        bs, rs = c // 2, c % 2
        xq = np.stack([x_prompt[bl, rl * N_OWN:(rl + 1) * N_OWN], x_sample[bs, rs * N_OWN:(rs + 1) * N_OWN]])
        csm, csg = [], []
        for r in (rl, rs):
            t = np.arange(r * N_OWN, (r + 1) * N_OWN, dtype=np.float32)
            pos = np.concatenate([t + np.float32(NMETA), np.arange(NMETA, dtype=np.float32)])
            rows = np.concatenate([np.floor(t / 64.0), np.zeros(NMETA)]).astype(np.float32)
            cols = np.concatenate([np.mod(t, 64.0), np.zeros(NMETA)]).astype(np.float32)
            a, b = _tables(pos, rows, cols)
            csm.append(a)
            csg.append(b)
        m = dict(shared)
        m["xq"] = np.ascontiguousarray(xq)
        m["csm"] = np.stack(csm)
        m["csg"] = np.stack(csg)
        import os as _os3
        if _os3.environ.get("K_PAD"):
            m["pad"] = np.zeros((int(_os3.environ["K_PAD"]), 1024), np.float32)
        in_maps.append(m)
    key = (N_OWN, depth, nphase, debug)
    if key not in _NC_CACHE:
        _NC_CACHE[key] = build(N_OWN, depth, nphase=nphase, debug=debug)
    nc = _NC_CACHE[key]
    res = run_bass_kernel_spmd(nc, in_maps, core_ids=list(range(8)), trace=trace)
    y_prompt = np.empty_like(x_prompt)
    y_sample = np.empty_like(x_sample)
    for c in range(8):
        y = np.asarray(res.results[c]["y"], dtype=np.float32)
        y_prompt[c // 4, (c % 4) * N_OWN:(c % 4 + 1) * N_OWN] = y[0]
        y_sample[c // 2, (c % 2) * N_OWN:(c % 2 + 1) * N_OWN] = y[1]
    return (y_prompt, y_sample), res


def kernel(**inputs):
    out, _ = run_model(**inputs)
    return out
```

```python
import contextlib
import numpy as np
import ml_dtypes
import concourse.bass as bass
import concourse.mybir as mybir
from concourse.bass_utils import run_bass_kernel_spmd

F32 = mybir.dt.float32
BF16 = mybir.dt.bfloat16
AF = mybir.ActivationFunctionType
ALU = mybir.AluOpType
AX = mybir.AxisListType

ENGS = ("pe", "act", "dve", "pool", "sp")
D = 1024
EPS = 1e-6
NMETA = 16


class Tok:
    __slots__ = ("eng", "sem", "val", "dma")

    def __init__(self, eng, sem, val, dma):
        self.eng, self.sem, self.val, self.dma = eng, sem, val, dma


class Buf:
    __slots__ = ("name", "w", "r")

    def __init__(self, name=""):
        self.name = name
        self.w = None
        self.r = {}


class Tracker:
    def __init__(self, sems, rings):
        self.sem = sems
        self.rings = rings
        self.streams = {e: [] for e in ENGS}
        self.cnt = {e: 0 for e in ENGS}
        self.waited = {e: {} for e in ENGS}
        self.ring_idx = {q: 0 for q in rings}
        self.ring_val = {}
        self.ring_tok = {}

    def _need(self, eng, tok, waits):
        if tok is None:
            return
        if (not tok.dma) and tok.eng == eng and eng == "pe":
            return
        w = self.waited[eng]
        key = id(tok.sem)
        if w.get(key, 0) >= tok.val:
            return
        w[key] = tok.val
        waits.append((tok.sem, tok.val))

    def _deps(self, eng, reads, writes):
        waits = []
        for b in reads:
            self._need(eng, b.w, waits)
        for b in writes:
            self._need(eng, b.w, waits)
            for t in b.r.values():
                self._need(eng, t, waits)
        return waits

    def _commit(self, tok, reads, writes):
        k = id(tok.sem)
        for b in reads:
            o = b.r.get(k)
            if o is None or o.val < tok.val:
                b.r[k] = tok
        for b in writes:
            b.w = tok
            b.r = {}

    def _skip(self):
        self.nrec = getattr(self, "nrec", 0) + 1
        return self.nrec > getattr(self, "maxops", 1 << 60)

    def op(self, eng, fn, reads=(), writes=()):
        if self._skip():
            return None
        waits = self._deps(eng, reads, writes)
        self.cnt[eng] += 1
        tok = Tok(eng, self.sem[eng], self.cnt[eng], False)
        self.streams[eng].append((waits, fn, self.sem[eng], 1))
        self._commit(tok, reads, writes)
        return tok

    def _ring(self, q, fn, inc, reads, writes):
        if self._skip():
            return None
        ring = self.rings[q]
        sem = ring[self.ring_idx[q] % len(ring)]
        self.ring_idx[q] += 1
        waits = self._deps(q, reads, writes)
        prev = self.ring_tok.get(id(sem))
        if prev is not None:
            self._need(q, prev, waits)
        val = self.ring_val.get(id(sem), 0) + inc
        self.ring_val[id(sem)] = val
        tok = Tok(q, sem, val, True)
        self.ring_tok[id(sem)] = tok
        self.streams[q].append((waits, fn, sem, inc))
        self._commit(tok, reads, writes)
        return tok

    def dma(self, q, out, in_, reads=(), writes=()):
        return self._ring(q, lambda e, out=out, in_=in_: e.dma_start(out=out, in_=in_), 16, reads, writes)

    def custom(self, q, fn, inc, reads=(), writes=(), ring="cc"):
        if self._skip():
            return None
        rg = self.rings[ring]
        sem = rg[self.ring_idx[ring] % len(rg)]
        self.ring_idx[ring] += 1
        waits = self._deps(q, reads, writes)
        prev = self.ring_tok.get(id(sem))
        if prev is not None:
            self._need(q, prev, waits)
        val = self.ring_val.get(id(sem), 0) + inc
        self.ring_val[id(sem)] = val
        tok = Tok(q, sem, val, True)
        self.ring_tok[id(sem)] = tok
        self.streams[q].append((waits, fn, sem, inc))
        self._commit(tok, reads, writes)
        return tok

    def barrier(self):
        toks = []
        for e in ENGS:
            if self.cnt[e] > 0:
                toks.append(Tok(e, self.sem[e], self.cnt[e], False))
        toks.extend(self.ring_tok.values())
        for e in ENGS:
            waits = []
            for t in toks:
                if (not t.dma) and t.eng == e:
                    continue
                self._need(e, t, waits)
            if waits:
                self.streams[e].append((waits, None, None, 0))

    def replay(self, eng, e):
        for waits, fn, sem, inc in self.streams[eng]:
            for s, v in waits:
                e.wait_ge(s, v)
            if fn is not None:
                fn(e).then_inc(sem, inc)


class Pool:
    def __init__(self, tensor, size):
        self.t, self.size, self.off, self.marks = tensor, size, 0, []

    def alloc(self, n, name=""):
        a = self.off
        self.off += n
        assert self.off <= self.size, (name, self.off, self.size)
        return self.t[:, a:a + n], Buf(name)

    def mark(self):
        self.marks.append(self.off)

    def release(self):
        self.off = self.marks.pop()


class Ctx:
    pass


def build(N_OWN, depth=2, N16=80100, N32=10500, nphase=None, debug=False):
    NT = N_OWN // 128
    NC = NT
    NQ = N_OWN + NMETA
    RS = (4, 2)
    nc = bass.Bass("TRN2", target_bir_lowering=False)

    def din(name, shape, dt=F32):
        return nc.dram_tensor(name, list(shape), dt, kind="ExternalInput").ap()

    xq = din("xq", [2, N_OWN, D])
    meta = din("meta", [NMETA, D])
    ident = din("ident", [128, 128])
    csm_d = din("csm", [2, NQ, 64])
    csg_d = din("csg", [2, NQ, 128])
    w_in_d = din("w_in", [depth, D, 1440])
    w_qb_d = din("w_q_b", [depth, 384, 768])
    w_kvb_d = din("w_kv_b", [depth, 256, 1024])
    w_out_d = din("w_out", [depth, D, D])
    w_up_d = din("w_up", [depth, D, 4096])
    w_down_d = din("w_down", [depth, 4096, D])
    g_attn_d = din("attn_norm_g", [depth, D])
    g_qa_d = din("q_a_norm_g", [depth, 384])
    g_kva_d = din("kv_a_norm_g", [depth, 256])
    g_gq_d = din("gqa_q_norm_g", [depth, 64])
    g_gk_d = din("gqa_k_norm_g", [depth, 64])
    g_out_d = din("out_norm_g", [depth, D])
    g_mlp_d = din("mlp_norm_g", [depth, D])
    g_fin_d = din("final_norm_g", [D])
    y_d = nc.dram_tensor("y", [2, N_OWN, D], F32, kind="ExternalOutput").ap()
    import os as _os2
    if _os2.environ.get("K_PAD"):
        din("pad", [int(_os2.environ["K_PAD"]), 1024])

    dk = dict(kind="ExternalOutput") if debug else {}
    QTm = [nc.dram_tensor(f"QTm{s}", [8 * 96, NQ], BF16, **dk).ap() for s in range(2)]
    QTg = [nc.dram_tensor(f"QTg{s}", [8 * 64, NQ], BF16, **dk).ap() for s in range(2)]
    KTloc = [nc.dram_tensor(f"KTloc{s}", [672, N_OWN], BF16) for s in range(2)]
    Vloc = [nc.dram_tensor(f"Vloc{s}", [10 * N_OWN, 65], BF16) for s in range(2)]
    KPIECES = [(h * 64, 64) for h in range(8)] + [(512, 32), (544, 64), (608, 64)]
    KTall = [[nc.dram_tensor(f"KTall{s}_{i}", [RS[s] * n, N_OWN], BF16) for i, (a, n) in enumerate(KPIECES)] for s in range(2)]
    Vall = [[nc.dram_tensor(f"Vall{s}_{h}", [RS[s] * N_OWN, 65], BF16) for h in range(10)] for s in range(2)]
    KTmeta = [nc.dram_tensor(f"KTmeta{s}", [672, NMETA], BF16, **dk).ap() for s in range(2)]
    Vmeta = [nc.dram_tensor(f"Vmeta{s}", [10 * NMETA, 65], BF16, **dk).ap() for s in range(2)]
    AO = [nc.dram_tensor(f"AO{s}", [NQ, D], F32, **dk).ap() for s in range(2)]
    X1 = [nc.dram_tensor(f"X1{s}", [NQ, D], F32, **dk).ap() for s in range(2)]

    es = contextlib.ExitStack()
    with es:
        sb32 = es.enter_context(nc.sbuf_tensor("sb32", [128, N32], F32))
        sb16 = es.enter_context(nc.sbuf_tensor("sb16", [128, N16], BF16))
        ps32 = es.enter_context(nc.psum_tensor("ps32", [128, 7 * 512], F32))
        ps16 = es.enter_context(nc.psum_tensor("ps16", [128, 1024], BF16))
        sems = {e: es.enter_context(nc.semaphore("s_" + e)) for e in ENGS}
        rings = {q: [es.enter_context(nc.semaphore(f"r_{q}{i}")) for i in range(8 if q != "cc" else 4)] for q in ("sp", "pool", "cc")}
        block = es.enter_context(nc.Block())
        T = Tracker(sems, rings)
        import os as _os
        if _os.environ.get("K_MAXOPS"):
            T.maxops = int(_os.environ["K_MAXOPS"])
        P32 = Pool(sb32, N32)
        P16 = Pool(sb16, N16)
        bank = [ps32[:, i * 512:(i + 1) * 512] for i in range(7)]
        bankb = [Buf(f"bank{i}") for i in range(7)]
        pT = ps16
        pTb = Buf("pT16")

        id32, id32b = P32.alloc(128, "id32")
        idb, idbb = P16.alloc(128, "idb")
        T.dma("sp", id32, ident, writes=[id32b])
        T.op("dve", lambda e: e.tensor_copy(out=idb, in_=id32), reads=[id32b], writes=[idbb])

        def bcast_load(dst, dbuf, src1d, n):
            T.dma("sp", dst[:, :n], src1d.partition_broadcast(128), writes=[dbuf])

        stage_n = 2048

        def load_w(dst3, src2, stages, engs=("dve", "act")):
            rows, N = src2.shape
            KC = (rows + 127) // 128
            i = 0
            for k in range(KC):
                pr = min(128, rows - k * 128)
                for n0 in range(0, N, stage_n):
                    n1 = min(N, n0 + stage_n)
                    st, stb = stages[load_w.i % len(stages)]
                    eng = engs[load_w.i % len(engs)]
                    load_w.i += 1
                    T.dma("sp", st[:pr, :n1 - n0], src2[k * 128:k * 128 + pr, n0:n1], writes=[stb])
                    if eng == "dve":
                        T.op("dve", lambda e, o=dst3[:pr, k, n0:n1], i_=st[:pr, :n1 - n0]: e.tensor_copy(out=o, in_=i_),
                             reads=[stb], writes=[Buf()])
                    else:
                        T.op("act", lambda e, o=dst3[:pr, k, n0:n1], i_=st[:pr, :n1 - n0]: e.activation(out=o, in_=i_, func=AF.Copy),
                             reads=[stb], writes=[Buf()])
        load_w.i = 0

        def rstd(src, srcb, P, n, c):
            T.op("act", lambda e: e.activation(out=c.junk[:P, :src.shape[-1]] if len(src.shape) == 2 else c.junk[:P, :src.shape[-1]],
                                               in_=src, func=AF.Square, accum_out=c.ss[:P]),
                 reads=[srcb], writes=[c.junkb, c.ssb])
            T.op("act", lambda e: e.activation(out=c.sd[:P], in_=c.ss[:P], func=AF.Sqrt, scale=1.0 / n, bias=EPS),
                 reads=[c.ssb], writes=[c.sdb])
            T.op("dve", lambda e: e.reciprocal(out=c.r[:P], in_=c.sd[:P]), reads=[c.sdb], writes=[c.rb])

        def transposes(src16, srcb, P, nblk, width, dstT, dstTb):
            def f(e):
                ins = None
                for j in range(nblk):
                    ins = e.transpose(out=pT[:width, j * 128:j * 128 + P], in_=src16[:P, j * width:(j + 1) * width],
                                      identity=idb[:P, :P])
                return ins
            T.op("pe", f, reads=[srcb, idbb], writes=[pTb])
            pv = pT[:width, :nblk * 128].rearrange("f (j p) -> f j p", p=128)[:, :, :P]
            T.op("act", lambda e: e.activation(out=dstT[:width, :nblk, :P], in_=pv, func=AF.Copy),
                 reads=[pTb], writes=[dstTb])

        def mm_tokmajor(out_ps, outb, lhsT3, lhsTb, P, W3, kc, c0, c1):
            def f(e):
                ins = None
                for k in range(kc):
                    ins = e.matmul(out_ps[:P, :c1 - c0], lhsT=lhsT3[:, k, :P], rhs=W3[:, k, c0:c1],
                                   start=(k == 0), stop=(k == kc - 1))
                return ins
            T.op("pe", f, reads=[lhsTb], writes=[outb])

        def rope(src3, srcb, P, H, Dh, blocks, cs, csb, out3, outb, c):
            t1 = c.rt1[:P, :H * Dh].rearrange("p (h d) -> p h d", d=Dh)
            t2 = c.rt2[:P, :H * Dh].rearrange("p (h d) -> p h d", d=Dh)
            cosb = cs[:P, 0:Dh].unsqueeze(1).broadcast_to([P, H, Dh])
            T.op("dve", lambda e: e.tensor_tensor(out=t1, in0=src3, in1=cosb, op=ALU.mult),
                 reads=[srcb, csb], writes=[c.rt1b])
            hb = Dh // blocks // 2
            for b in range(blocks):
                lo = b * 2 * hb
                s_lo = cs[:P, Dh + lo:Dh + lo + hb].unsqueeze(1).broadcast_to([P, H, hb])
                s_hi = cs[:P, Dh + lo + hb:Dh + lo + 2 * hb].unsqueeze(1).broadcast_to([P, H, hb])
                T.op("dve", lambda e, lo=lo, s_lo=s_lo: e.tensor_tensor(out=t2[:, :, lo:lo + hb], in0=src3[:, :, lo + hb:lo + 2 * hb],
                                                                       in1=s_lo, op=ALU.mult),
                     reads=[srcb, csb], writes=[c.rt2b])
                T.op("dve", lambda e, lo=lo, s_hi=s_hi: e.tensor_tensor(out=t2[:, :, lo + hb:lo + 2 * hb], in0=src3[:, :, lo:lo + hb],
                                                                       in1=s_hi, op=ALU.mult),
                     reads=[srcb, csb], writes=[c.rt2b])
            T.op("dve", lambda e: e.tensor_tensor(out=out3, in0=t1, in1=t2, op=ALU.add),
                 reads=[c.rt1b, c.rt2b], writes=[outb])

        def phase1(l):
            P16.mark()
            P32.mark()
            Win_f, _ = P16.alloc(8 * 1440, "Win")
            Win = Win_f.rearrange("p (k n) -> p k n", n=1440)
            Wq_f, _ = P16.alloc(3 * 768, "Wq")
            Wq = Wq_f.rearrange("p (k n) -> p k n", n=768)
            Wkv_f, _ = P16.alloc(2 * 1024, "Wkv")
            Wkv = Wkv_f.rearrange("p (k n) -> p k n", n=1024)
            g_attn, gb1 = P32.alloc(1024)
            g_qa, gb2 = P32.alloc(384)
            g_kva, gb3 = P32.alloc(256)
            g_gq, gb4 = P32.alloc(64)
            g_gk, gb5 = P32.alloc(64)
            bcast_load(g_attn, gb1, g_attn_d[l], 1024)
            bcast_load(g_qa, gb2, g_qa_d[l], 384)
            bcast_load(g_kva, gb3, g_kva_d[l], 256)
            bcast_load(g_gq, gb4, g_gq_d[l], 64)
            bcast_load(g_gk, gb5, g_gk_d[l], 64)
            gbufs = [gb1, gb2, gb3, gb4, gb5]
            P32.mark()
            stages = [P32.alloc(stage_n) for _ in range(3)]
            load_w(Win, w_in_d[l], stages)
            load_w(Wq, w_qb_d[l], stages)
            load_w(Wkv, w_kvb_d[l], stages)
            T.barrier()
            P32.release()

            ctxs = []
            for i in range(2):
                c = Ctx()
                c.x, c.xb = P32.alloc(1024)
                c.csm, c.csmb = P32.alloc(64)
                c.csg, c.csgb = P32.alloc(128)
                c.ss, c.ssb = P32.alloc(1)
                c.sd, c.sdb = P32.alloc(1)
                c.r, c.rb = P32.alloc(1)
                c.ss10, c.ss10b = P32.alloc(10)
                c.sd10, c.sd10b = P32.alloc(10)
                c.r10, c.r10b = P32.alloc(10)
                c.rt1, c.rt1b = P32.alloc(640)
                c.rt2, c.rt2b = P32.alloc(512)
                c.gn, c.gnb = P32.alloc(640)
                c.q32, c.q32b = P32.alloc(768)
                c.sq10, c.sq10b = c.rt1, c.rt1b
                c.kr32, c.kr32b = P32.alloc(32)
                c.junk, c.junkb = P16.alloc(1024)
                c.hn, c.hnb = P16.alloc(1024)
                c.hnT, c.hnTb = P16.alloc(1024)
                c.cqn, c.cqnb = P16.alloc(384)
                c.cqnT, c.cqnTb = P16.alloc(384)
                c.ckvn, c.ckvnb = P16.alloc(256)
                c.ckvnT, c.ckvnTb = P16.alloc(256)
                c.q16, c.q16b = P16.alloc(768)
                c.qT, c.qTb = P16.alloc(1024)
                c.kn, c.knb = P16.alloc(512)
                c.knT, c.knTb = P16.alloc(512)
                c.vst, c.vstb = P16.alloc(650)
                T.op("dve", lambda e, c=c: e.memset(c.vst.rearrange("p (h d) -> p h d", d=65)[:, :, 64:65], 1.0), writes=[c.vstb])
                c.kpe, c.kpeb = P16.alloc(32)
                c.kpeT, c.kpeTb = P16.alloc(128)
                c.gq16, c.gq16b = P16.alloc(512)
                c.gqT, c.gqTb = P16.alloc(512)
                c.gk16, c.gk16b = P16.alloc(128)
                c.gkT, c.gkTb = P16.alloc(128)
                ctxs.append(c)

            def tile_gen(tile_i, s, t):
                if True:
                    c = ctxs[tile_i % 2]
                    is_meta = (t == NT)
                    P = NMETA if is_meta else 128
                    tok0 = t * 128
                    if l == 0:
                        src = meta[:, :] if is_meta else xq[s, tok0:tok0 + P, :]
                    else:
                        src = X1[s][tok0:tok0 + P, :]
                    T.dma("sp", c.x[:P], src, writes=[c.xb])
                    T.dma("sp", c.csm[:P], csm_d[s, tok0:tok0 + P, :], writes=[c.csmb])
                    T.dma("sp", c.csg[:P], csg_d[s, tok0:tok0 + P, :], writes=[c.csgb])
                    rstd(c.x[:P], c.xb, P, 1024, c)
                    T.op("dve", lambda e, c=c, P=P: e.scalar_tensor_tensor(out=c.hn[:P], in0=c.x[:P], scalar=c.r[:P], in1=g_attn[:P],
                                                                         op0=ALU.mult, op1=ALU.mult),
                         reads=[c.xb, c.rb] + gbufs, writes=[c.hnb])
                    hnT3 = c.hnT.rearrange("p (k t) -> p k t", t=128)
                    transposes(c.hn, c.hnb, P, 8, 128, hnT3, c.hnTb)
                    yield
                    mm_tokmajor(bank[0], bankb[0], hnT3, c.hnTb, P, Win, 8, 0, 512)
                    mm_tokmajor(bank[2], bankb[2], hnT3, c.hnTb, P, Win, 8, 1024, 1440)
                    mm_tokmajor(bank[1], bankb[1], hnT3, c.hnTb, P, Win, 8, 512, 1024)
                    yield
                    rstd(bank[0][:P, 0:384], bankb[0], P, 384, c)
                    T.op("dve", lambda e, c=c, P=P: e.scalar_tensor_tensor(out=c.cqn[:P], in0=bank[0][:P, 0:384], scalar=c.r[:P],
                                                                         in1=g_qa[:P], op0=ALU.mult, op1=ALU.mult),
                         reads=[bankb[0], c.rb] + gbufs, writes=[c.cqnb])
                    cqnT3 = c.cqnT.rearrange("p (k t) -> p k t", t=128)
                    transposes(c.cqn, c.cqnb, P, 3, 128, cqnT3, c.cqnTb)
                    rstd(bank[2][:P, 0:256], bankb[2], P, 256, c)
                    T.op("dve", lambda e, c=c, P=P: e.scalar_tensor_tensor(out=c.ckvn[:P], in0=bank[2][:P, 0:256], scalar=c.r[:P],
                                                                         in1=g_kva[:P], op0=ALU.mult, op1=ALU.mult),
                         reads=[bankb[2], c.rb] + gbufs, writes=[c.ckvnb])
                    ckvnT3 = c.ckvnT.rearrange("p (k t) -> p k t", t=128)
                    transposes(c.ckvn, c.ckvnb, P, 2, 128, ckvnT3, c.ckvnTb)
                    mm_tokmajor(bank[3], bankb[3], cqnT3, c.cqnTb, P, Wq, 3, 0, 384)
                    mm_tokmajor(bank[4], bankb[4], cqnT3, c.cqnTb, P, Wq, 3, 384, 768)
                    mm_tokmajor(bank[5], bankb[5], ckvnT3, c.ckvnTb, P, Wkv, 2, 0, 512)
                    mm_tokmajor(bank[6], bankb[6], ckvnT3, c.ckvnTb, P, Wkv, 2, 512, 1024)
                    q3 = c.q16.rearrange("p (h d) -> p h d", d=96)
                    q32 = c.q32.rearrange("p (h d) -> p h d", d=96)
                    for hb_, bk in ((0, 3), (1, 4)):
                        T.op("act", lambda e, bk=bk, hb_=hb_, P=P, c=c: e.activation(out=c.q32[:P, hb_ * 384:(hb_ + 1) * 384], in_=bank[bk][:P, 0:384],
                                                                                   func=AF.Copy),
                             reads=[bankb[bk]], writes=[c.q32b])
                    T.op("dve", lambda e, P=P, q3=q3, q32=q32: e.tensor_copy(out=q3[:P, :, 0:64], in_=q32[:P, :, 0:64]),
                         reads=[c.q32b], writes=[c.q16b])
                    rope(q32[:P, :, 64:96], c.q32b, P, 8, 32, 1, c.csm, c.csmb, q3[:P, :, 64:96], c.q16b, c)
                    def fq(e, c=c, P=P):
                        ins = None
                        for h in range(8):
                            ins = e.transpose(out=pT[:96, h * 128:h * 128 + P], in_=c.q16[:P, h * 96:(h + 1) * 96], identity=idb[:P, :P])
                        return ins
                    T.op("pe", fq, reads=[c.q16b, idbb], writes=[pTb])
                    qT3 = c.qT.rearrange("p (h t) -> p h t", t=128)
                    T.op("act", lambda e, P=P, qT3=qT3: e.activation(out=qT3[:96, :, :P],
                                                                   in_=pT[:96, :].rearrange("f (h t) -> f h t", t=128)[:, :, :P], func=AF.Copy),
                         reads=[pTb], writes=[c.qTb])
                    T.dma("pool", QTm[s].rearrange("(h d) t -> d h t", d=96)[:, :, tok0:tok0 + P], qT3[:96, :, :P],
                          reads=[c.qTb], writes=[Buf()])
                    v3 = c.vst.rearrange("p (h d) -> p h d", d=65)
                    for hb_, bk in ((0, 5), (1, 6)):
                        pkv = bank[bk][:P, :].rearrange("p (h d) -> p h d", d=128)
                        T.op("act", lambda e, pkv=pkv, hb_=hb_, P=P, v3=v3: e.activation(out=v3[:P, hb_ * 4:hb_ * 4 + 4, 0:64], in_=pkv[:, :, 64:128],
                                                                                       func=AF.Copy),
                             reads=[bankb[bk]], writes=[c.vstb])
                    T.op("act", lambda e, P=P, v3=v3: e.activation(out=v3[:P, 8:10, 0:64],
                                                                 in_=bank[0][:P, 384:512].rearrange("p (h d) -> p h d", d=64), func=AF.Copy),
                         reads=[bankb[0]], writes=[c.vstb])
                    if is_meta:
                        vdst = Vmeta[s].rearrange("(h t) d -> t h d", t=NMETA)
                    else:
                        vdst = Vloc[s].ap().rearrange("(h t) d -> t h d", t=N_OWN)[tok0:tok0 + P]
                    T.dma("pool", vdst, v3[:P], reads=[c.vstb], writes=[Buf()])
                    kn3 = c.kn.rearrange("p (h d) -> p h d", d=64)
                    for hb_, bk in ((0, 5), (1, 6)):
                        pkv = bank[bk][:P, :].rearrange("p (h d) -> p h d", d=128)
                        T.op("dve", lambda e, pkv=pkv, hb_=hb_, P=P, kn3=kn3: e.tensor_copy(out=kn3[:P, hb_ * 4:hb_ * 4 + 4, :], in_=pkv[:, :, 0:64]),
                             reads=[bankb[bk]], writes=[c.knb])
                    knT3 = c.knT.rearrange("p (j t) -> p j t", t=128)
                    transposes(c.kn, c.knb, P, 4, 128, knT3, c.knTb)
                    ktd = KTmeta[s] if is_meta else KTloc[s].ap()[:, tok0:tok0 + P]
                    T.dma("pool", ktd[0:512].rearrange("(j p) t -> p j t", p=128), knT3[:, :, :P], reads=[c.knTb], writes=[Buf()])
                    kpe3 = c.kpe[:, 0:32].rearrange("p (h d) -> p h d", d=32)
                    T.op("act", lambda e, c=c, P=P: e.activation(out=c.kr32[:P], in_=bank[2][:P, 256:288], func=AF.Copy),
                         reads=[bankb[2]], writes=[c.kr32b])
                    rope(c.kr32[:P].rearrange("p (h d) -> p h d", d=32), c.kr32b, P, 1, 32, 1, c.csm, c.csmb, kpe3[:P], c.kpeb, c)
                    kpeT3 = c.kpeT.rearrange("p (j t) -> p j t", t=128)
                    transposes(c.kpe, c.kpeb, P, 1, 32, kpeT3, c.kpeTb)
                    T.dma("pool", ktd[512:544], kpeT3[:32, 0, :P], reads=[c.kpeTb], writes=[Buf()])
                    gn3 = c.gn[:P, :].rearrange("p (h d) -> p h d", d=64)
                    T.op("act", lambda e, c=c, P=P: e.activation(out=c.gn[:P, 0:512], in_=bank[1][:P, :], func=AF.Copy),
                         reads=[bankb[1]], writes=[c.gnb])
                    T.op("act", lambda e, c=c, P=P: e.activation(out=c.gn[:P, 512:640], in_=bank[2][:P, 288:416], func=AF.Copy),
                         reads=[bankb[2]], writes=[c.gnb])
                    T.op("dve", lambda e, c=c, P=P: e.tensor_tensor(out=c.sq10[:P], in0=c.gn[:P], in1=c.gn[:P], op=ALU.mult),
                         reads=[c.gnb], writes=[c.sq10b])
                    T.op("dve", lambda e, c=c, P=P: e.tensor_reduce(out=c.ss10[:P], in_=c.sq10[:P].rearrange("p (h d) -> p h d", d=64),
                                                                  axis=AX.X, op=ALU.add),
                         reads=[c.sq10b], writes=[c.ss10b])
                    T.op("act", lambda e, c=c, P=P: e.activation(out=c.sd10[:P], in_=c.ss10[:P], func=AF.Sqrt, scale=1.0 / 64, bias=EPS),
                         reads=[c.ss10b], writes=[c.sd10b])
                    T.op("dve", lambda e, c=c, P=P: e.reciprocal(out=c.r10[:P], in_=c.sd10[:P]), reads=[c.sd10b], writes=[c.r10b])
                    T.op("dve", lambda e, c=c, P=P, gn3=gn3: e.tensor_tensor(
                        out=gn3, in0=gn3, in1=c.r10[:P, 0:10].unsqueeze(2).broadcast_to([P, 10, 64]), op=ALU.mult),
                        reads=[c.gnb, c.r10b], writes=[c.gnb])
                    T.op("dve", lambda e, P=P, gn3=gn3: e.tensor_tensor(
                        out=gn3[:, 0:8, :], in0=gn3[:, 0:8, :], in1=g_gq[:P, :].unsqueeze(1).broadcast_to([P, 8, 64]), op=ALU.mult),
                        reads=[c.gnb] + gbufs, writes=[c.gnb])
                    T.op("dve", lambda e, P=P, gn3=gn3: e.tensor_tensor(
                        out=gn3[:, 8:10, :], in0=gn3[:, 8:10, :], in1=g_gk[:P, :].unsqueeze(1).broadcast_to([P, 2, 64]), op=ALU.mult),
                        reads=[c.gnb] + gbufs, writes=[c.gnb])
                    gq16_3 = c.gq16.rearrange("p (h d) -> p h d", d=64)
                    gk16_3 = c.gk16.rearrange("p (h d) -> p h d", d=64)
                    rope(gn3[:, 0:8, :], c.gnb, P, 8, 64, 2, c.csg, c.csgb, gq16_3[:P], c.gq16b, c)
                    rope(gn3[:, 8:10, :], c.gnb, P, 2, 64, 2, c.csg, c.csgb, gk16_3[:P], c.gk16b, c)
                    gqT3 = c.gqT.rearrange("p (j t) -> p j t", t=128)
                    transposes(c.gq16, c.gq16b, P, 4, 128, gqT3, c.gqTb)
                    T.dma("pool", QTg[s].rearrange("(j p) t -> p j t", p=128)[:, :, tok0:tok0 + P], gqT3[:, :, :P],
                          reads=[c.gqTb], writes=[Buf()])
                    gkT3 = c.gkT.rearrange("p (j t) -> p j t", t=128)
                    transposes(c.gk16, c.gk16b, P, 1, 128, gkT3, c.gkTb)
                    T.dma("pool", ktd[544:672], gkT3[:, 0, :P], reads=[c.gkTb], writes=[Buf()])
            gens = [tile_gen(i, s_, t_) for i, (s_, t_) in enumerate((s_, t_) for s_ in range(2) for t_ in range(NT + 1))]
            next(gens[0])
            for k in range(len(gens)):
                next(gens[k])
                if k + 1 < len(gens):
                    next(gens[k + 1])
                for _ in gens[k]:
                    pass
            T.barrier()
            P16.release()
            P32.release()

        pieceb = {}

        def phase2():
            for s in range(2):
                R = RS[s]
                groups = [list(range(g * R, (g + 1) * R)) for g in range(8 // R)]
                kpc = [(("k", s, i), KTloc[s].ap()[a:a + n], KTall[s][i].ap()) for i, (a, n) in enumerate(KPIECES)]
                vpc = [(("v", s, h), Vloc[s].ap()[h * N_OWN:(h + 1) * N_OWN], Vall[s][h].ap()) for h in range(10)]
                order = [kpc[8]]
                for h in range(8):
                    order += [kpc[h], vpc[h]]
                order += [kpc[9], vpc[8], kpc[10], vpc[9]]
                for key, src, dst in order:
                    pieceb[key] = Buf()
                    T.custom("pool", lambda e, src=src, dst=dst, groups=groups: e.collective_compute(
                        "AllGather", ALU.bypass, replica_groups=groups, ins=[src], outs=[dst]), 1, writes=[pieceb[key]])

        def phase3(l):
            P16.mark()
            P32.mark()
            nq_eff = NQ if l == 0 else N_OWN
            ngrp = (nq_eff + 511) // 512
            units = nq_eff // 16
            base = units // ngrp
            widths = [16 * (base + (1 if i < units - base * ngrp else 0)) for i in range(ngrp)]
            assert sum(widths) == nq_eff and max(widths) <= 512
            g0s = [sum(widths[:i]) for i in range(ngrp)]
            LKMAX = 4 * N_OWN + NMETA
            NCHMAX = 4 * NC
            Kt = [(P16.alloc(LKMAX, f"K{i}")[0], [Buf() for _ in range(4)]) for i in range(2)]
            Vt = [P16.alloc(NCHMAX * 65, f"V{i}") for i in range(2)]
            Vm = [P16.alloc(65, f"Vm{i}") for i in range(2)]
            Qt = [(P16.alloc(NQ, f"Q{i}")[0], [Buf(), Buf()]) for i in range(2)]
            for i in range(2):
                T.op("dve", lambda e, i=i: e.memset(Kt[i][0][64:128, :], 0.0), writes=[Kt[i][1][1], Kt[i][1][3]])
            NPB = 3
            Pt = [P16.alloc(1024, f"P{i}") for i in range(NPB)]
            Osb = [P32.alloc(512, f"Osb{i}") for i in range(2)]
            aost = [P32.alloc(256, f"ao{i}") for i in range(2)]
            rd = [P32.alloc(4, f"rd{i}") for i in range(2)]
            Sb = [(ps32[:, b * 1024:(b + 1) * 1024], Buf(f"S{b}")) for b in range(2)]
            Ob = [(bank[4], bankb[4]), (bank[5], bankb[5])]
            OT, OTb = bank[6], bankb[6]

            kvsets = []
            for s in range(2):
                for h in range(8):
                    kvsets.append((s, "mla", h))
                for j in range(2):
                    kvsets.append((s, "gqa", j))

            def load_kv(idx):
                s, kind, h = kvsets[idx]
                R = RS[s]
                K, Kb = Kt[idx % 2]
                V, Vb = Vt[idx % 2]
                VM, VMb = Vm[idx % 2]
                def kp(i):
                    return KTall[s][i].ap().rearrange("(r f) t -> f r t", f=KPIECES[i][1])
                Kv = K[:, 0:R * N_OWN].rearrange("d (r t) -> d r t", t=N_OWN)
                mcol = slice(R * N_OWN, R * N_OWN + NMETA)
                if kind == "mla":
                    T.dma("sp", Kv[0:64], kp(h), reads=[pieceb[("k", s, h)]], writes=[Kb[0]])
                    T.dma("sp", Kv[64:96], kp(8), reads=[pieceb[("k", s, 8)]], writes=[Kb[1]])
                    T.dma("sp", K[0:64, mcol], KTmeta[s][h * 64:(h + 1) * 64, :], writes=[Kb[2]])
                    T.dma("sp", K[64:96, mcol], KTmeta[s][512:544, :], writes=[Kb[3]])
                    hv = h
                else:
                    T.dma("sp", Kv[0:64], kp(9 + h), reads=[pieceb[("k", s, 9 + h)]], writes=[Kb[0]])
                    T.dma("sp", K[0:64, mcol], KTmeta[s][544 + h * 64:544 + (h + 1) * 64, :], writes=[Kb[2]])
                    hv = 8 + h
                vall = Vall[s][hv].ap().rearrange("(r p c) d -> p r c d", p=128, c=NC)
                V4 = V[:, 0:R * NC * 65].rearrange("p (r c d) -> p r c d", c=NC, d=65)
                T.dma("sp", V4, vall, reads=[pieceb[("v", s, hv)]], writes=[Vb])
                T.dma("sp", VM[0:NMETA, 0:65], Vmeta[s][hv * NMETA:(hv + 1) * NMETA, :], writes=[VMb])

            def qheads(idx):
                s, kind, h = kvsets[idx]
                if kind == "mla":
                    return [(s, "mla", h, h)]
                return [(s, "gqa", h * 4 + g, 8 + h * 4 + g) for g in range(4)]

            qlist = []
            for idx in range(len(kvsets)):
                for qh in qheads(idx):
                    qlist.append((idx, qh))

            def load_q(qi):
                idx, (s, kind, qh, _) = qlist[qi]
                Q, Qb = Qt[qi % 2]
                if kind == "mla":
                    T.dma("sp", Q[0:96, 0:nq_eff], QTm[s][qh * 96:(qh + 1) * 96, 0:nq_eff], writes=[Qb[0], Qb[1]])
                else:
                    T.dma("sp", Q[0:64, 0:nq_eff], QTg[s][qh * 64:(qh + 1) * 64, 0:nq_eff], writes=[Qb[0]])
                    T.op("dve", lambda e, Q=Q: e.memset(Q[64:128, 0:nq_eff], 0.0), writes=[Qb[1]])

            steps = []
            for qi, (idx, (s, kind, qh, cb)) in enumerate(qlist):
                R = RS[s]
                nch = R * NC + 1
                for gi in range(ngrp):
                    for c0 in range(0, R * NC, 2):
                        steps.append((qi, gi, [c0, c0 + 1], nch))
                    steps.append((qi, gi, [R * NC], nch))
            LA = 2
            gcount = [0]
            pending = []

            def kcols(idx, cix):
                s, kind, h = kvsets[idx]
                R = RS[s]
                d = 96 if kind == "mla" else 128
                K, Kb = Kt[idx % 2]
                if cix == R * NC:
                    return K[0:d, R * N_OWN:R * N_OWN + NMETA], NMETA, d
                r, cl = divmod(cix, NC)
                return K[0:d, r * N_OWN:(r + 1) * N_OWN].rearrange("d (p c) -> d p c", c=NC)[:, :, cl], 128, d

            def emit_qk(i):
                qi, gi, chunks, nch = steps[i]
                idx, (s, kind, qh, cb) = qlist[qi]
                Q, Qb = Qt[qi % 2]
                G = widths[gi]
                g0 = g0s[gi]
                S, Sbuf = Sb[i % 2]

                def f(e):
                    ins = None
                    for k, cix in enumerate(chunks):
                        lhsT, kc, d = kcols(idx, cix)
                        ins = e.matmul(S[:kc, k * 512:k * 512 + G], lhsT=lhsT, rhs=Q[0:d, g0:g0 + G], start=True, stop=True)
                    return ins
                T.op("pe", f, reads=Kt[idx % 2][1] + Qb, writes=[Sbuf])

            def emit_exp_pv(i):
                qi, gi, chunks, nch = steps[i]
                idx, (s, kind, qh, cb) = qlist[qi]
                R = RS[s]
                scale = (96.0 if kind == "mla" else 64.0) ** -0.5
                n = len(chunks)
                kc = NMETA if chunks[0] == R * NC else 128
                G = widths[gi]
                S, Sbuf = Sb[i % 2]
                Pp, Pb = Pt[i % NPB]
                gidx = gcount[0]
                O, Obuf = Ob[gidx % 2]
                S3 = S.rearrange("p (k g) -> p k g", g=512)[:kc, 0:n, 0:G]
                P3 = Pp.rearrange("p (k g) -> p k g", g=512)[:kc, 0:n, 0:G]
                T.op("act", lambda e, S3=S3, P3=P3, scale=scale: e.activation(out=P3, in_=S3, func=AF.Exp, scale=scale),
                     reads=[Sbuf], writes=[Pb])
                j = i + LA
                if j < len(steps):
                    ensure_loaded(j)
                    emit_qk(j)
                if kc == 128:
                    V, Vb = Vt[idx % 2]
                    V3 = V.rearrange("p (c d) -> p c d", d=65)
                    lhs = [V3[:, cix, :] for cix in chunks]
                else:
                    V, Vb = Vm[idx % 2]
                    lhs = [V[0:NMETA, 0:65]]

                def fpv(e):
                    ins = None
                    for k, cix in enumerate(chunks):
                        ins = e.matmul(O[0:65, :G], lhsT=lhs[k], rhs=Pp[:kc, k * 512:k * 512 + G],
                                       start=(cix == 0), stop=(cix == nch - 1))
                    return ins
                T.op("pe", fpv, reads=[Vb, Pb], writes=[Obuf])
                cix = chunks[-1]
                if cix == nch - 1:
                    gcount[0] += 1
                    Os, Osbuf = Osb[gidx % 2]
                    T.op("dve", lambda e, Os=Os, O=O, G=G: e.tensor_copy(out=Os[0:65, :G], in_=O[0:65, :G]),
                         reads=[Obuf], writes=[Osbuf])
                    def epi(Os=Os, Osbuf=Osbuf, O=O, G=G, gidx=gidx, gi=gi, s=s, cb=cb):
                        nsub = (G + 127) // 128
                        OT3 = OT[:, 0:4 * 65].rearrange("p (j d) -> p j d", d=65)

                        def ftr(e, Os=Os, G=G, nsub=nsub):
                            ins = None
                            for j in range(nsub):
                                w = min(128, G - j * 128)
                                ins = e.transpose(out=OT3[:w, j, :], in_=Os[0:65, j * 128:j * 128 + w], identity=id32[0:65, 0:65])
                            return ins
                        T.op("pe", ftr, reads=[Osbuf, id32b], writes=[OTb])
                        rdt, rdb = rd[gidx % 2]
                        ao, aob = aost[gidx % 2]
                        ao3 = ao.rearrange("p (j d) -> p j d", d=64)
                        g0 = g0s[gi]
                        full = G // 128
                        rem = G - full * 128
                        parts = []
                        if full:
                            parts.append((128, 0, full))
                        if rem:
                            parts.append((rem, full, full + 1))
                        for (pw, j0, j1) in parts:
                            T.op("dve", lambda e, pw=pw, j0=j0, j1=j1, rdt=rdt: e.reciprocal(out=rdt[:pw, j0:j1], in_=OT3[:pw, j0:j1, 64]),
                                 reads=[OTb], writes=[rdb])
                            for jj in range(j0, j1):
                                T.op("dve", lambda e, pw=pw, jj=jj, rdt=rdt, ao3=ao3: e.tensor_scalar(
                                    out=ao3[:pw, jj, :], in0=OT3[:pw, jj, 0:64], scalar1=rdt[:pw, jj:jj + 1], scalar2=None, op0=ALU.mult),
                                    reads=[OTb, rdb], writes=[aob])
                            if j1 - j0 > 1 or True:
                                dst = AO[s][g0 + j0 * 128:g0 + j0 * 128 + (j1 - j0 - 1) * 128 + pw, cb * 64:(cb + 1) * 64]
                                if j1 - j0 == 1:
                                    T.dma("sp", dst, ao3[:pw, j0, :], reads=[aob], writes=[Buf()])
                                else:
                                    T.dma("sp", dst.rearrange("(j p) d -> p j d", p=128), ao3[:pw, j0:j1, :], reads=[aob], writes=[Buf()])
                    pending.append((i + 2, epi))

            load_kv(0)
            load_q(0)
            started_kv = {0}
            started_q = {0}
            n = len(steps)
            def ensure_loaded(j):
                qi = steps[j][0]
                if qi not in started_q:
                    load_q(qi)
                    started_q.add(qi)
                idx = qlist[qi][0]
                if idx not in started_kv:
                    load_kv(idx)
                    started_kv.add(idx)

            for j in range(min(LA, n)):
                ensure_loaded(j)
                emit_qk(j)
            for i in range(n):
                if i >= 0:
                    emit_exp_pv(i)
                    while pending and pending[0][0] <= i:
                        pending.pop(0)[1]()
                    qi = steps[i][0]
                    if steps[i][1] == 0 and steps[i][2][0] == 0:
                        if qi + 1 < len(qlist) and (qi + 1) not in started_q:
                            load_q(qi + 1)
                            started_q.add(qi + 1)
                            nidx = qlist[qi + 1][0]
                            if nidx not in started_kv:
                                load_kv(nidx)
                                started_kv.add(nidx)
            while pending:
                pending.pop(0)[1]()
            T.barrier()
            P16.release()
            P32.release()

        def phase4(l):
            last = (l == depth - 1)
            P16.mark()
            P32.mark()
            Wo_f, _ = P16.alloc(8 * 1024, "Wo")
            Wo = Wo_f.rearrange("p (k n) -> p k n", n=1024)
            Wu_f, _ = P16.alloc(8 * 4096, "Wu")
            Wu = Wu_f.rearrange("p (k n) -> p k n", n=4096)
            Wd_f, _ = P16.alloc(32 * 1024, "Wd")
            Wd = Wd_f.rearrange("p (k n) -> p k n", n=1024)
            g_out, gb1 = P32.alloc(1024)
            g_mlp, gb2 = P32.alloc(1024)
            bcast_load(g_out, gb1, g_out_d[l], 1024)
            bcast_load(g_mlp, gb2, g_mlp_d[l], 1024)
            gbufs = [gb1, gb2]
            if last:
                g_fin, gb3 = P32.alloc(1024)
                bcast_load(g_fin, gb3, g_fin_d, 1024)
                gbufs.append(gb3)
            P32.mark()
            stages = [P32.alloc(stage_n) for _ in range(3)]
            load_w(Wo, w_out_d[l], stages)
            load_w(Wu, w_up_d[l], stages)
            load_w(Wd, w_down_d[l], stages)
            T.barrier()
            P32.release()

            ctxs = []
            for i in range(2):
                c = Ctx()
                c.x, c.xb = P32.alloc(1024)
                c.ao, c.aob = P32.alloc(1024)
                c.ss, c.ssb = P32.alloc(1)
                c.sd, c.sdb = P32.alloc(1)
                c.r, c.rb = P32.alloc(1)
                c.ss2, c.ss2b = P32.alloc(2)
                c.sd2, c.sd2b = P32.alloc(2)
                c.r2, c.r2b = P32.alloc(2)
                if i == 0:
                    c.junk, c.junkb = P16.alloc(1024)
                    c.mix, c.mixb = P16.alloc(1024)
                    c.mixT, c.mixTb = P16.alloc(1024)
                    c.hn, c.hnb = P16.alloc(1024)
                    c.hnT, c.hnTb = P16.alloc(1024)
                else:
                    for nm in ("junk", "mix", "mixT", "hn", "hnT"):
                        setattr(c, nm, getattr(ctxs[0], nm))
                        setattr(c, nm + "b", getattr(ctxs[0], nm + "b"))
                ctxs.append(c)
            xmid, xmidb = P32.alloc(1024)
            xnew, xnewb = P32.alloc(1024)
            r32 = [P32.alloc(512) for _ in range(2)]
            aT = [P16.alloc(512) for _ in range(2)]
            pso = [(bank[0], bankb[0]), (bank[1], bankb[1])]
            pu = [(bank[2], bankb[2]), (bank[3], bankb[3])]
            py = [(bank[4], bankb[4]), (bank[5], bankb[5])]

            tile_i = 0
            for s in range(2):
                for t in range(NT + (0 if last else 1)):
                    c = ctxs[tile_i % 2]
                    tile_i += 1
                    is_meta = (t == NT)
                    P = NMETA if is_meta else 128
                    tok0 = t * 128
                    if l == 0:
                        src = meta[:, :] if is_meta else xq[s, tok0:tok0 + P, :]
                    else:
                        src = X1[s][tok0:tok0 + P, :]
                    T.dma("sp", c.x[:P], src, writes=[c.xb])
                    T.dma("sp", c.ao[:P], AO[s][tok0:tok0 + P, :], writes=[c.aob])
                    for hf in range(2):
                        T.op("act", lambda e, c=c, P=P, hf=hf: e.activation(out=c.junk[:P, 0:512], in_=c.ao[:P, hf * 512:(hf + 1) * 512],
                                                                          func=AF.Square, accum_out=c.ss2[:P, hf:hf + 1]),
                             reads=[c.aob], writes=[c.junkb, c.ss2b])
                    T.op("act", lambda e, c=c, P=P: e.activation(out=c.sd2[:P], in_=c.ss2[:P], func=AF.Sqrt, scale=1.0 / 512, bias=EPS),
                         reads=[c.ss2b], writes=[c.sd2b])
                    T.op("dve", lambda e, c=c, P=P: e.reciprocal(out=c.r2[:P], in_=c.sd2[:P]), reads=[c.sd2b], writes=[c.r2b])
                    for hf in range(2):
                        sl = slice(hf * 512, (hf + 1) * 512)
                        T.op("dve", lambda e, c=c, P=P, hf=hf, sl=sl: e.scalar_tensor_tensor(
                            out=c.mix[:P, sl], in0=c.ao[:P, sl], scalar=c.r2[:P, hf:hf + 1], in1=g_out[:P, sl], op0=ALU.mult, op1=ALU.mult),
                            reads=[c.aob, c.r2b] + gbufs, writes=[c.mixb])
                    mixT3 = c.mixT.rearrange("p (k t) -> p k t", t=128)
                    transposes(c.mix, c.mixb, P, 8, 128, mixT3, c.mixTb)
                    for j in range(2):
                        mm_tokmajor(pso[j][0], pso[j][1], mixT3, c.mixTb, P, Wo, 8, j * 512, (j + 1) * 512)
                    for j in range(2):
                        sl = slice(j * 512, (j + 1) * 512)
                        T.op("dve", lambda e, c=c, P=P, j=j, sl=sl: e.tensor_tensor(out=xmid[:P, sl], in0=pso[j][0][:P, :], in1=c.x[:P, sl], op=ALU.add),
                             reads=[pso[j][1], c.xb], writes=[xmidb])
                    rstd(xmid[:P], xmidb, P, 1024, c)
                    T.op("dve", lambda e, c=c, P=P: e.scalar_tensor_tensor(out=c.hn[:P], in0=xmid[:P], scalar=c.r[:P], in1=g_mlp[:P],
                                                                         op0=ALU.mult, op1=ALU.mult),
                         reads=[xmidb, c.rb] + gbufs, writes=[c.hnb])
                    hnT3 = c.hnT.rearrange("p (k t) -> p k t", t=128)
                    transposes(c.hn, c.hnb, P, 8, 128, hnT3, c.hnTb)

                    def up(fb, c=c, P=P, hnT3=hnT3):
                        U, Ub = pu[fb % 2]

                        def f(e):
                            ins = None
                            for q in range(4):
                                fidx = fb * 4 + q
                                for k in range(8):
                                    ins = e.matmul(U[:, q * 128:q * 128 + P], lhsT=Wu[:, k, fidx * 128:(fidx + 1) * 128], rhs=hnT3[:, k, :P],
                                                   start=(k == 0), stop=(k == 7))
                            return ins
                        T.op("pe", f, reads=[c.hnTb], writes=[Ub])
                        U3 = U.rearrange("p (q t) -> p q t", t=128)[:, :, :P]
                        R3 = r32[fb % 2][0].rearrange("p (q t) -> p q t", t=128)[:, :, :P]
                        A3 = aT[fb % 2][0].rearrange("p (q t) -> p q t", t=128)[:, :, :P]
                        T.op("act", lambda e: e.activation(out=R3, in_=U3, func=AF.Relu), reads=[Ub], writes=[r32[fb % 2][1]])
                        T.op("dve", lambda e: e.tensor_tensor(out=A3, in0=R3, in1=R3, op=ALU.mult), reads=[r32[fb % 2][1]], writes=[aT[fb % 2][1]])

                    def down(fb, P=P):
                        A3 = aT[fb % 2][0].rearrange("p (q t) -> p q t", t=128)

                        def f(e):
                            ins = None
                            for q in range(4):
                                fidx = fb * 4 + q
                                for j in range(2):
                                    ins = e.matmul(py[j][0][:P, :], lhsT=A3[:, q, :P], rhs=Wd[:, fidx, j * 512:(j + 1) * 512],
                                                   start=(fidx == 0), stop=(fidx == 31))
                            return ins
                        T.op("pe", f, reads=[aT[fb % 2][1]], writes=[py[0][1], py[1][1]])

                    for fb in range(8):
                        up(fb)
                        if fb >= 1:
                            down(fb - 1)
                    down(7)
                    for j in range(2):
                        sl = slice(j * 512, (j + 1) * 512)
                        T.op("dve", lambda e, P=P, j=j, sl=sl: e.tensor_tensor(out=xnew[:P, sl], in0=py[j][0][:P, :], in1=xmid[:P, sl], op=ALU.add),
                             reads=[py[j][1], xmidb], writes=[xnewb])
                    if not last:
                        T.dma("pool", X1[s][tok0:tok0 + P, :], xnew[:P], reads=[xnewb], writes=[Buf()])
                    else:
                        rstd(xnew[:P], xnewb, P, 1024, c)
                        T.op("dve", lambda e, c=c, P=P: e.scalar_tensor_tensor(out=c.ao[:P], in0=xnew[:P], scalar=c.r[:P], in1=g_fin[:P],
                                                                             op0=ALU.mult, op1=ALU.mult),
                             reads=[xnewb, c.rb] + gbufs, writes=[c.aob])
                        T.dma("pool", y_d[s, tok0:tok0 + P, :], c.ao[:P], reads=[c.aob], writes=[Buf()])
            T.barrier()
            P16.release()
            P32.release()

        plist = []
        for l in range(depth):
            plist += [lambda l=l: phase1(l), phase2, lambda l=l: phase3(l), lambda l=l: phase4(l)]
        for ph in plist[:nphase]:
            ph()
        if debug:
            dbg = {}
            for s in range(2):
                for nm, t in (("KTloc", KTloc[s]), ("Vloc", Vloc[s])):
                    o = nc.dram_tensor(f"dbg_{nm}{s}", list(t.ap().shape), BF16, kind="ExternalOutput")
                    T.dma("pool", o.ap(), t.ap())
            T.barrier()

        @block.sync
        def _(e):
            T.replay("sp", e)

        @block.tensor
        def _(e):
            T.replay("pe", e)

        @block.scalar
        def _(e):
            T.replay("act", e)

        @block.vector
        def _(e):
            T.replay("dve", e)

        @block.gpsimd
        def _(e):
            T.replay("pool", e)
    return nc


def _inv_freq(dim):
    return (np.float32(1.0) / np.power(np.float32(10000.0), np.arange(0, dim, 2, dtype=np.float32) / np.float32(dim))).astype(np.float32)


def _tables(pos, rows, cols):
    def cs(p, f):
        ang = (p[:, None].astype(np.float32) * f[None, :].astype(np.float32)).astype(np.float32)
        a = np.concatenate([ang, ang], axis=-1).astype(np.float64)
        c, s = np.cos(a), np.sin(a)
        h = ang.shape[1]
        sp = np.concatenate([-s[:, :h], s[:, h:]], axis=-1)
        return c.astype(np.float32), sp.astype(np.float32)
    cm, sm = cs(pos, _inv_freq(32))
    cr, sr = cs(rows, _inv_freq(32))
    cc, sc = cs(cols, _inv_freq(32))
    csm = np.concatenate([cm, sm], axis=-1)
    csg = np.concatenate([cr, cc, sr, sc], axis=-1)
    return np.ascontiguousarray(csm, np.float32), np.ascontiguousarray(csg, np.float32)


_PERM = np.concatenate([np.arange(0, 384), np.arange(1312, 1440), np.arange(672, 1184),
                        np.arange(384, 640), np.arange(640, 672), np.arange(1184, 1312)])

_NC_CACHE = {}


def run_model(x_prompt, x_sample, meta_tokens, attn_norm_g, w_in, q_a_norm_g, w_q_b, kv_a_norm_g, w_kv_b,
              gqa_q_norm_g, gqa_k_norm_g, mla_out_norm_g, gqa_out_norm_g, w_out, mlp_norm_g, w_up, w_down,
              final_norm_g, trace=False, nphase=None, debug=False):
    f = lambda a: np.ascontiguousarray(np.asarray(a), dtype=np.float32)
    x_prompt, x_sample = f(x_prompt), f(x_sample)
    B, n_long, _ = x_prompt.shape
    Bs, n_short, _ = x_sample.shape
    assert B == 2 and Bs == 4 and n_long == 2 * n_short
    N_OWN = n_long // 4
    depth = np.asarray(w_in).shape[0]
    shared = {
        "meta": f(meta_tokens), "ident": np.eye(128, dtype=np.float32),
        "w_in": np.ascontiguousarray(f(w_in)[:, :, _PERM]), "w_q_b": f(w_q_b), "w_kv_b": f(w_kv_b), "w_out": f(w_out),
        "w_up": f(w_up), "w_down": f(w_down), "attn_norm_g": f(attn_norm_g), "q_a_norm_g": f(q_a_norm_g),
        "kv_a_norm_g": f(kv_a_norm_g), "gqa_q_norm_g": f(gqa_q_norm_g), "gqa_k_norm_g": f(gqa_k_norm_g),
        "out_norm_g": np.ascontiguousarray(np.concatenate([f(mla_out_norm_g), f(gqa_out_norm_g)], axis=-1)),
        "mlp_norm_g": f(mlp_norm_g), "final_norm_g": f(final_norm_g),
    }
    in_maps = []
    for c in range(8):
        bl, rl = c // 4, c % 4
        bs, rs = c // 2, c % 2
        xq = np.stack([x_prompt[bl, rl * N_OWN:(rl + 1) * N_OWN], x_sample[bs, rs * N_OWN:(rs + 1) * N_OWN]])
        csm, csg = [], []
        for r in (rl, rs):
            t = np.arange(r * N_OWN, (r + 1) * N_OWN, dtype=np.float32)
            pos = np.concatenate([t + np.float32(NMETA), np.arange(NMETA, dtype=np.float32)])
            rows = np.concatenate([np.floor(t / 64.0), np.zeros(NMETA)]).astype(np.float32)
            cols = np.concatenate([np.mod(t, 64.0), np.zeros(NMETA)]).astype(np.float32)
            a, b = _tables(pos, rows, cols)
            csm.append(a)
            csg.append(b)
        m = dict(shared)
        m["xq"] = np.ascontiguousarray(xq)
        m["csm"] = np.stack(csm)
        m["csg"] = np.stack(csg)
        import os as _os3
        if _os3.environ.get("K_PAD"):
            m["pad"] = np.zeros((int(_os3.environ["K_PAD"]), 1024), np.float32)
        in_maps.append(m)
    key = (N_OWN, depth, nphase, debug)
    if key not in _NC_CACHE:
        _NC_CACHE[key] = build(N_OWN, depth, nphase=nphase, debug=debug)
    nc = _NC_CACHE[key]
    res = run_bass_kernel_spmd(nc, in_maps, core_ids=list(range(8)), trace=trace)
    y_prompt = np.empty_like(x_prompt)
    y_sample = np.empty_like(x_sample)
    for c in range(8):
        y = np.asarray(res.results[c]["y"], dtype=np.float32)
        y_prompt[c // 4, (c % 4) * N_OWN:(c % 4 + 1) * N_OWN] = y[0]
        y_sample[c // 2, (c % 2) * N_OWN:(c % 2 + 1) * N_OWN] = y[1]
    return (y_prompt, y_sample), res


def kernel(**inputs):
    out, _ = run_model(**inputs)
    return out
```
